# Optimizing a Trainium2 kernel written in Bass

```python
import math
import jax, jax.numpy as jnp
from jax import lax
import numpy as np

D_MODEL = 1024
BATCH = 2
SEQ = 8192
DEPTH = 1

GRID_W = 64
HEAD_DIM = 64
NA_HEADS = 8
NA_KH_MAX = 8
NA_KW = 16
NA_WIDTH = NA_HEADS * HEAD_DIM
SWA_Q_HEADS = 8
SWA_KV_HEADS = 2
SWA_WINDOW = 128
SWA_BLOCK = 128
SWA_Q_WIDTH = SWA_Q_HEADS * HEAD_DIM
SWA_KV_WIDTH = SWA_KV_HEADS * HEAD_DIM
ROPE_THETA = 500000.0
ROPE_DIM = HEAD_DIM // 4
D_FF = 4 * D_MODEL
IN_WIDTH = 3 * NA_WIDTH + SWA_Q_WIDTH + 2 * SWA_KV_WIDTH + 2 * D_MODEL
IN_SPLITS = [
    NA_WIDTH,
    2 * NA_WIDTH,
    3 * NA_WIDTH,
    3 * NA_WIDTH + SWA_Q_WIDTH,
    3 * NA_WIDTH + SWA_Q_WIDTH + SWA_KV_WIDTH,
    3 * NA_WIDTH + SWA_Q_WIDTH + 2 * SWA_KV_WIDTH,
    3 * NA_WIDTH + SWA_Q_WIDTH + 2 * SWA_KV_WIDTH + D_MODEL,
]
DEEPNORM_ALPHA = (2.0 * DEPTH) ** 0.25
DEEPNORM_BETA = (8.0 * DEPTH) ** -0.25
LN_EPS = 1e-5
MASK_VALUE = -1e30

kernel_name = "hybrid_natten_swa_gated_deepnorm_encoder"


def layer_norm(x, g, b):
    xf = x.astype(jnp.float32)
    mu = jnp.mean(xf, axis=-1, keepdims=True)
    xc = xf - mu
    var = jnp.mean(xc * xc, axis=-1, keepdims=True)
    y = xc * lax.rsqrt(var + LN_EPS) * g.astype(jnp.float32) + b.astype(jnp.float32)
    return y.astype(x.dtype)


def partial_rotary(x, pos):
    half = ROPE_DIM // 2
    inv_freq = jnp.power(ROPE_THETA, -jnp.arange(0, ROPE_DIM, 2, dtype=jnp.float32) / ROPE_DIM)
    ang = pos.astype(jnp.float32)[:, None] * inv_freq[None, :]
    cos = jnp.cos(ang)[:, None, :]
    sin = jnp.sin(ang)[:, None, :]
    xr = x[..., :ROPE_DIM].astype(jnp.float32)
    x1, x2 = xr[..., :half], xr[..., half:]
    rot = jnp.concatenate([x1 * cos - x2 * sin, x2 * cos + x1 * sin], axis=-1)
    return jnp.concatenate([rot.astype(x.dtype), x[..., ROPE_DIM:]], axis=-1)


def neighbourhood_attention(q, k, v, rpb):
    B, S, H, Dh = q.shape
    rows = S // GRID_W
    kh = min(NA_KH_MAX, rows)
    kw = NA_KW
    qg = q.reshape(B, rows, GRID_W, H, Dh)
    kg = k.reshape(B, rows, GRID_W, H, Dh)
    vg = v.reshape(B, rows, GRID_W, H, Dh)
    cols = jnp.arange(GRID_W)
    col_start = jnp.clip(cols - kw // 2, 0, GRID_W - kw)
    col_idx = col_start[:, None] + jnp.arange(kw)[None, :]
    dc = col_idx - cols[:, None]
    scale = Dh ** -0.5

    def one_row(r):
        rs = jnp.clip(r - kh // 2, 0, rows - kh)
        kr = lax.dynamic_slice_in_dim(kg, rs, kh, axis=1)
        vr = lax.dynamic_slice_in_dim(vg, rs, kh, axis=1)
        kn = kr[:, :, col_idx]
        vn = vr[:, :, col_idx]
        qr = lax.dynamic_index_in_dim(qg, r, axis=1, keepdims=False)
        s = jnp.einsum('bwhd,bkwjhd->bhwkj', qr, kn).astype(jnp.float32) * scale
        dr = rs + jnp.arange(kh) - r
        bias = rpb[:, (dr + NA_KH_MAX - 1)[None, :, None], (dc + kw - 1)[:, None, :]]
        s = s + bias.astype(jnp.float32)[None]
        p = jax.nn.softmax(s.reshape(B, H, GRID_W, kh * kw), axis=-1)
        p = p.reshape(B, H, GRID_W, kh, kw).astype(v.dtype)
        return jnp.einsum('bhwkj,bkwjhd->bwhd', p, vn)

    out = lax.map(one_row, jnp.arange(rows))
    return jnp.moveaxis(out, 0, 1).reshape(B, S, H * Dh)


def windowed_gqa_with_sink(q, k, v, sink):
    B, S, Hq, Dh = q.shape
    G = k.shape[2]
    rep = Hq // G
    blk = SWA_BLOCK
    nb = S // blk
    qb = q.reshape(B, nb, blk, G, rep, Dh)
    pad = ((0, 0), (blk, blk), (0, 0), (0, 0))
    kp = jnp.pad(k, pad).reshape(B, nb + 2, blk, G, Dh)
    vp = jnp.pad(v, pad).reshape(B, nb + 2, blk, G, Dh)
    kb = jnp.concatenate([kp[:, :-2], kp[:, 1:-1], kp[:, 2:]], axis=2)
    vb = jnp.concatenate([vp[:, :-2], vp[:, 1:-1], vp[:, 2:]], axis=2)
    s = jnp.einsum('bnqgrd,bnkgd->bngrqk', qb, kb).astype(jnp.float32) * (Dh ** -0.5)
    qpos = jnp.arange(nb)[:, None] * blk + jnp.arange(blk)[None, :]
    kpos = jnp.arange(nb)[:, None] * blk - blk + jnp.arange(3 * blk)[None, :]
    rel = kpos[:, None, :] - qpos[:, :, None]
    valid = (jnp.abs(rel) <= SWA_WINDOW) & (kpos[:, None, :] >= 0) & (kpos[:, None, :] < S)
    s = jnp.where(valid[None, :, None, None], s, MASK_VALUE)
    sink_l = sink.astype(jnp.float32).reshape(1, 1, G, rep, 1, 1)
    m = jnp.maximum(jnp.max(s, axis=-1, keepdims=True), sink_l)
    e = jnp.exp(s - m)
    p = e / (jnp.sum(e, axis=-1, keepdims=True) + jnp.exp(sink_l - m))
    o = jnp.einsum('bngrqk,bnkgd->bnqgrd', p.astype(v.dtype), vb)
    return o.reshape(B, S, Hq * Dh)


def setup_inputs(seed: int = 0) -> dict:
    key = jax.random.key(seed)
    ks = jax.random.split(key, 24)
    f32 = jnp.float32
    beta = DEEPNORM_BETA

    def nrm(k, shape, scale):
        return jax.random.normal(k, shape, f32) * scale

    col_scale = jnp.concatenate([
        jnp.ones((2 * NA_WIDTH,), f32), jnp.full((NA_WIDTH,), beta, f32),
        jnp.ones((SWA_Q_WIDTH + SWA_KV_WIDTH,), f32), jnp.full((SWA_KV_WIDTH,), beta, f32),
        jnp.ones((2 * D_MODEL,), f32)])
    w_in = nrm(ks[1], (DEPTH, D_MODEL, IN_WIDTH), D_MODEL ** -0.5) * col_scale
    return {
        "x": jax.random.normal(ks[0], (BATCH, SEQ, D_MODEL), f32),
        "ln0_g": 1.0 + nrm(ks[2], (D_MODEL,), 0.05),
        "ln0_b": nrm(ks[3], (D_MODEL,), 0.02),
        "w_in": w_in,
        "b_in": nrm(ks[4], (DEPTH, IN_WIDTH), 0.02),
        "na_rpb": nrm(ks[5], (DEPTH, NA_HEADS, 2 * NA_KH_MAX - 1, 2 * NA_KW - 1), 0.1),
        "swa_sink": nrm(ks[6], (DEPTH, SWA_Q_HEADS), 0.5),
        "w_branch_na": nrm(ks[7], (DEPTH, NA_WIDTH, D_MODEL), NA_WIDTH ** -0.5 * beta),
        "w_branch_swa": nrm(ks[8], (DEPTH, SWA_Q_WIDTH, D_MODEL), SWA_Q_WIDTH ** -0.5 * beta),
        "w_out": nrm(ks[9], (DEPTH, D_MODEL, D_MODEL), D_MODEL ** -0.5 * beta),
        "b_out": nrm(ks[10], (DEPTH, D_MODEL), 0.02),
        "ln1_g": 1.0 + nrm(ks[11], (DEPTH, D_MODEL), 0.05),
        "ln1_b": nrm(ks[12], (DEPTH, D_MODEL), 0.02),
        "w_ff1": nrm(ks[13], (DEPTH, D_MODEL, D_FF), D_MODEL ** -0.5 * beta),
        "b_ff1": nrm(ks[14], (DEPTH, D_FF), 0.02),
        "w_ff2": nrm(ks[15], (DEPTH, D_FF, D_MODEL), D_FF ** -0.5 * beta),
        "b_ff2": nrm(ks[16], (DEPTH, D_MODEL), 0.02),
        "ln2_g": 1.0 + nrm(ks[17], (DEPTH, D_MODEL), 0.05),
        "ln2_b": nrm(ks[18], (DEPTH, D_MODEL), 0.02),
    }


def reference(x, ln0_g, ln0_b, w_in, b_in, na_rpb, swa_sink, w_branch_na, w_branch_swa,
              w_out, b_out, ln1_g, ln1_b, w_ff1, b_ff1, w_ff2, b_ff2, ln2_g, ln2_b):
    B, S, _ = x.shape
    pos = jnp.arange(S)
    h = layer_norm(x, ln0_g, ln0_b)
    for l in range(DEPTH):
        proj = h @ w_in[l] + b_in[l]
        qa, ka, va, qb, kb, vb, ga, gb = jnp.split(proj, IN_SPLITS, axis=-1)
        qa = qa.reshape(B, S, NA_HEADS, HEAD_DIM)
        ka = ka.reshape(B, S, NA_HEADS, HEAD_DIM)
        va = va.reshape(B, S, NA_HEADS, HEAD_DIM)
        qb = partial_rotary(qb.reshape(B, S, SWA_Q_HEADS, HEAD_DIM), pos)
        kb = partial_rotary(kb.reshape(B, S, SWA_KV_HEADS, HEAD_DIM), pos)
        vb = vb.reshape(B, S, SWA_KV_HEADS, HEAD_DIM)
        y_na = neighbourhood_attention(qa, ka, va, na_rpb[l]) @ w_branch_na[l]
        y_swa = windowed_gqa_with_sink(qb, kb, vb, swa_sink[l]) @ w_branch_swa[l]
        mixed = jax.nn.sigmoid(ga) * y_na + jax.nn.sigmoid(gb) * y_swa
        attn_out = mixed @ w_out[l] + b_out[l]
        h = layer_norm(DEEPNORM_ALPHA * h + attn_out, ln1_g[l], ln1_b[l])
        u = jnp.square(jax.nn.relu(h @ w_ff1[l] + b_ff1[l]))
        ffn_out = u @ w_ff2[l] + b_ff2[l]
        h = layer_norm(DEEPNORM_ALPHA * h + ffn_out, ln2_g[l], ln2_b[l])
    return h
```

```python
import numpy as np
from contextlib import ExitStack
import concourse.bass as bass
import concourse.mybir as mybir
from concourse.bass_utils import run_bass_kernel_spmd

F32 = mybir.dt.float32
BF16 = mybir.dt.bfloat16
AF = mybir.ActivationFunctionType
ALU = mybir.AluOpType

ENGS = ("tensor", "vector", "scalar", "gpsimd", "sync")

NCORES = 8
D = 1024
NT = 20
NTO = 16
TOK = NT * 128
HALO = 256
IN_W = 4352
NEG = -30000.0
ALPHA = 2.0 ** 0.25
EPS = 1e-5


class Prog:
    def __init__(self, nc):
        self.nc = nc
        self.ops = {e: [] for e in ENGS}
        self.cnt = {e: 0 for e in ENGS}
        self.last_w = {}
        self.readers = {}
        self.waited = {e: {} for e in ENGS}
        self.dma_sems = {}
        self.final = {}
        self.alias = {}
        self.rename = {}
        self.expand = {}

    def register(self, phase, names, lo, hi):
        for n in names:
            q = phase + "." + n
            self.rename[n] = q
            self.alias.setdefault(q, []).append((lo, hi))

    def _map(self, names):
        out = []
        for n in names:
            for m in self.expand.get(n, [n]):
                out.append(self.rename.get(m, m))
        return out

    def _overlaps(self, r):
        res = []
        for (lo, hi) in self.alias.get(r, ()):
            for q, ivs in self.alias.items():
                if q == r:
                    continue
                for (a, b) in ivs:
                    if a < hi and lo < b:
                        res.append(q)
                        break
        return res

    def _need(self, eng, waits, tok):
        if tok is None:
            return
        key, val = tok
        if key == eng and eng == "tensor":
            return
        if val > self.waited[eng].get(key, 0):
            self.waited[eng][key] = val
            waits[key] = max(waits.get(key, 0), val)

    def op(self, eng, fn, reads=(), writes=(), dma_sem=None):
        waits = {}
        reads = self._map(reads)
        writes = self._map(writes)
        for r in reads:
            self._need(eng, waits, self.last_w.get(r))
        for r in writes:
            self._need(eng, waits, self.last_w.get(r))
            for k, v in self.readers.get(r, {}).items():
                self._need(eng, waits, (k, v))
            for q in self._overlaps(r):
                self._need(eng, waits, self.last_w.get(q))
                for k, v in self.readers.get(q, {}).items():
                    self._need(eng, waits, (k, v))
        if dma_sem is not None:
            ent = self.dma_sems.setdefault(dma_sem, [0])
            ent[0] += 16
            tok = ("dma:" + dma_sem, ent[0])
            inc = ("dma:" + dma_sem, 16)
        else:
            self.cnt[eng] += 1
            tok = (eng, self.cnt[eng])
            inc = (eng, 1)
        for r in reads:
            d = self.readers.setdefault(r, {})
            d[tok[0]] = max(d.get(tok[0], 0), tok[1])
        for r in writes:
            self.last_w[r] = tok
            self.readers[r] = {}
        self.ops[eng].append((waits, fn, inc))
        return tok

    def barrier(self):
        snap = {e: self.cnt[e] for e in ENGS if self.cnt[e] > 0}
        for k, v in self.dma_sems.items():
            snap["dma:" + k] = v[0]
        for e in ENGS:
            waits = {}
            for k, v in snap.items():
                if k == e:
                    continue
                if v > self.waited[e].get(k, 0):
                    self.waited[e][k] = v
                    waits[k] = v
            if waits:
                self.ops[e].append((waits, None, None))

    def finish(self, eng, toks):
        d = self.final.setdefault(eng, {})
        for k, v in toks:
            d[k] = max(d.get(k, 0), v)

    def emit(self):
        nc = self.nc
        with ExitStack() as st:
            sems = {}
            for e in ENGS:
                sems[e] = st.enter_context(nc.semaphore("s_" + e))
            for k in self.dma_sems:
                sems["dma:" + k] = st.enter_context(nc.semaphore("d_" + k))
            block = st.enter_context(nc.Block())

            def make(eng):
                def body(e):
                    for waits, fn, inc in self.ops[eng]:
                        for k, v in waits.items():
                            e.wait_ge(sems[k], v)
                        if fn is not None:
                            ins = fn(e)
                            ins.then_inc(sems[inc[0]], inc[1])
                    for k, v in self.final.get(eng, {}).items():
                        e.wait_ge(sems[k], v)
                return body

            for eng in ENGS:
                if self.ops[eng] or eng in self.final:
                    getattr(block, eng)(make(eng))


BC_QA, BC_KA, BC_QB, BC_QBS, BC_KB, BC_KBS, BC_GA, BC_GB, BC_FF1 = 0, 4, 8, 12, 16, 18, 20, 28, 36
BC_G0, BC_B0, BC_G1, BC_B1 = 68, 76, 84, 92
NBCOL = 100
BR_LN0G, BR_LN0B, BR_BV, BR_BOUT, BR_LN1G, BR_LN1B, BR_BFF2, BR_LN2G, BR_LN2B = (
    0, 1024, 2048, 2688, 3712, 4736, 5760, 6784, 7808)
NBROW = 8832


def build_program(stop_after=None, dumps=()):
    nc = bass.Bass("TRN2", target_bir_lowering=False)

    def din(name, shape, dt=F32):
        return nc.dram_tensor(name, list(shape), dt, kind="ExternalInput").ap()

    x_d = din("x", [TOK, D])
    w_in_d = din("w_in", [D, IN_W])
    perm_d = din("perm", [128, 128])
    w_na_d = din("w_na", [512, D])
    w_swa_d = din("w_swa", [512, D])
    w_out_d = din("w_out", [D, D])
    w_ff1_d = din("w_ff1", [D, 4096])
    w_ff2_d = din("w_ff2", [4096, D])
    bcol_d = din("bcol", [128, NBCOL])
    brow_d = din("brow", [128, NBROW])
    sink_d = din("sink", [128, 8])
    nab_d = din("nabias", [4, 128, 6400])
    smask_d = din("swamask", [128, 1536])
    cq_d = din("rot_cq", [128, 2048])
    sq_d = din("rot_sq", [128, 2048])
    ck_d = din("rot_ck", [128, TOK])
    sk_d = din("rot_sk", [128, TOK])
    ident_d = din("ident", [128, 128])
    out_d = nc.dram_tensor("out", [NTO * 128, D], F32, kind="ExternalOutput").ap()
    dump_d = {}
    for name, shape, dt in dumps:
        dump_d[name] = nc.dram_tensor("dbg_" + name, list(shape), dt, kind="ExternalOutput").ap()

    w_in_v = w_in_d.rearrange("(kc p) c -> p kc c", p=128)
    w_na_v = w_na_d.rearrange("(kc p) c -> p kc c", p=128)
    w_swa_v = w_swa_d.rearrange("(kc p) c -> p kc c", p=128)
    w_out_v = w_out_d.rearrange("(kc p) c -> p kc c", p=128)
    w_ff1_v = w_ff1_d.rearrange("(kc p) c -> p kc c", p=128)
    w_ff2_v = w_ff2_d.rearrange("(kc p) c -> p kc c", p=128)

    with ExitStack() as st:
        def sb(name, shape, dt):
            return st.enter_context(nc.sbuf_tensor("sb_" + name, list(shape), dt))

        def ps(name, shape, dt=F32):
            return st.enter_context(nc.psum_tensor("ps_" + name, list(shape), dt))

        R = sb("R", [128, NTO, D], F32)
        hT = sb("hT", [128, 8, TOK], BF16)
        ident = sb("ident", [128, 128], BF16)
        permb = sb("permb", [128, 128], BF16)
        bcol = sb("bcol", [128, NBCOL], F32)
        esink = sb("esink", [128, 8], F32)
        small = sb("small", [128, 64], F32)
        epsc = sb("epsc", [128, 1], F32)
        sinkrow = sb("sinkrow", [1, 4 * 256], BF16)
        indrow = sb("indrow", [1, 2 * 128], BF16)
        ARENA = 50960
        arena = sb("arena", [128, ARENA], BF16)

        psS = ps("psS", [128, 2048])
        psA = [ps("psA0", [128, 512]), ps("psA1", [128, 512])]
        psT = [ps("psT0", [128, 1024], BF16), ps("psT1", [128, 1024], BF16)]
        psO = [psT[0][:, :].bitcast(F32), psT[1][:, :].bitcast(F32)]

        class Bump:
            def __init__(self, phase="X"):
                self.off = 0
                self.phase = phase
                self.names = None

            def at(self, off):
                self.off = off
                return self

            def n(self, *names):
                self.names = list(names)
                return self

            def take(self, nelem_bf16):
                a = self.off
                self.off += (nelem_bf16 + 15) // 16 * 16
                assert self.off <= ARENA, self.off
                assert self.names, "arena buffer needs resource names"
                P.register(self.phase, self.names, a, self.off)
                self.names = None
                return a

            def bf(self, shape):
                n = int(np.prod(shape))
                a = self.take(n)
                v = arena[:, a:a + n]
                if len(shape) == 2:
                    return v.rearrange("p (a b) -> p a b", b=shape[1])
                if len(shape) == 3:
                    return v.rearrange("p (a b c) -> p a b c", b=shape[1], c=shape[2])
                return v

            def f32(self, shape):
                n = int(np.prod(shape))
                a = self.take(2 * n)
                v = arena[:, a:a + 2 * n].bitcast(F32)
                if len(shape) == 2:
                    return v.rearrange("p (a b) -> p a b", b=shape[1])
                return v

        P = Prog(nc)
        P.expand = {"psS0": ["pb0", "pb1"], "psS1": ["pb2", "pb3"], "psA0": ["pb4"], "psA1": ["pb5"],
                    "psT0": ["pb6"], "psT1": ["pb7"],
                    "bank0": ["pb4"], "bank1": ["pb5"], "bank2": ["pb0"], "bank3": ["pb1"], "bank4": ["pb2"],
                    "bank5": ["pb3"], "bank6": ["pb6"], "bank7": ["pb7"],
                    "xb0": ["pb4"], "xb1": ["pb5"], "xb2": ["pb6"], "xb3": ["pb7"],
                    "yb0": ["pb0"], "yb1": ["pb1"], "yb2": ["pb2"], "yb3": ["pb3"]}

        def dma(out, in_):
            return lambda e: e.dma_start(out=out, in_=in_)

        def mm_group(out_ps, pairs):
            def fn(e):
                n = len(pairs)
                ins = None
                for i, (l, r) in enumerate(pairs):
                    ins = e.matmul(out_ps, lhsT=l, rhs=r, start=(i == 0), stop=(i == n - 1))
                return ins
            return fn

        P.op("gpsimd", dma(ident[:], ident_d), writes=["ident"], dma_sem="c0")
        P.op("gpsimd", dma(permb[:], perm_d), writes=["permb"], dma_sem="c5")
        P.op("sync", dma(bcol[:], bcol_d), writes=["bcol"], dma_sem="c1")
        P.op("sync", dma(esink[:], sink_d), writes=["esink"], dma_sem="c2")
        P.op("vector", lambda e: e.memset(epsc[:], EPS), writes=["epsc"])
        P.op("scalar", lambda e: e.activation(out=esink[:], in_=esink[:], func=AF.Exp),
             reads=["esink"], writes=["esink"])

        def f_ind(e):
            e.memset(indrow[0:1, 0:64], 0.0)
            e.memset(indrow[0:1, 64:128], 1.0)
            e.memset(indrow[0:1, 128:192], 1.0)
            return e.memset(indrow[0:1, 192:256], 0.0)
        P.op("vector", f_ind, writes=["indrow"])
        P.op("vector", lambda e: e.memset(sinkrow[:], 1.0), writes=["sinkrow"])

        def f_srow(e):
            ins = None
            for g_ in range(2):
                for e_ in range(2):
                    for c_ in range(2):
                        h_ = 4 * g_ + 2 * c_ + e_
                        o_ = (g_ * 2 + e_) * 256 + c_ * 128
                        ins = e.tensor_scalar(out=sinkrow[0:1, o_:o_ + 128], in0=sinkrow[0:1, o_:o_ + 128],
                                              scalar1=esink[0:1, h_:h_ + 1], scalar2=None, op0=ALU.mult)
            return ins
        P.op("vector", f_srow, reads=["esink"], writes=["sinkrow"])

        def layer_norm(src, dst, g_bc, b_bc, k, res_src, res_dst, defer=None):
            o = (k % 4) * 16
            stt = small[:, o:o + 12]
            mv = small[:, o + 12:o + 14]
            rs = small[:, o + 14:o + 15]
            nm = small[:, o + 15:o + 16]
            sn = "lnst%d" % (k % 4)

            def f_stats(e):
                e.bn_stats(stt[:, 0:6], src[:, 0:512])
                return e.bn_stats(stt[:, 6:12], src[:, 512:1024])
            P.op("vector", f_stats, reads=[res_src], writes=[sn + "a"])
            P.op("vector", lambda e: e.bn_aggr(mv, stt), reads=[sn + "a"], writes=[sn + "b"])
            P.op("scalar", lambda e: e.activation(out=rs, in_=mv[:, 1:2], func=AF.Sqrt, bias=epsc[:, 0:1], scale=1.0),
                 reads=[sn + "b", "epsc"], writes=[sn + "c0"])
            P.op("vector", lambda e: e.tensor_scalar(out=nm, in0=mv[:, 0:1], scalar1=-1.0, scalar2=None, op0=ALU.mult),
                 reads=[sn + "b"], writes=[sn + "d"])
            P.op("vector", lambda e: e.reciprocal(out=rs, in_=rs), reads=[sn + "c0"], writes=[sn + "c"])
            P.op("vector", lambda e: e.tensor_scalar(out=src, in0=src, scalar1=nm, scalar2=rs, op0=ALU.add, op1=ALU.mult),
                 reads=[sn + "d", sn + "c"], writes=[res_src])
            P.op("gpsimd", lambda e: e.tensor_tensor(out=src, in0=src, in1=g_bc, op=ALU.mult),
                 reads=["lnparams"], writes=[res_src])
            def fin():
                P.op("vector", lambda e: e.tensor_tensor(out=dst, in0=src, in1=b_bc, op=ALU.add),
                     reads=[res_src, "lnparams"], writes=[res_dst])
            if defer is None:
                fin()
            else:
                defer.append(fin)

        def to_feature_major(hf, res_hf, hb, res_hb, k, tok0):
            P.op("scalar", lambda e: e.activation(out=hb, in_=hf, func=AF.Identity), reads=[res_hf], writes=[res_hb])
            pt = psT[k % 2]
            rp = "psT%d" % (k % 2)

            def f_tr(e):
                ins = None
                for c in range(8):
                    ins = e.transpose(out=pt[:, c * 128:(c + 1) * 128], in_=hb[:, c * 128:(c + 1) * 128],
                                      identity=ident[:])
                return ins
            P.op("tensor", f_tr, reads=[res_hb, "ident"], writes=[rp])
            ptv = pt[:, :].rearrange("p (c t) -> p c t", t=128)
            eng = "scalar"
            if eng == "vector":
                P.op("vector", lambda e: e.tensor_copy(out=hT[:, :, tok0:tok0 + 128], in_=ptv),
                     writes=[rp, "hT:%d" % (tok0 // 128)])
            else:
                P.op("scalar", lambda e: e.activation(out=hT[:, :, tok0:tok0 + 128], in_=ptv, func=AF.Identity),
                     writes=[rp, "hT:%d" % (tok0 // 128)])

        def hT_res(tok0, n):
            return ["hT:%d" % t for t in range(tok0 // 128, (tok0 + n) // 128)]

        class Pipe:
            def __init__(self):
                self.jobs = []

            def push(self, stages):
                self.jobs.append(list(stages))

            def tick(self):
                for job in list(self.jobs):
                    job.pop(0)()
                    if not job:
                        self.jobs.remove(job)

            def drain(self):
                while self.jobs:
                    self.tick()

        bA = Bump("A")
        xs = [bA.n("xs%d" % i).f32([1024]) for i in range(4)]
        hbA = [bA.n("hb%d" % i).bf([1024]) for i in range(3)]
        prm = bA.n("lnparams").f32([2, 1024])
        P.op("sync", dma(prm[:, 0, :], brow_d[:, BR_LN0G:BR_LN0G + 1024]), writes=["lnparams"], dma_sem="prm")
        P.op("sync", dma(prm[:, 1, :], brow_d[:, BR_LN0B:BR_LN0B + 1024]), writes=["lnparams"], dma_sem="prm")
        def ln0_job(t):
            xb = xs[t % 4]
            rx = "xs%d" % (t % 4)
            own = 2 <= t < 2 + NTO
            hb = hbA[t % 3]
            rhb = "hb%d" % (t % 3)
            o = (t % 4) * 16
            stt = small[:, o:o + 12]
            mv = small[:, o + 12:o + 14]
            rs = small[:, o + 14:o + 15]
            nm = small[:, o + 15:o + 16]
            sn = "lnst%d" % (t % 4)
            pt = psT[t % 2]
            rp = "psT%d" % (t % 2)
            ptv = pt[:, :].rearrange("p (c q) -> p c q", q=128)

            def st_ld():
                P.op("sync", dma(xb, x_d[t * 128:(t + 1) * 128, :]), writes=[rx], dma_sem=rx)

            def st_stats():
                def f_stats(e):
                    e.bn_stats(stt[:, 0:6], xb[:, 0:512])
                    return e.bn_stats(stt[:, 6:12], xb[:, 512:1024])
                P.op("vector", f_stats, reads=[rx], writes=[sn + "a"])
                P.op("vector", lambda e: e.bn_aggr(mv, stt), reads=[sn + "a"], writes=[sn + "b"])
                P.op("scalar", lambda e: e.activation(out=rs, in_=mv[:, 1:2], func=AF.Sqrt, bias=epsc[:, 0:1], scale=1.0),
                     reads=[sn + "b", "epsc"], writes=[sn + "c0"])

            def st_norm():
                P.op("vector", lambda e: e.reciprocal(out=rs, in_=rs), reads=[sn + "c0"], writes=[sn + "c"])
                P.op("vector", lambda e: e.tensor_scalar(out=nm, in0=mv[:, 0:1], scalar1=-1.0, scalar2=rs,
                                                         op0=ALU.mult, op1=ALU.mult),
                     reads=[sn + "b", sn + "c"], writes=[sn + "d"])
                P.op("scalar", lambda e: e.activation(out=hb, in_=xb, func=AF.Identity, bias=nm, scale=rs),
                     reads=[rx, sn + "c", sn + "d"], writes=[rhb])
                if own:
                    P.op("scalar", lambda e: e.activation(out=R[:, t - 2, :], in_=xb, func=AF.Identity, bias=nm, scale=rs),
                         reads=[rx, sn + "c", sn + "d"], writes=["R:%d" % (t - 2)])

            def st_tr():
                def f_tr(e):
                    ins = None
                    for c in range(8):
                        ins = e.transpose(out=pt[:, c * 128:(c + 1) * 128], in_=hb[:, c * 128:(c + 1) * 128],
                                          identity=ident[:])
                    return ins
                P.op("tensor", f_tr, reads=[rhb, "ident"], writes=[rp])

            def st_evac():
                def f_ev(e):
                    ins = None
                    for c in range(5):
                        ins = e.tensor_scalar(out=hT[:, c, t * 128:(t + 1) * 128], in0=ptv[:, c, :],
                                              scalar1=bcol[:, BC_G0 + c:BC_G0 + c + 1],
                                              scalar2=bcol[:, BC_B0 + c:BC_B0 + c + 1], op0=ALU.mult, op1=ALU.add)
                    return ins
                P.op("vector", f_ev, reads=["bcol"], writes=[rp, "hT:%d" % t])

                def f_ev2(e):
                    ins = None
                    for c in range(5, 8):
                        ins = e.activation(out=hT[:, c, t * 128:(t + 1) * 128], in_=ptv[:, c, :], func=AF.Identity,
                                           bias=bcol[:, BC_B0 + c:BC_B0 + c + 1], scale=bcol[:, BC_G0 + c:BC_G0 + c + 1])
                    return ins
                P.op("scalar", f_ev2, reads=["bcol"], writes=[rp, "hT:%d" % t])
            return [st_ld, st_stats, st_norm, st_tr, st_evac]

        pipeA = Pipe()
        for t in range(NT):
            pipeA.push(ln0_job(t))
            pipeA.tick()
        pipeA.drain()

        def r_fixup(tt):
            P.op("gpsimd", lambda e: e.tensor_tensor(out=R[:, tt, :], in0=R[:, tt, :], in1=prm[:, 0, :], op=ALU.mult),
                 reads=["lnparams"], writes=["R:%d" % tt])
            P.op("gpsimd", lambda e: e.tensor_tensor(out=R[:, tt, :], in0=R[:, tt, :], in1=prm[:, 1, :], op=ALU.add),
                 reads=["lnparams"], writes=["R:%d" % tt])

        def do_dumps(which):
            toks = []
            for name, ap in which:
                if name in dump_d:
                    toks.append(P.op("sync", dma(dump_d[name], ap), reads=[], dma_sem="dump"))
            return toks

        def finish_now(extra_toks):
            P.barrier()
            P.finish("sync", extra_toks)
            P.emit()
            return nc

        if stop_after == "A":
            P.barrier()
            toks = do_dumps([("hT", hT[:]), ("R", R[:])])
            return finish_now(toks)

        bB = Bump("B")
        attn_na = bB.n("attn_na").bf([4, 2048])
        attn_swa = bB.n("attn_swa").bf([4, 2048])
        qT = bB.n("qT").bf([2, 2048])
        kT = bB.n("kT").bf([TOK])
        vaug = bB.n("vaug", "vones").bf([NT, 192])
        PT = [bB.n("PT%d" % i).bf([768]) for i in range(2)]
        btab = bB.n("btab").bf([6400])
        smask = bB.n("smask").bf([1536])
        wq = bB.n("wq").bf([8, 256])
        qbb = [bB.n("qbb%d" % i).bf([512]) for i in range(2)]
        wk = bB.n("wk").bf([8, 128])
        wv = bB.n("wv").bf([8, 128])
        rotC = [bB.n("rotC%d" % i).f32([512]) for i in range(2)]
        rotS = [bB.n("rotS%d" % i).f32([512]) for i in range(2)]
        tmpa = [bB.n("tmpa%d" % i).f32([512]) for i in range(2)]
        tmpb = bB.n("tmpb").f32([512])
        bv_bc = bB.n("bv").f32([640])
        rec = [bB.n("rec%d" % i, "rec%da" % i, "rec%db" % i).f32([256]) for i in range(2)]
        assert bB.off == 50944, bB.off

        P.op("sync", dma(bv_bc, brow_d[:, BR_BV:BR_BV + 640]), writes=["bv"], dma_sem="c3")
        P.op("gpsimd", dma(smask, smask_d), writes=["smask"], dma_sem="c4")
        P.op("gpsimd", lambda e: e.memset(vaug[:, :, 64:128], 1.0), writes=["vones"])

        psS_v = psS[:, :].rearrange("p (b n) -> p b n", b=2)

        def qcls(j):
            return {2: 0, 3: 1, 16: 3, 17: 4}.get(j, 2)

        def scls(j):
            return {2: 0, 17: 2}.get(j, 1)

        def project_fm(w_t, ncol0, tok0, bias_col, dst_ap, res_dst, k, eng, scale=None, wres="wk"):
            pa = psA[k % 2]
            rp = "psA%d" % (k % 2)
            P.op("tensor", mm_group(pa[:], [(w_t[:, kc, ncol0:ncol0 + 128], hT[:, kc, tok0:tok0 + 512])
                                            for kc in range(8)]),
                 reads=[wres] + hT_res(tok0, 512), writes=[rp])
            bc = bcol[:, bias_col:bias_col + 1]
            if eng == "split":
                def f_split(e):
                    ins = None
                    for (lo, hi, dap) in dst_ap:
                        ins = e.tensor_scalar(out=dap, in0=pa[lo:hi, :], scalar1=bcol[lo:hi, bias_col:bias_col + 1],
                                              scalar2=scale, op0=ALU.add, op1=ALU.mult)
                    return ins
                P.op("vector", f_split, reads=["bcol"], writes=[rp, res_dst])
            elif eng == "vector":
                P.op("vector", lambda e: e.tensor_scalar(out=dst_ap, in0=pa[:], scalar1=bc, scalar2=scale,
                                                         op0=ALU.add, op1=ALU.mult),
                     reads=["bcol"], writes=[rp, res_dst])
            else:
                P.op("scalar", lambda e: e.activation(out=dst_ap, in_=pa[:], func=AF.Identity, bias=bc, scale=1.0),
                     reads=["bcol"], writes=[rp, res_dst])

        def project_v(ncols, bvoff, dsts, k0):
            kk = k0
            for t0 in range(0, NT, 4):
                pa = psA[kk % 2]
                rp = "psA%d" % (kk % 2)

                def fn(e, t0=t0, pa=pa):
                    ins = None
                    for i in range(4):
                        for kc in range(8):
                            ins = e.matmul(pa[:, i * ncols:(i + 1) * ncols],
                                           lhsT=hT[:, kc, (t0 + i) * 128:(t0 + i + 1) * 128],
                                           rhs=wv[:, kc, 0:ncols], start=(kc == 0), stop=(kc == 7))
                    return ins
                P.op("tensor", fn, reads=["wv"] + hT_res(t0 * 128, 512), writes=[rp])
                pav = pa[:, 0:4 * ncols].rearrange("p (t c) -> p t c", t=4)
                for (src_off, dst_off) in dsts:
                    for i in range(4):
                        P.op("vector", lambda e, i=i, so=src_off, do=dst_off, pav=pav, t0=t0:
                             e.tensor_tensor(out=vaug[:, t0 + i, do:do + 64], in0=pav[:, i, so:so + 64],
                                             in1=bv_bc[:, bvoff + so:bvoff + so + 64], op=ALU.add),
                             reads=["bv"], writes=[rp, "vaug"])
                kk += 1
            return kk

        kctr = 0
        P.op("gpsimd", lambda e: e.memset(qT[:, :, :], 0.0), writes=["qT"])
        for hp in range(4):
            P.op("gpsimd", dma(wq[:, :, 0:128], w_in_v[:, :, 128 * hp:128 * hp + 128]), writes=["wq"], dma_sem="wq")
            P.op("gpsimd", dma(wk, w_in_v[:, :, 512 + 128 * hp:512 + 128 * hp + 128]), writes=["wk"], dma_sem="wk")
            P.op("gpsimd", dma(wv, w_in_v[:, :, 1024 + 128 * hp:1024 + 128 * hp + 128]), writes=["wv"], dma_sem="wv")
            P.op("gpsimd", dma(btab, nab_d[hp]), writes=["btab"], dma_sem="bt")
            for tt in range(4 * hp, 4 * hp + 4):
                r_fixup(tt)
            for nb in range(4):
                project_fm(wq, 0, HALO + 512 * nb, BC_QA + hp,
                           [(0, 64, qT[0:64, 0, 512 * nb:512 * nb + 512]), (64, 128, qT[64:128, 1, 512 * nb:512 * nb + 512])],
                           "qT", kctr, "split", scale=0.125, wres="wq")
                kctr += 1
            for kb in range(5):
                project_fm(wk, 0, 512 * kb, BC_KA + hp, kT[:, 512 * kb:512 * kb + 512], "kT", kctr, "scalar")
                kctr += 1
            kctr = project_v(128, 128 * hp, [(0, 0), (64, 128)], kctr)
            if stop_after == "B0":
                P.barrier()
                toks = do_dumps([("attn_na", attn_na), ("attn_swa", attn_swa), ("qT", qT), ("kT", kT), ("vaug", vaug)])
                return finish_now(toks)

            items = [(2 + 2 * jp + jj, e) for jp in range(NTO // 2) for e in range(2) for jj in range(2)]

            def na_scores(w, j, e, hp=hp):
                buf = w % 2
                cl = qcls(j)

                def fn(en):
                    ins = None
                    bo0 = ((cl * 5 + 0) * 2 + e) * 128
                    rhs4 = btab[:, bo0:bo0 + 1024].rearrange("p (s x) -> p s x", x=256)[:, :, 0:128]
                    en.matmul(psS_v[:, buf, 0:512].rearrange("p (s q) -> p s q", q=128), lhsT=ident[:], rhs=rhs4,
                              start=True, stop=False)
                    bo4 = ((cl * 5 + 4) * 2 + e) * 128
                    en.matmul(psS_v[:, buf, 512:640], lhsT=ident[:], rhs=btab[:, bo4:bo4 + 128], start=True, stop=False)
                    for si in range(5):
                        t = j + si - 2
                        o = psS_v[:, buf, si * 128:(si + 1) * 128]
                        ins = en.matmul(o, lhsT=kT[:, t * 128:(t + 1) * 128],
                                        rhs=qT[:, e, (j - 2) * 128:(j - 1) * 128], start=False, stop=(si >= 3))
                    return ins
                P.op("tensor", fn, reads=["kT", "qT", "btab", "ident"], writes=["psS%d" % buf])
                P.op("scalar", lambda en: en.activation(out=PT[buf][:, 0:640], in_=psS_v[:, buf, 0:640], func=AF.Exp),
                     writes=["psS%d" % buf, "PT%d" % buf])

            def na_pv(w, j, e, hp=hp):
                buf = w % 2
                ob = (w // 2) % 2
                jj = w % 2
                po = psO[ob][:, jj * 128:(jj + 1) * 128]

                def fn(en):
                    ins = None
                    for si in range(5):
                        t = j + si - 2
                        ins = en.matmul(po, lhsT=vaug[:, t, 64 * e:64 * e + 128],
                                        rhs=PT[buf][:, si * 128:(si + 1) * 128], start=(si == 0), stop=(si == 4))
                    return ins
                P.op("tensor", fn, reads=["PT%d" % buf, "vaug", "vones"], writes=["psT%d" % ob])
                if jj == 0:
                    return
                dl, dh = (64, 128) if e == 0 else (0, 64)
                ol, oh = (0, 64) if e == 0 else (64, 128)
                rc = rec[ob]
                pp = psO[ob][:, 0:256]
                j0 = j - 1
                P.op("scalar", lambda en: en.activation(out=rc[dl:dh, 0:256], in_=pp[dl:dh, :], func=AF.Ln),
                     writes=["psT%d" % ob, "rec%da" % ob])
                P.op("scalar", lambda en: en.activation(out=rc[dl:dh, 0:256], in_=rc[dl:dh, 0:256], func=AF.Exp, scale=-1.0),
                     reads=["rec%da" % ob], writes=["rec%d" % ob])
                P.op("vector", lambda en: en.tensor_tensor(out=attn_na[ol:oh, hp, (j0 - 2) * 128:j0 * 128],
                                                           in0=pp[ol:oh, :], in1=rc[dl:dh, 0:256], op=ALU.mult),
                     reads=["rec%d" % ob], writes=["psT%d" % ob, "attn_na"])

            for w in range(len(items) + 1):
                if w < len(items):
                    na_scores(w, *items[w])
                if w >= 1:
                    na_pv(w - 1, *items[w - 1])
            if stop_after == "B1":
                P.barrier()
                toks = do_dumps([("attn_na", attn_na), ("attn_swa", attn_swa), ("qT", qT), ("kT", kT), ("vaug", vaug)])
                return finish_now(toks)

        kTb = btab[:, 0:TOK]
        for g in range(2):
            P.op("gpsimd", dma(wq, w_in_v[:, :, 1536 + 256 * g:1536 + 256 * g + 256]), writes=["wq"], dma_sem="wq")
            for hh in range(2):
                P.op("gpsimd", dma(wk[:, :, 64 * hh:64 * hh + 64], w_in_v[:, :, 2048 + 64 * g:2048 + 64 * g + 64]),
                     writes=["wk"], dma_sem="wk")
            P.op("gpsimd", dma(wv[:, :, 0:64], w_in_v[:, :, 2176 + 64 * g:2176 + 64 * g + 64]), writes=["wv"], dma_sem="wv")
            if g == 0:
                P.op("gpsimd", lambda e: e.memset(kT[64:128, :], 0.0), writes=["kT"])
                P.op("gpsimd", lambda e: e.memset(kTb[0:64, :], 0.0), writes=["kT", "btab"])

            def rot_stages(w_a, wres, col0, tok0, c_d, s_d, tabtok0, bca, dst_ap, res_dst, k, split=None):
                rb = k % 2
                pa = psA[rb]
                rpa = "psA%d" % rb
                pp = psS[:, 1024 * rb:1024 * rb + 512]
                rpp = "psS%d" % rb
                qb = qbb[rb]
                ta = tmpa[rb]
                bc = bcol[:, bca:bca + 1]

                def stA():
                    P.op("sync", dma(rotC[rb], c_d[:, tabtok0:tabtok0 + 512]), writes=["rotC%d" % rb], dma_sem="rc%d" % rb)
                    P.op("sync", dma(rotS[rb], s_d[:, tabtok0:tabtok0 + 512]), writes=["rotS%d" % rb], dma_sem="rs%d" % rb)
                    P.op("tensor", mm_group(pa[:], [(w_a[:, kc, col0:col0 + 128], hT[:, kc, tok0:tok0 + 512])
                                                    for kc in range(8)]),
                         reads=[wres] + hT_res(tok0, 512), writes=[rpa])
                    P.op("scalar", lambda e: e.activation(out=qb, in_=pa[:], func=AF.Identity, bias=bc, scale=1.0),
                         reads=["bcol"], writes=[rpa, "qbb%d" % rb])
                    P.op("vector", lambda e: e.scalar_tensor_tensor(out=ta, in0=pa[:], scalar=bc, in1=rotC[rb],
                                                                    op0=ALU.add, op1=ALU.mult),
                         reads=["bcol", "rotC%d" % rb], writes=[rpa, "tmpa%d" % rb])

                def stB():
                    P.op("tensor", lambda e: e.matmul(pp, lhsT=permb[:], rhs=qb, start=True, stop=True),
                         reads=["permb", "qbb%d" % rb], writes=[rpp])
                    P.op("vector", lambda e: e.tensor_tensor(out=tmpb, in0=pp, in1=rotS[rb], op=ALU.mult),
                         reads=["rotS%d" % rb], writes=[rpp, "tmpb"])
                    if split is None:
                        P.op("gpsimd", lambda e: e.tensor_tensor(out=dst_ap, in0=ta, in1=tmpb, op=ALU.add),
                             reads=["tmpa%d" % rb, "tmpb"], writes=[res_dst])
                    else:
                        def f_sp(e):
                            ins = None
                            for (lo, hi, dap) in split:
                                ins = e.tensor_tensor(out=dap, in0=ta[lo:hi, :], in1=tmpb[lo:hi, :], op=ALU.add)
                            return ins
                        P.op("gpsimd", f_sp, reads=["tmpa%d" % rb, "tmpb"], writes=[res_dst, "btab"])
                return stA, stB

            blocks = []
            for c in range(2):
                for nb in range(4):
                    blocks.append(rot_stages(wq, "wq", 128 * c, HALO + 512 * nb, cq_d, sq_d, 512 * nb,
                                             BC_QB + 2 * g + c, qT[:, c, 512 * nb:512 * nb + 512], "qT", kctr))
                    kctr += 1
            for kb in range(5):
                blocks.append(rot_stages(wk, "wk", 0, 512 * kb, ck_d, sk_d, 512 * kb, BC_KB + g, None, "kT", kctr,
                                         split=[(0, 64, kT[0:64, 512 * kb:512 * kb + 512]),
                                                (64, 128, kTb[64:128, 512 * kb:512 * kb + 512])]))
                kctr += 1
            for bi in range(len(blocks) + 1):
                if bi < len(blocks):
                    blocks[bi][0]()
                if bi >= 1:
                    blocks[bi - 1][1]()
            kctr = project_v(64, 512 + 64 * g, [(0, 0), (0, 128)], kctr)

            items = [(j, e) for j in range(2, 2 + NTO) for e in range(2)]

            def sw_scores(w, j, e, g=g):
                buf = w % 2
                cl = scls(j)

                def fn(en):
                    ins = None
                    for si in range(3):
                        t = j + si - 1
                        o = psS_v[:, buf, si * 256:(si + 1) * 256]
                        kk_ = kT if e == 0 else kTb
                        ins = en.matmul(o.rearrange("p (c q) -> p c q", c=2),
                                        lhsT=kk_[:, t * 128:(t + 1) * 128],
                                        rhs=qT[:, :, (j - 2) * 128:(j - 1) * 128],
                                        start=True, stop=(si == 1))
                        if si != 1:
                            mo = (cl * 2 + (0 if si == 0 else 1)) * 256
                            ins = en.matmul(o, lhsT=ident[:], rhs=smask[:, mo:mo + 256], start=False, stop=True)
                    return ins
                P.op("tensor", fn, reads=["kT", "qT", "smask", "ident"], writes=["psS%d" % buf])
                P.op("scalar", lambda en: en.activation(out=PT[buf][:, 0:768], in_=psS_v[:, buf, 0:768], func=AF.Exp),
                     writes=["psS%d" % buf, "PT%d" % buf])

            def sw_pv(w, j, e, g=g):
                buf = w % 2
                ob = w % 2
                po = psO[ob][:, 0:256]

                def fn(en):
                    ins = None
                    for si in range(3):
                        t = j + si - 1
                        ins = en.matmul(po, lhsT=vaug[:, t, 64 * e:64 * e + 128],
                                        rhs=PT[buf][:, si * 256:(si + 1) * 256], start=(si == 0), stop=False)
                    so = (g * 2 + e) * 256
                    return en.matmul(po, lhsT=indrow[0:1, e * 128:(e + 1) * 128], rhs=sinkrow[0:1, so:so + 256],
                                     start=False, stop=True)
                P.op("tensor", fn, reads=["PT%d" % buf, "vaug", "vones", "indrow", "sinkrow"], writes=["psT%d" % ob])
                dl, dh = (64, 128) if e == 0 else (0, 64)
                ol, oh = (0, 64) if e == 0 else (64, 128)
                rc = rec[ob]

                P.op("vector", lambda en: en.reciprocal(out=rc[dl:dh, 128:256], in_=po[dl:dh, 128:256]),
                     writes=["psT%d" % ob, "rec%db" % ob])
                P.op("scalar", lambda en: en.activation(out=rc[dl:dh, 0:128], in_=po[dl:dh, 0:128], func=AF.Ln),
                     writes=["psT%d" % ob, "rec%da" % ob])
                P.op("scalar", lambda en: en.activation(out=rc[dl:dh, 0:128], in_=rc[dl:dh, 0:128], func=AF.Exp, scale=-1.0),
                     reads=["rec%da" % ob, "rec%db" % ob], writes=["rec%d" % ob])
                pov = po.rearrange("p (c q) -> p c q", c=2)
                rcv = rc[:, :].rearrange("p (c q) -> p c q", c=2)
                P.op("vector", lambda en: en.tensor_tensor(
                    out=attn_swa[ol:oh, 2 * g:2 * g + 2, (j - 2) * 128:(j - 1) * 128],
                    in0=pov[ol:oh], in1=rcv[dl:dh], op=ALU.mult),
                    reads=["rec%d" % ob], writes=["psT%d" % ob, "attn_swa"])

            for w in range(len(items) + 1):
                if w < len(items):
                    sw_scores(w, *items[w])
                if w >= 1:
                    sw_pv(w - 1, *items[w - 1])

        if stop_after == "B":
            P.barrier()
            toks = do_dumps([("attn_na", attn_na), ("attn_swa", attn_swa), ("qT", qT), ("kT", kT), ("vaug", vaug)])
            return finish_now(toks)

        bC = Bump("C1")
        mixed = bC.at(16384).n("mixed").bf([8, 2048])
        bC.at(36352)
        wg = [[bC.n("wg%d" % b).bf([8, 128]) for _ in range(2)] for b in range(2)]
        wb = [[bC.n("wb%d" % b).bf([4, 128]) for _ in range(2)] for b in range(2)]
        sg = [bC.n("sg%d" % i).f32([512]) for i in range(2)]
        tm = [bC.n("tm%d" % i).f32([512]) for i in range(2)]
        wout_a = bC.n("wout0").bf([8, 512])
        assert bC.off <= ARENA
        boutb = bC.at(32768).n("boutb").f32([1024])
        P.op("sync", dma(boutb, brow_d[:, BR_BOUT:BR_BOUT + 1024]), writes=["boutb"], dma_sem="bo")
        pre_tiles = list(range(NTO))
        banks = [psA[0][:], psA[1][:], psS[:, 0:512], psS[:, 512:1024], psS[:, 1024:1536], psS[:, 1536:2048],
                 psO[0], psO[1]]
        it = 0
        def load_f(f):
            fb = f % 2
            P.op("gpsimd", dma(wg[fb][0], w_in_v[:, :, 2304 + 128 * f:2304 + 128 * f + 128]), writes=["wg%d" % fb], dma_sem="wg%d" % fb)
            P.op("gpsimd", dma(wg[fb][1], w_in_v[:, :, 3328 + 128 * f:3328 + 128 * f + 128]), writes=["wg%d" % fb], dma_sem="wg%d" % fb)
            P.op("gpsimd", dma(wb[fb][0], w_na_v[:, :, 128 * f:128 * f + 128]), writes=["wb%d" % fb], dma_sem="wb%d" % fb)
            P.op("gpsimd", dma(wb[fb][1], w_swa_v[:, :, 128 * f:128 * f + 128]), writes=["wb%d" % fb], dma_sem="wb%d" % fb)

        load_f(0)
        for f in range(8):
            fb = f % 2
            if f + 1 < 8:
                load_f(f + 1)
            if f == 1:
                P.op("gpsimd", dma(wout_a, w_out_v[:, :, 0:512]), writes=["wout0"], dma_sem="wout0")
            for nb in range(4):
                s4 = (it % 2) * 4
                it += 1
                tok0 = HALO + 512 * nb
                pga, pgb, pyn, pys = banks[s4:s4 + 4]
                rn = ["bank%d" % (s4 + i) for i in range(4)]
                P.op("tensor", mm_group(pga, [(wg[fb][0][:, kc, :], hT[:, kc, tok0:tok0 + 512]) for kc in range(8)]),
                     reads=["wg%d" % fb] + hT_res(tok0, 512), writes=[rn[0]])
                P.op("tensor", mm_group(pgb, [(wg[fb][1][:, kc, :], hT[:, kc, tok0:tok0 + 512]) for kc in range(8)]),
                     reads=["wg%d" % fb] + hT_res(tok0, 512), writes=[rn[1]])
                P.op("tensor", mm_group(pyn, [(wb[fb][0][:, c, :], attn_na[:, c, 512 * nb:512 * nb + 512]) for c in range(4)]),
                     reads=["wb%d" % fb, "attn_na"], writes=[rn[2]])
                P.op("tensor", mm_group(pys, [(wb[fb][1][:, c, :], attn_swa[:, c, 512 * nb:512 * nb + 512]) for c in range(4)]),
                     reads=["wb%d" % fb, "attn_swa"], writes=[rn[3]])
                P.op("scalar", lambda e, pga=pga, f=f: e.activation(out=sg[0], in_=pga, func=AF.Sigmoid,
                                                                     bias=bcol[:, BC_GA + f:BC_GA + f + 1], scale=1.0),
                     reads=["bcol"], writes=[rn[0], "sg0"])
                P.op("scalar", lambda e, pgb=pgb, f=f: e.activation(out=sg[1], in_=pgb, func=AF.Sigmoid,
                                                                     bias=bcol[:, BC_GB + f:BC_GB + f + 1], scale=1.0),
                     reads=["bcol"], writes=[rn[1], "sg1"])
                P.op("vector", lambda e, pyn=pyn: e.tensor_tensor(out=tm[0], in0=pyn, in1=sg[0], op=ALU.mult),
                     reads=["sg0"], writes=[rn[2], "tm0"])
                P.op("vector", lambda e, pys=pys: e.tensor_tensor(out=tm[1], in0=pys, in1=sg[1], op=ALU.mult),
                     reads=["sg1"], writes=[rn[3], "tm1"])
                P.op("gpsimd", lambda e, f=f, nb=nb: e.tensor_tensor(out=mixed[:, f, 512 * nb:512 * nb + 512],
                                                                      in0=tm[0], in1=tm[1], op=ALU.add),
                     reads=["tm0", "tm1"], writes=["mixed"])
                if pre_tiles:
                    tt = pre_tiles.pop(0)
                    P.op("vector", lambda e, tt=tt: e.scalar_tensor_tensor(out=R[:, tt, :], in0=R[:, tt, :], scalar=ALPHA,
                                                                          in1=boutb, op0=ALU.mult, op1=ALU.add),
                         reads=["boutb"], writes=["R:%d" % tt])

        bD = Bump("C2")
        wout_b = bD.at(0).n("wout1").bf([8, 512])
        prm1 = bD.n("lnparams").f32([2, 1024])
        zb = [bD.n("z%d" % i).f32([1024]) for i in range(3)]
        hbC = [bD.n("hb%d" % i).bf([1024]) for i in range(2)]
        assert bD.off <= 16384
        wouts = [wout_a, wout_b]
        P.op("gpsimd", dma(wout_b, w_out_v[:, :, 512:1024]), writes=["wout1"], dma_sem="wout1")
        P.op("sync", dma(prm1[:, 0, :], brow_d[:, BR_LN1G:BR_LN1G + 1024]), writes=["lnparams"], dma_sem="prm")
        P.op("sync", dma(prm1[:, 1, :], brow_d[:, BR_LN1B:BR_LN1B + 1024]), writes=["lnparams"], dma_sem="prm")
        def c2_mm(tt):
            for half in range(2):
                P.op("tensor", mm_group(psA[half][:], [(mixed[:, f, tt * 128:(tt + 1) * 128],
                                                        wouts[half][:, f, :]) for f in range(8)]),
                     reads=["mixed", "wout%d" % half], writes=["psA%d" % half])

        def c2_add(tt):
            z = zb[tt % 3]
            rz = "z%d" % (tt % 3)
            for half in range(2):
                P.op("vector", lambda e, half=half, z=z, tt=tt: e.tensor_tensor(
                    out=z[:, 512 * half:512 * half + 512], in0=R[:, tt, 512 * half:512 * half + 512],
                    in1=psA[half][:], op=ALU.add),
                    reads=["R:%d" % tt], writes=["psA%d" % half, rz])

        def ln1_job(tt):
            z = zb[tt % 3]
            rz = "z%d" % (tt % 3)
            hb = hbC[tt % 2]
            rhb = "hb%d" % (tt % 2)
            o = (tt % 4) * 16
            stt = small[:, o:o + 12]
            mv = small[:, o + 12:o + 14]
            rs = small[:, o + 14:o + 15]
            nm = small[:, o + 15:o + 16]
            sn = "lnst%d" % (tt % 4)
            pt = psT[tt % 2]
            rp = "psT%d" % (tt % 2)
            ptv = pt[:, :].rearrange("p (c q) -> p c q", q=128)
            tok0 = HALO + tt * 128

            def st_stats():
                def f_stats(e):
                    e.bn_stats(stt[:, 0:6], z[:, 0:512])
                    return e.bn_stats(stt[:, 6:12], z[:, 512:1024])
                P.op("vector", f_stats, reads=[rz], writes=[sn + "a"])
                P.op("vector", lambda e: e.bn_aggr(mv, stt), reads=[sn + "a"], writes=[sn + "b"])
                P.op("scalar", lambda e: e.activation(out=rs, in_=mv[:, 1:2], func=AF.Sqrt, bias=epsc[:, 0:1], scale=1.0),
                     reads=[sn + "b", "epsc"], writes=[sn + "c0"])

            def st_norm():
                P.op("vector", lambda e: e.reciprocal(out=rs, in_=rs), reads=[sn + "c0"], writes=[sn + "c"])
                P.op("vector", lambda e: e.tensor_scalar(out=nm, in0=mv[:, 0:1], scalar1=-1.0, scalar2=rs,
                                                         op0=ALU.mult, op1=ALU.mult),
                     reads=[sn + "b", sn + "c"], writes=[sn + "d"])
                P.op("scalar", lambda e: e.activation(out=hb, in_=z, func=AF.Identity, bias=nm, scale=rs),
                     reads=[rz, sn + "c", sn + "d"], writes=[rhb])
                P.op("scalar", lambda e: e.activation(out=R[:, tt, :], in_=z, func=AF.Identity, bias=nm, scale=rs),
                     reads=[rz, sn + "c", sn + "d"], writes=["R:%d" % tt])

            def st_tr():
                def f_tr(e):
                    ins = None
                    for c in range(8):
                        ins = e.transpose(out=pt[:, c * 128:(c + 1) * 128], in_=hb[:, c * 128:(c + 1) * 128],
                                          identity=ident[:])
                    return ins
                P.op("tensor", f_tr, reads=[rhb, "ident"], writes=[rp])
                P.op("gpsimd", lambda e: e.tensor_tensor(out=R[:, tt, :], in0=R[:, tt, :], in1=prm1[:, 0, :], op=ALU.mult),
                     reads=["lnparams"], writes=["R:%d" % tt])
                P.op("gpsimd", lambda e: e.tensor_tensor(out=R[:, tt, :], in0=R[:, tt, :], in1=prm1[:, 1, :], op=ALU.add),
                     reads=["lnparams"], writes=["R:%d" % tt])

            def st_evac():
                def f_ev(e):
                    ins = None
                    for c in range(4):
                        ins = e.tensor_scalar(out=hT[:, c, tok0:tok0 + 128], in0=ptv[:, c, :],
                                              scalar1=bcol[:, BC_G1 + c:BC_G1 + c + 1],
                                              scalar2=bcol[:, BC_B1 + c:BC_B1 + c + 1], op0=ALU.mult, op1=ALU.add)
                    return ins
                P.op("vector", f_ev, reads=["bcol"], writes=[rp, "hT:%d" % (tok0 // 128)])

                def f_ev2(e):
                    ins = None
                    for c in range(4, 8):
                        ins = e.activation(out=hT[:, c, tok0:tok0 + 128], in_=ptv[:, c, :], func=AF.Identity,
                                           bias=bcol[:, BC_B1 + c:BC_B1 + c + 1], scale=bcol[:, BC_G1 + c:BC_G1 + c + 1])
                    return ins
                P.op("scalar", f_ev2, reads=["bcol"], writes=[rp, "hT:%d" % (tok0 // 128)])
            return [st_stats, st_norm, st_tr, st_evac]

        bE = Bump("D")
        w1 = [None, None]
        w2 = [None, None]
        w1[0] = bE.at(32768).n("w1_0").bf([8, 1024])
        P.op("gpsimd", dma(w1[0], w_ff1_v[:, :, 0:1024]), writes=["w1_0"], dma_sem="w1_0")
        pipeC = Pipe()
        c2_mm(0)
        c2_add(0)
        for tt in range(NTO):
            if tt + 1 < NTO:
                c2_mm(tt + 1)
            pipeC.push(ln1_job(tt))
            pipeC.tick()
            if tt + 1 < NTO:
                c2_add(tt + 1)
        pipeC.drain()

        rt = [bE.n("rt%d" % i).f32([512]) for i in range(2)]
        prm2 = bE.n("lnparams").f32([3, 1024])
        w2[0] = bE.at(0).n("w2_0").bf([8, 1024])
        P.op("gpsimd", dma(w2[0], w_ff2_v[:, 0:8, :]), writes=["w2_0"], dma_sem="w2_0")
        ub = [bE.n(*["u%d_%d" % (i, f) for f in range(8)]).bf([8, 512]) for i in range(2)]
        w1[1] = bE.at(16384).n("w1_1").bf([8, 1024])
        w2[1] = bE.n("w2_1").bf([8, 1024])
        P.op("sync", dma(prm2[:, 0, :], brow_d[:, BR_LN2G:BR_LN2G + 1024]), writes=["lnparams"], dma_sem="prm")
        P.op("sync", dma(prm2[:, 1, :], brow_d[:, BR_LN2B:BR_LN2B + 1024]), writes=["lnparams"], dma_sem="prm")
        P.op("sync", dma(prm2[:, 2, :], brow_d[:, BR_BFF2:BR_BFF2 + 1024]), writes=["lnparams"], dma_sem="prm")

        if stop_after == "C":
            P.barrier()
            toks = do_dumps([("hT", hT[:]), ("R", R[:])])
            return finish_now(toks)

        xbanks = [psA[0][:], psA[1][:], psO[0], psO[1]]
        ybanks = [psS[:, 0:512], psS[:, 512:1024], psS[:, 1024:1536], psS[:, 1536:2048]]
        out_toks = []
        deferred = []
        cnt = {"x": 0, "y": 0}

        def load_q(qf):
            wbuf = qf % 2
            if qf > 0:
                P.op("gpsimd", dma(w1[wbuf], w_ff1_v[:, :, 1024 * qf:1024 * qf + 1024]), writes=["w1_%d" % wbuf], dma_sem="w1_%d" % wbuf)
                P.op("gpsimd", dma(w2[wbuf], w_ff2_v[:, 8 * qf:8 * qf + 8, :]), writes=["w2_%d" % wbuf], dma_sem="w2_%d" % wbuf)

        def ffn1(bi, qf, nb):
            wbuf = qf % 2
            u = ub[bi % 2]
            ru = "u%d" % (bi % 2)
            tok0 = HALO + 512 * nb
            for fp in range(4):
                xa = cnt["x"] % 4
                cnt["x"] += 2
                pxs = [xbanks[xa], xbanks[xa + 1]]
                rpxs = ["xb%d" % xa, "xb%d" % (xa + 1)]

                def fn(e, fp=fp, pxs=pxs, wbuf=wbuf, tok0=tok0):
                    ins = None
                    for i in range(2):
                        ffc = 2 * fp + i
                        for kc in range(8):
                            ins = e.matmul(pxs[i], lhsT=w1[wbuf][:, kc, 128 * ffc:128 * ffc + 128],
                                           rhs=hT[:, kc, tok0:tok0 + 512], start=(kc == 0), stop=(kc == 7))
                    return ins
                P.op("tensor", fn, reads=["w1_%d" % wbuf] + hT_res(tok0, 512), writes=rpxs)
                for i in range(2):
                    ffc = 2 * fp + i
                    r_ = rt[i]
                    rr = "rt%d" % i
                    bcidx = BC_FF1 + 8 * qf + ffc
                    P.op("scalar", lambda e, px=pxs[i], r_=r_, bcidx=bcidx: e.activation(
                        out=r_, in_=px, func=AF.Relu, bias=bcol[:, bcidx:bcidx + 1], scale=1.0),
                        reads=["bcol"], writes=[rpxs[i], rr])
                    P.op("gpsimd", lambda e, r_=r_, u=u, ffc=ffc: e.tensor_tensor(out=u[:, ffc, :], in0=r_, in1=r_, op=ALU.mult),
                         reads=[rr], writes=[ru + "_%d" % ffc])

        def ffn2(bi, qf, nb):
            wbuf = qf % 2
            u = ub[bi % 2]
            ru = "u%d" % (bi % 2)
            for ti in range(4):
                tt = 4 * nb + ti
                ya = cnt["y"] % 4
                cnt["y"] += 2
                pys = [ybanks[ya], ybanks[ya + 1]]
                rpys = ["yb%d" % ya, "yb%d" % (ya + 1)]

                def fn(e, ti=ti, pys=pys, u=u, wbuf=wbuf):
                    ins = None
                    for half in range(2):
                        for ffc in range(8):
                            ins = e.matmul(pys[half], lhsT=u[:, ffc, ti * 128:(ti + 1) * 128],
                                           rhs=w2[wbuf][:, ffc, 512 * half:512 * half + 512],
                                           start=(ffc == 0), stop=(ffc == 7))
                    return ins
                P.op("tensor", fn, reads=["w2_%d" % wbuf] + [ru + "_%d" % ffc for ffc in range(8)], writes=rpys)
                for half in range(2):
                    py = pys[half]
                    if qf == 0:
                        P.op("vector", lambda e, py=py, tt=tt, half=half: e.scalar_tensor_tensor(
                            out=R[:, tt, 512 * half:512 * half + 512], in0=R[:, tt, 512 * half:512 * half + 512],
                            scalar=ALPHA, in1=py, op0=ALU.mult, op1=ALU.add),
                            writes=[rpys[half], "R:%d" % tt])
                    else:
                        P.op("vector", lambda e, py=py, tt=tt, half=half: e.tensor_tensor(
                            out=R[:, tt, 512 * half:512 * half + 512], in0=R[:, tt, 512 * half:512 * half + 512],
                            in1=py, op=ALU.add),
                            writes=[rpys[half], "R:%d" % tt])
                if qf == 0:
                    P.op("gpsimd", lambda e, tt=tt: e.tensor_tensor(out=R[:, tt, :], in0=R[:, tt, :], in1=prm2[:, 2, :],
                                                                   op=ALU.add),
                         reads=["lnparams"], writes=["R:%d" % tt])
                if qf == 3:
                    prev = list(deferred)
                    del deferred[:]
                    layer_norm(R[:, tt, :], R[:, tt, :], prm2[:, 0, :], prm2[:, 1, :], tt, "R:%d" % tt, "R:%d" % tt,
                               defer=deferred)

                    def st_out(tt=tt):
                        out_toks.append(P.op("sync", dma(out_d[tt * 128:(tt + 1) * 128, :], R[:, tt, :]),
                                             reads=["R:%d" % tt], dma_sem="out"))
                    deferred.append(st_out)
                    for f_ in prev:
                        f_()

        steps = [(qf, nb) for qf in range(4) for nb in range(4)]
        load_q(0)
        load_q(1)
        ffn1(0, *steps[0])
        for bi, (qf, nb) in enumerate(steps):
            if bi + 1 < len(steps):
                ffn1(bi + 1, *steps[bi + 1])
            ffn2(bi, qf, nb)
            if nb == 3 and qf + 2 < 4:
                load_q(qf + 2)
        for f_ in deferred:
            f_()
        P.finish("sync", out_toks)
        P.emit()
    return nc


def _na_tables(rpb, ch):
    g0 = 16 * ch
    rep_local = [2, 3, 4, 16, 17]
    kk = np.arange(128)
    rk, ck = kk // 64, kk % 64
    out = np.full((4, 128, 5, 5, 2, 128), NEG, np.float32)
    for cl, jq in enumerate(rep_local):
        gq = g0 + jq - 2
        Rq = 2 * gq + rk
        cq = ck
        rs = np.clip(Rq - 4, 0, 120)
        cs = np.clip(cq - 8, 0, 48)
        for si in range(5):
            lk = jq + si - 2
            gk = g0 + lk - 2
            if ch == 0 and lk == 0:
                gk = 3
            if ch == 3 and lk == 19:
                gk = 60
            if gk < 0 or gk > 63:
                continue
            Rk = 2 * gk + rk
            valid = ((Rk[:, None] >= rs[None, :]) & (Rk[:, None] <= rs[None, :] + 7)
                     & (ck[:, None] >= cs[None, :]) & (ck[:, None] < cs[None, :] + 16))
            dr = np.clip(Rk[:, None] - Rq[None, :] + 7, 0, 14)
            dc = np.clip(ck[:, None] - cq[None, :] + 15, 0, 30)
            for h in range(8):
                tab = np.where(valid, rpb[h][dr, dc], np.float32(NEG)).astype(np.float32)
                out[h // 2, :, cl, si, h % 2, :] = tab
    return np.ascontiguousarray(out.reshape(4, 128, 6400))


def _swa_masks(ch):
    g0 = 16 * ch
    kk = np.arange(128)
    out = np.zeros((128, 3, 2, 2, 128), np.float32)
    for cl, jq in enumerate([2, 4, 17]):
        gq = g0 + jq - 2
        qpos = gq * 128 + kk
        for si, s in enumerate([-1, 1]):
            kpos = (gq + s) * 128 + kk
            valid = (np.abs(kpos[:, None] - qpos[None, :]) <= 128) & (kpos[:, None] >= 0) & (kpos[:, None] < 8192)
            m = np.where(valid, np.float32(0.0), np.float32(NEG)).astype(np.float32)
            out[:, cl, si, 0, :] = m
            out[:, cl, si, 1, :] = m
    return np.ascontiguousarray(out.reshape(128, 1536))


def _rot_tables(pos, scale):
    inv_freq = np.power(np.float32(500000.0), -np.arange(0, 16, 2, dtype=np.float32) / np.float32(16)).astype(np.float32)
    ang = pos.astype(np.float32)[:, None] * inv_freq[None, :]
    cos = np.cos(ang).astype(np.float32).T
    sin = np.sin(ang).astype(np.float32).T
    C = np.ones((128, pos.shape[0]), np.float32)
    S = np.zeros((128, pos.shape[0]), np.float32)
    for base in (0, 64):
        C[base:base + 8] = cos
        C[base + 8:base + 16] = cos
        S[base:base + 8] = -sin
        S[base + 8:base + 16] = sin
    return np.ascontiguousarray(C * np.float32(scale)), np.ascontiguousarray(S * np.float32(scale))


def _swap_cols(w, nheads):
    out = np.zeros_like(w)
    for h in range(nheads):
        b = 64 * h
        out[..., b:b + 8] = w[..., b + 8:b + 16]
        out[..., b + 8:b + 16] = w[..., b:b + 8]
    return out


def make_in_maps(inputs):
    f = lambda a: np.ascontiguousarray(np.asarray(a, dtype=np.float32))
    x = f(inputs["x"])
    w_in = f(inputs["w_in"])[0]
    b_in = f(inputs["b_in"])[0]
    rpb = f(inputs["na_rpb"])[0]
    sink = f(inputs["swa_sink"])[0]
    wq_b = w_in[:, 1536:2048]
    wk_b = w_in[:, 2048:2176]
    perm = np.zeros((128, 128), np.float32)
    for m in range(128):
        b0, d = (m // 64) * 64, m % 64
        k = b0 + d + 8 if d < 8 else (b0 + d - 8 if d < 16 else m)
        perm[k, m] = 1.0
    bq_s = _swap_cols(b_in[1536:2048], 8)
    bk_s = _swap_cols(b_in[2048:2176], 2)
    bcol = np.zeros((128, NBCOL), np.float32)
    bcol[:, BC_QA:BC_QA + 4] = b_in[0:512].reshape(4, 128).T
    bcol[:, BC_KA:BC_KA + 4] = b_in[512:1024].reshape(4, 128).T
    bcol[:, BC_QB:BC_QB + 4] = b_in[1536:2048].reshape(4, 128).T
    bcol[:, BC_QBS:BC_QBS + 4] = bq_s.reshape(4, 128).T
    for g in range(2):
        bcol[:, BC_KB + g] = np.tile(b_in[2048 + 64 * g:2048 + 64 * g + 64], 2)
        bcol[:, BC_KBS + g] = np.tile(bk_s[64 * g:64 * g + 64], 2)
    bcol[:, BC_GA:BC_GA + 8] = b_in[2304:3328].reshape(8, 128).T
    bcol[:, BC_GB:BC_GB + 8] = b_in[3328:4352].reshape(8, 128).T
    bcol[:, BC_FF1:BC_FF1 + 32] = f(inputs["b_ff1"])[0].reshape(32, 128).T
    bcol[:, BC_G0:BC_G0 + 8] = f(inputs["ln0_g"]).reshape(8, 128).T
    bcol[:, BC_B0:BC_B0 + 8] = f(inputs["ln0_b"]).reshape(8, 128).T
    bcol[:, BC_G1:BC_G1 + 8] = f(inputs["ln1_g"])[0].reshape(8, 128).T
    bcol[:, BC_B1:BC_B1 + 8] = f(inputs["ln1_b"])[0].reshape(8, 128).T
    brow = np.zeros((NBROW,), np.float32)
    brow[BR_LN0G:BR_LN0G + 1024] = f(inputs["ln0_g"])
    brow[BR_LN0B:BR_LN0B + 1024] = f(inputs["ln0_b"])
    brow[BR_BV:BR_BV + 512] = b_in[1024:1536]
    brow[BR_BV + 512:BR_BV + 640] = b_in[2176:2304]
    brow[BR_BOUT:BR_BOUT + 1024] = f(inputs["b_out"])[0]
    brow[BR_LN1G:BR_LN1G + 1024] = f(inputs["ln1_g"])[0]
    brow[BR_LN1B:BR_LN1B + 1024] = f(inputs["ln1_b"])[0]
    brow[BR_BFF2:BR_BFF2 + 1024] = f(inputs["b_ff2"])[0]
    brow[BR_LN2G:BR_LN2G + 1024] = f(inputs["ln2_g"])[0]
    brow[BR_LN2B:BR_LN2B + 1024] = f(inputs["ln2_b"])[0]
    brow_bc = np.ascontiguousarray(np.broadcast_to(brow[None, :], (128, NBROW)))
    sink_bc = np.ascontiguousarray(np.broadcast_to(sink[None, :], (128, 8)))
    shared = {
        "w_in": w_in, "perm": perm, "w_na": f(inputs["w_branch_na"])[0], "w_swa": f(inputs["w_branch_swa"])[0],
        "w_out": f(inputs["w_out"])[0], "w_ff1": f(inputs["w_ff1"])[0], "w_ff2": f(inputs["w_ff2"])[0],
        "bcol": bcol, "brow": brow_bc, "sink": sink_bc, "ident": np.eye(128, dtype=np.float32),
    }
    tabs = {ch: (_na_tables(rpb, ch), _swa_masks(ch)) for ch in range(4)}
    in_maps = []
    for core in range(NCORES):
        b, ch = core // 4, core % 4
        g0 = 16 * ch
        xl = np.zeros((TOK, D), np.float32)
        pos = np.zeros((TOK,), np.int64)
        for t in range(NT):
            gt = g0 + t - 2
            if ch == 0 and t == 0:
                gt = 3
            if ch == 3 and t == 19:
                gt = 60
            if 0 <= gt < 64:
                xl[t * 128:(t + 1) * 128] = x[b, gt * 128:(gt + 1) * 128]
                pos[t * 128:(t + 1) * 128] = gt * 128 + np.arange(128)
        cq, sq = _rot_tables(pos[HALO:HALO + 2048], 0.125)
        ck, sk = _rot_tables(pos, 1.0)
        m = dict(shared)
        m.update({"x": xl, "nabias": tabs[ch][0], "swamask": tabs[ch][1],
                  "rot_cq": cq, "rot_sq": sq, "rot_ck": ck, "rot_sk": sk})
        in_maps.append(m)
    return in_maps


_NC_CACHE = {}


def kernel(**inputs):
    in_maps = make_in_maps(inputs)
    if "nc" not in _NC_CACHE:
        _NC_CACHE["nc"] = build_program()
    nc = _NC_CACHE["nc"]
    res = run_bass_kernel_spmd(nc, in_maps, core_ids=list(range(NCORES)))
    out = np.zeros((2, 8192, D), np.float32)
    for core in range(NCORES):
        b, ch = core // 4, core % 4
        out[b, ch * 2048:(ch + 1) * 2048] = res.results[core]["out"]
    return out
```

```python
import numpy as np
from contextlib import ExitStack
import concourse.bass as bass
import concourse.mybir as mybir
from concourse.bass_utils import run_bass_kernel_spmd

F32 = mybir.dt.float32
BF16 = mybir.dt.bfloat16
AF = mybir.ActivationFunctionType
ALU = mybir.AluOpType

ENGS = ("tensor", "vector", "scalar", "gpsimd", "sync")

NCORES = 8
D = 1024
NT = 20
NTO = 16
TOK = NT * 128
HALO = 256
IN_W = 4352
NEG = -30000.0
ALPHA = 2.0 ** 0.25
EPS = 1e-5


class Prog:
    def __init__(self, nc):
        self.nc = nc
        self.ops = {e: [] for e in ENGS}
        self.cnt = {e: 0 for e in ENGS}
        self.last_w = {}
        self.readers = {}
        self.waited = {e: {} for e in ENGS}
        self.dma_sems = {}
        self.final = {}
        self.alias = {}
        self.rename = {}
        self.expand = {}

    def register(self, phase, names, lo, hi):
        for n in names:
            q = phase + "." + n
            self.rename[n] = q
            self.alias.setdefault(q, []).append((lo, hi))

    def _map(self, names):
        out = []
        for n in names:
            for m in self.expand.get(n, [n]):
                out.append(self.rename.get(m, m))
        return out

    def _overlaps(self, r):
        res = []
        for (lo, hi) in self.alias.get(r, ()):
            for q, ivs in self.alias.items():
                if q == r:
                    continue
                for (a, b) in ivs:
                    if a < hi and lo < b:
                        res.append(q)
                        break
        return res

    def _need(self, eng, waits, tok):
        if tok is None:
            return
        key, val = tok
        if key == eng and eng == "tensor":
            return
        if val > self.waited[eng].get(key, 0):
            self.waited[eng][key] = val
            waits[key] = max(waits.get(key, 0), val)

    def op(self, eng, fn, reads=(), writes=(), dma_sem=None):
        waits = {}
        reads = self._map(reads)
        writes = self._map(writes)
        for r in reads:
            self._need(eng, waits, self.last_w.get(r))
        for r in writes:
            self._need(eng, waits, self.last_w.get(r))
            for k, v in self.readers.get(r, {}).items():
                self._need(eng, waits, (k, v))
            for q in self._overlaps(r):
                self._need(eng, waits, self.last_w.get(q))
                for k, v in self.readers.get(q, {}).items():
                    self._need(eng, waits, (k, v))
        if dma_sem is not None:
            ent = self.dma_sems.setdefault(dma_sem, [0])
            ent[0] += 16
            tok = ("dma:" + dma_sem, ent[0])
            inc = ("dma:" + dma_sem, 16)
        else:
            self.cnt[eng] += 1
            tok = (eng, self.cnt[eng])
            inc = (eng, 1)
        for r in reads:
            d = self.readers.setdefault(r, {})
            d[tok[0]] = max(d.get(tok[0], 0), tok[1])
        for r in writes:
            self.last_w[r] = tok
            self.readers[r] = {}
        self.ops[eng].append((waits, fn, inc))
        return tok

    def barrier(self):
        snap = {e: self.cnt[e] for e in ENGS if self.cnt[e] > 0}
        for k, v in self.dma_sems.items():
            snap["dma:" + k] = v[0]
        for e in ENGS:
            waits = {}
            for k, v in snap.items():
                if k == e:
                    continue
                if v > self.waited[e].get(k, 0):
                    self.waited[e][k] = v
                    waits[k] = v
            if waits:
                self.ops[e].append((waits, None, None))

    def finish(self, eng, toks):
        d = self.final.setdefault(eng, {})
        for k, v in toks:
            d[k] = max(d.get(k, 0), v)

    def emit(self):
        nc = self.nc
        with ExitStack() as st:
            sems = {}
            for e in ENGS:
                sems[e] = st.enter_context(nc.semaphore("s_" + e))
            for k in self.dma_sems:
                sems["dma:" + k] = st.enter_context(nc.semaphore("d_" + k))
            block = st.enter_context(nc.Block())

            def make(eng):
                def body(e):
                    for waits, fn, inc in self.ops[eng]:
                        for k, v in waits.items():
                            e.wait_ge(sems[k], v)
                        if fn is not None:
                            ins = fn(e)
                            ins.then_inc(sems[inc[0]], inc[1])
                    for k, v in self.final.get(eng, {}).items():
                        e.wait_ge(sems[k], v)
                return body

            for eng in ENGS:
                if self.ops[eng] or eng in self.final:
                    getattr(block, eng)(make(eng))


BC_QA, BC_KA, BC_QB, BC_QBS, BC_KB, BC_KBS, BC_GA, BC_GB, BC_FF1 = 0, 4, 8, 12, 16, 18, 20, 28, 36
BC_G0, BC_B0, BC_G1, BC_B1 = 68, 76, 84, 92
NBCOL = 100
BR_LN0G, BR_LN0B, BR_BV, BR_BOUT, BR_LN1G, BR_LN1B, BR_BFF2, BR_LN2G, BR_LN2B = (
    0, 1024, 2048, 2688, 3712, 4736, 5760, 6784, 7808)
NBROW = 8832


def build_program(stop_after=None, dumps=()):
    nc = bass.Bass("TRN2", target_bir_lowering=False)

    def din(name, shape, dt=F32):
        return nc.dram_tensor(name, list(shape), dt, kind="ExternalInput").ap()

    x_d = din("x", [TOK, D])
    w_in_d = din("w_in", [D, IN_W])
    perm_d = din("perm", [128, 128])
    w_na_d = din("w_na", [512, D])
    w_swa_d = din("w_swa", [512, D])
    w_out_d = din("w_out", [D, D])
    w_ff1_d = din("w_ff1", [D, 4096])
    w_ff2_d = din("w_ff2", [4096, D])
    bcol_d = din("bcol", [128, NBCOL])
    brow_d = din("brow", [128, NBROW])
    sink_d = din("sink", [128, 8])
    nab_d = din("nabias", [4, 128, 6400])
    smask_d = din("swamask", [128, 1536])
    cq_d = din("rot_cq", [128, 2048])
    sq_d = din("rot_sq", [128, 2048])
    ck_d = din("rot_ck", [128, TOK])
    sk_d = din("rot_sk", [128, TOK])
    ident_d = din("ident", [128, 128])
    out_d = nc.dram_tensor("out", [NTO * 128, D], F32, kind="ExternalOutput").ap()
    dump_d = {}
    for name, shape, dt in dumps:
        dump_d[name] = nc.dram_tensor("dbg_" + name, list(shape), dt, kind="ExternalOutput").ap()

    w_in_v = w_in_d.rearrange("(kc p) c -> p kc c", p=128)
    w_na_v = w_na_d.rearrange("(kc p) c -> p kc c", p=128)
    w_swa_v = w_swa_d.rearrange("(kc p) c -> p kc c", p=128)
    w_out_v = w_out_d.rearrange("(kc p) c -> p kc c", p=128)
    w_ff1_v = w_ff1_d.rearrange("(kc p) c -> p kc c", p=128)
    w_ff2_v = w_ff2_d.rearrange("(kc p) c -> p kc c", p=128)

    with ExitStack() as st:
        def sb(name, shape, dt):
            return st.enter_context(nc.sbuf_tensor("sb_" + name, list(shape), dt))

        def ps(name, shape, dt=F32):
            return st.enter_context(nc.psum_tensor("ps_" + name, list(shape), dt))

        R = sb("R", [128, NTO, D], F32)
        hT = sb("hT", [128, 8, TOK], BF16)
        ident = sb("ident", [128, 128], BF16)
        permb = sb("permb", [128, 128], BF16)
        bcol = sb("bcol", [128, NBCOL], F32)
        esink = sb("esink", [128, 8], F32)
        small = sb("small", [128, 64], F32)
        epsc = sb("epsc", [128, 1], F32)
        sinkrow = sb("sinkrow", [1, 4 * 256], BF16)
        indrow = sb("indrow", [1, 2 * 128], BF16)
        ARENA = 50960
        arena = sb("arena", [128, ARENA], BF16)

        psS = ps("psS", [128, 2048])
        psA = [ps("psA0", [128, 512]), ps("psA1", [128, 512])]
        psT = [ps("psT0", [128, 1024], BF16), ps("psT1", [128, 1024], BF16)]
        psO = [psT[0][:, :].bitcast(F32), psT[1][:, :].bitcast(F32)]

        class Bump:
            def __init__(self, phase="X"):
                self.off = 0
                self.phase = phase
                self.names = None

            def at(self, off):
                self.off = off
                return self

            def n(self, *names):
                self.names = list(names)
                return self

            def take(self, nelem_bf16):
                a = self.off
                self.off += (nelem_bf16 + 15) // 16 * 16
                assert self.off <= ARENA, self.off
                assert self.names, "arena buffer needs resource names"
                P.register(self.phase, self.names, a, self.off)
                self.names = None
                return a

            def bf(self, shape):
                n = int(np.prod(shape))
                a = self.take(n)
                v = arena[:, a:a + n]
                if len(shape) == 2:
                    return v.rearrange("p (a b) -> p a b", b=shape[1])
                if len(shape) == 3:
                    return v.rearrange("p (a b c) -> p a b c", b=shape[1], c=shape[2])
                return v

            def f32(self, shape):
                n = int(np.prod(shape))
                a = self.take(2 * n)
                v = arena[:, a:a + 2 * n].bitcast(F32)
                if len(shape) == 2:
                    return v.rearrange("p (a b) -> p a b", b=shape[1])
                return v

        P = Prog(nc)
        P.expand = {"psS0": ["pb0", "pb1"], "psS1": ["pb2", "pb3"], "psA0": ["pb4"], "psA1": ["pb5"],
                    "psT0": ["pb6"], "psT1": ["pb7"],
                    "bank0": ["pb4"], "bank1": ["pb5"], "bank2": ["pb0"], "bank3": ["pb1"], "bank4": ["pb2"],
                    "bank5": ["pb3"], "bank6": ["pb6"], "bank7": ["pb7"],
                    "xb0": ["pb4"], "xb1": ["pb5"], "xb2": ["pb6"], "xb3": ["pb7"],
                    "yb0": ["pb0"], "yb1": ["pb1"], "yb2": ["pb2"], "yb3": ["pb3"]}

        def dma(out, in_):
            return lambda e: e.dma_start(out=out, in_=in_)

        def mm_group(out_ps, pairs):
            def fn(e):
                n = len(pairs)
                ins = None
                for i, (l, r) in enumerate(pairs):
                    ins = e.matmul(out_ps, lhsT=l, rhs=r, start=(i == 0), stop=(i == n - 1))
                return ins
            return fn

        P.op("gpsimd", dma(ident[:], ident_d), writes=["ident"], dma_sem="c0")
        P.op("gpsimd", dma(permb[:], perm_d), writes=["permb"], dma_sem="c5")
        P.op("scalar", dma(bcol[:], bcol_d), writes=["bcol"], dma_sem="c1")
        P.op("scalar", dma(esink[:], sink_d), writes=["esink"], dma_sem="c2")
        P.op("vector", lambda e: e.memset(epsc[:], EPS), writes=["epsc"])
        P.op("scalar", lambda e: e.activation(out=esink[:], in_=esink[:], func=AF.Exp),
             reads=["esink"], writes=["esink"])

        def f_ind(e):
            e.memset(indrow[0:1, 0:64], 0.0)
            e.memset(indrow[0:1, 64:128], 1.0)
            e.memset(indrow[0:1, 128:192], 1.0)
            return e.memset(indrow[0:1, 192:256], 0.0)
        P.op("vector", f_ind, writes=["indrow"])
        P.op("vector", lambda e: e.memset(sinkrow[:], 1.0), writes=["sinkrow"])

        def f_srow(e):
            ins = None
            for g_ in range(2):
                for e_ in range(2):
                    for c_ in range(2):
                        h_ = 4 * g_ + 2 * c_ + e_
                        o_ = (g_ * 2 + e_) * 256 + c_ * 128
                        ins = e.tensor_scalar(out=sinkrow[0:1, o_:o_ + 128], in0=sinkrow[0:1, o_:o_ + 128],
                                              scalar1=esink[0:1, h_:h_ + 1], scalar2=None, op0=ALU.mult)
            return ins
        P.op("vector", f_srow, reads=["esink"], writes=["sinkrow"])

        def layer_norm(src, dst, g_bc, b_bc, k, res_src, res_dst, defer=None):
            o = (k % 4) * 16
            stt = small[:, o:o + 12]
            mv = small[:, o + 12:o + 14]
            rs = small[:, o + 14:o + 15]
            nm = small[:, o + 15:o + 16]
            sn = "lnst%d" % (k % 4)

            def f_stats(e):
                e.bn_stats(stt[:, 0:6], src[:, 0:512])
                return e.bn_stats(stt[:, 6:12], src[:, 512:1024])
            P.op("vector", f_stats, reads=[res_src], writes=[sn + "a"])
            P.op("vector", lambda e: e.bn_aggr(mv, stt), reads=[sn + "a"], writes=[sn + "b"])
            P.op("scalar", lambda e: e.activation(out=rs, in_=mv[:, 1:2], func=AF.Sqrt, bias=epsc[:, 0:1], scale=1.0),
                 reads=[sn + "b", "epsc"], writes=[sn + "c0"])
            P.op("vector", lambda e: e.tensor_scalar(out=nm, in0=mv[:, 0:1], scalar1=-1.0, scalar2=None, op0=ALU.mult),
                 reads=[sn + "b"], writes=[sn + "d"])
            P.op("vector", lambda e: e.reciprocal(out=rs, in_=rs), reads=[sn + "c0"], writes=[sn + "c"])
            P.op("vector", lambda e: e.tensor_scalar(out=src, in0=src, scalar1=nm, scalar2=rs, op0=ALU.add, op1=ALU.mult),
                 reads=[sn + "d", sn + "c"], writes=[res_src])
            P.op("gpsimd", lambda e: e.tensor_tensor(out=src, in0=src, in1=g_bc, op=ALU.mult),
                 reads=["lnparams"], writes=[res_src])
            def fin():
                P.op("vector", lambda e: e.tensor_tensor(out=dst, in0=src, in1=b_bc, op=ALU.add),
                     reads=[res_src, "lnparams"], writes=[res_dst])
            if defer is None:
                fin()
            else:
                defer.append(fin)

        def to_feature_major(hf, res_hf, hb, res_hb, k, tok0):
            P.op("scalar", lambda e: e.activation(out=hb, in_=hf, func=AF.Identity), reads=[res_hf], writes=[res_hb])
            pt = psT[k % 2]
            rp = "psT%d" % (k % 2)

            def f_tr(e):
                ins = None
                for c in range(8):
                    ins = e.transpose(out=pt[:, c * 128:(c + 1) * 128], in_=hb[:, c * 128:(c + 1) * 128],
                                      identity=ident[:])
                return ins
            P.op("tensor", f_tr, reads=[res_hb, "ident"], writes=[rp])
            ptv = pt[:, :].rearrange("p (c t) -> p c t", t=128)
            eng = "scalar"
            if eng == "vector":
                P.op("vector", lambda e: e.tensor_copy(out=hT[:, :, tok0:tok0 + 128], in_=ptv),
                     writes=[rp, "hT:%d" % (tok0 // 128)])
            else:
                P.op("scalar", lambda e: e.activation(out=hT[:, :, tok0:tok0 + 128], in_=ptv, func=AF.Identity),
                     writes=[rp, "hT:%d" % (tok0 // 128)])

        def hT_res(tok0, n):
            return ["hT:%d" % t for t in range(tok0 // 128, (tok0 + n) // 128)]

        class Pipe:
            def __init__(self):
                self.jobs = []

            def push(self, stages):
                self.jobs.append(list(stages))

            def tick(self):
                for job in list(self.jobs):
                    job.pop(0)()
                    if not job:
                        self.jobs.remove(job)

            def drain(self):
                while self.jobs:
                    self.tick()

        bA = Bump("A")
        xs = [bA.n("xs%d" % i).f32([1024]) for i in range(4)]
        hbA = [bA.n("hb%d" % i).bf([1024]) for i in range(3)]
        prm = bA.n("lnparams").f32([2, 1024])
        P.op("scalar", dma(prm[:, 0, :], brow_d[:, BR_LN0G:BR_LN0G + 1024]), writes=["lnparams"], dma_sem="prm")
        P.op("scalar", dma(prm[:, 1, :], brow_d[:, BR_LN0B:BR_LN0B + 1024]), writes=["lnparams"], dma_sem="prm")
        def ln0_job(t):
            xb = xs[t % 4]
            rx = "xs%d" % (t % 4)
            own = 2 <= t < 2 + NTO
            hb = hbA[t % 3]
            rhb = "hb%d" % (t % 3)
            o = (t % 4) * 16
            stt = small[:, o:o + 12]
            mv = small[:, o + 12:o + 14]
            rs = small[:, o + 14:o + 15]
            nm = small[:, o + 15:o + 16]
            sn = "lnst%d" % (t % 4)
            pt = psT[t % 2]
            rp = "psT%d" % (t % 2)
            ptv = pt[:, :].rearrange("p (c q) -> p c q", q=128)

            def st_ld():
                P.op("sync", dma(xb, x_d[t * 128:(t + 1) * 128, :]), writes=[rx], dma_sem=rx)

            def st_stats():
                def f_stats(e):
                    e.bn_stats(stt[:, 0:6], xb[:, 0:512])
                    return e.bn_stats(stt[:, 6:12], xb[:, 512:1024])
                P.op("vector", f_stats, reads=[rx], writes=[sn + "a"])
                P.op("vector", lambda e: e.bn_aggr(mv, stt), reads=[sn + "a"], writes=[sn + "b"])
                P.op("scalar", lambda e: e.activation(out=rs, in_=mv[:, 1:2], func=AF.Sqrt, bias=epsc[:, 0:1], scale=1.0),
                     reads=[sn + "b", "epsc"], writes=[sn + "c0"])

            def st_norm():
                P.op("vector", lambda e: e.reciprocal(out=rs, in_=rs), reads=[sn + "c0"], writes=[sn + "c"])
                P.op("vector", lambda e: e.tensor_scalar(out=nm, in0=mv[:, 0:1], scalar1=-1.0, scalar2=rs,
                                                         op0=ALU.mult, op1=ALU.mult),
                     reads=[sn + "b", sn + "c"], writes=[sn + "d"])
                P.op("scalar", lambda e: e.activation(out=hb, in_=xb, func=AF.Identity, bias=nm, scale=rs),
                     reads=[rx, sn + "c", sn + "d"], writes=[rhb])
                if own:
                    P.op("scalar", lambda e: e.activation(out=R[:, t - 2, :], in_=xb, func=AF.Identity, bias=nm, scale=rs),
                         reads=[rx, sn + "c", sn + "d"], writes=["R:%d" % (t - 2)])

            def st_tr():
                def f_tr(e):
                    ins = None
                    for c in range(8):
                        ins = e.transpose(out=pt[:, c * 128:(c + 1) * 128], in_=hb[:, c * 128:(c + 1) * 128],
                                          identity=ident[:])
                    return ins
                P.op("tensor", f_tr, reads=[rhb, "ident"], writes=[rp])

            def st_evac():
                def f_ev(e):
                    ins = None
                    for c in range(5):
                        ins = e.tensor_scalar(out=hT[:, c, t * 128:(t + 1) * 128], in0=ptv[:, c, :],
                                              scalar1=bcol[:, BC_G0 + c:BC_G0 + c + 1],
                                              scalar2=bcol[:, BC_B0 + c:BC_B0 + c + 1], op0=ALU.mult, op1=ALU.add)
                    return ins
                P.op("vector", f_ev, reads=["bcol"], writes=[rp, "hT:%d" % t])

                def f_ev2(e):
                    ins = None
                    for c in range(5, 8):
                        ins = e.activation(out=hT[:, c, t * 128:(t + 1) * 128], in_=ptv[:, c, :], func=AF.Identity,
                                           bias=bcol[:, BC_B0 + c:BC_B0 + c + 1], scale=bcol[:, BC_G0 + c:BC_G0 + c + 1])
                    return ins
                P.op("scalar", f_ev2, reads=["bcol"], writes=[rp, "hT:%d" % t])
            return [st_ld, st_stats, st_norm, st_tr, st_evac]

        pipeA = Pipe()
        for t in range(NT):
            pipeA.push(ln0_job(t))
            pipeA.tick()
        pipeA.drain()

        def r_fixup(tt):
            P.op("gpsimd", lambda e: e.tensor_tensor(out=R[:, tt, :], in0=R[:, tt, :], in1=prm[:, 0, :], op=ALU.mult),
                 reads=["lnparams"], writes=["R:%d" % tt])
            P.op("gpsimd", lambda e: e.tensor_tensor(out=R[:, tt, :], in0=R[:, tt, :], in1=prm[:, 1, :], op=ALU.add),
                 reads=["lnparams"], writes=["R:%d" % tt])

        def do_dumps(which):
            toks = []
            for name, ap in which:
                if name in dump_d:
                    toks.append(P.op("sync", dma(dump_d[name], ap), reads=[], dma_sem="dump"))
            return toks

        def finish_now(extra_toks):
            P.barrier()
            P.finish("sync", extra_toks)
            P.emit()
            return nc

        if stop_after == "A":
            P.barrier()
            toks = do_dumps([("hT", hT[:]), ("R", R[:])])
            return finish_now(toks)

        bB = Bump("B")
        attn_na = bB.n("attn_na").bf([4, 2048])
        attn_swa = bB.n("attn_swa").bf([4, 2048])
        qT = bB.n("qT").bf([2, 2048])
        kT = bB.n("kT").bf([TOK])
        vaug = bB.n("vaug", "vones").bf([NT, 192])
        PT = [bB.n("PT%d" % i).bf([768]) for i in range(2)]
        btab = bB.n("btab").bf([6400])
        smask = bB.n("smask").bf([1536])
        wq = bB.n("wq").bf([8, 256])
        qbb = [bB.n("qbb%d" % i).bf([512]) for i in range(2)]
        wk = bB.n("wk").bf([8, 128])
        wv = bB.n("wv").bf([8, 128])
        rotC = [bB.n("rotC%d" % i).f32([512]) for i in range(2)]
        rotS = [bB.n("rotS%d" % i).f32([512]) for i in range(2)]
        tmpa = [bB.n("tmpa%d" % i).f32([512]) for i in range(2)]
        tmpb = bB.n("tmpb").f32([512])
        bv_bc = bB.n("bv").f32([640])
        rec = [bB.n("rec%d" % i, "rec%da" % i).f32([256]) for i in range(2)]
        assert bB.off == 50944, bB.off

        P.op("sync", dma(bv_bc, brow_d[:, BR_BV:BR_BV + 640]), writes=["bv"], dma_sem="c3")
        P.op("gpsimd", dma(smask, smask_d), writes=["smask"], dma_sem="c4")
        P.op("gpsimd", lambda e: e.memset(vaug[:, :, 64:128], 1.0), writes=["vones"])

        psS_v = psS[:, :].rearrange("p (b n) -> p b n", b=2)

        def qcls(j):
            return {2: 0, 3: 1, 16: 3, 17: 4}.get(j, 2)

        def scls(j):
            return {2: 0, 17: 2}.get(j, 1)

        def project_fm(w_t, ncol0, tok0, bias_col, dst_ap, res_dst, k, eng, scale=None, wres="wk"):
            pa = psA[k % 2]
            rp = "psA%d" % (k % 2)
            P.op("tensor", mm_group(pa[:], [(w_t[:, kc, ncol0:ncol0 + 128], hT[:, kc, tok0:tok0 + 512])
                                            for kc in range(8)]),
                 reads=[wres] + hT_res(tok0, 512), writes=[rp])
            bc = bcol[:, bias_col:bias_col + 1]
            if eng == "split":
                def f_split(e):
                    ins = None
                    for (lo, hi, dap) in dst_ap:
                        ins = e.tensor_scalar(out=dap, in0=pa[lo:hi, :], scalar1=bcol[lo:hi, bias_col:bias_col + 1],
                                              scalar2=scale, op0=ALU.add, op1=ALU.mult)
                    return ins
                P.op("vector", f_split, reads=["bcol"], writes=[rp, res_dst])
            elif eng == "vector":
                P.op("vector", lambda e: e.tensor_scalar(out=dst_ap, in0=pa[:], scalar1=bc, scalar2=scale,
                                                         op0=ALU.add, op1=ALU.mult),
                     reads=["bcol"], writes=[rp, res_dst])
            else:
                P.op("scalar", lambda e: e.activation(out=dst_ap, in_=pa[:], func=AF.Identity, bias=bc, scale=1.0),
                     reads=["bcol"], writes=[rp, res_dst])

        def project_v(ncols, bvoff, dsts, k0):
            kk = k0
            for t0 in range(0, NT, 4):
                pa = psA[kk % 2]
                rp = "psA%d" % (kk % 2)

                def fn(e, t0=t0, pa=pa):
                    ins = None
                    for i in range(4):
                        for kc in range(8):
                            ins = e.matmul(pa[:, i * ncols:(i + 1) * ncols],
                                           lhsT=hT[:, kc, (t0 + i) * 128:(t0 + i + 1) * 128],
                                           rhs=wv[:, kc, 0:ncols], start=(kc == 0), stop=(kc == 7))
                    return ins
                P.op("tensor", fn, reads=["wv"] + hT_res(t0 * 128, 512), writes=[rp])
                pav = pa[:, 0:4 * ncols].rearrange("p (t c) -> p t c", t=4)
                for (src_off, dst_off) in dsts:
                    for i in range(4):
                        P.op("vector", lambda e, i=i, so=src_off, do=dst_off, pav=pav, t0=t0:
                             e.tensor_tensor(out=vaug[:, t0 + i, do:do + 64], in0=pav[:, i, so:so + 64],
                                             in1=bv_bc[:, bvoff + so:bvoff + so + 64], op=ALU.add),
                             reads=["bv"], writes=[rp, "vaug"])
                kk += 1
            return kk

        kctr = 0
        P.op("gpsimd", lambda e: e.memset(qT[:, :, :], 0.0), writes=["qT"])
        for hp in range(4):
            P.op("gpsimd", dma(wq[:, :, 0:128], w_in_v[:, :, 128 * hp:128 * hp + 128]), writes=["wq"], dma_sem="wq")
            P.op("gpsimd", dma(wk, w_in_v[:, :, 512 + 128 * hp:512 + 128 * hp + 128]), writes=["wk"], dma_sem="wk")
            P.op("gpsimd", dma(wv, w_in_v[:, :, 1024 + 128 * hp:1024 + 128 * hp + 128]), writes=["wv"], dma_sem="wv")
            P.op("gpsimd", dma(btab, nab_d[hp]), writes=["btab"], dma_sem="bt")
            for tt in range(4 * hp, 4 * hp + 4):
                r_fixup(tt)
            for nb in range(4):
                project_fm(wq, 0, HALO + 512 * nb, BC_QA + hp,
                           [(0, 64, qT[0:64, 0, 512 * nb:512 * nb + 512]), (64, 128, qT[64:128, 1, 512 * nb:512 * nb + 512])],
                           "qT", kctr, "split", scale=0.125, wres="wq")
                kctr += 1
            for kb in range(5):
                project_fm(wk, 0, 512 * kb, BC_KA + hp, kT[:, 512 * kb:512 * kb + 512], "kT", kctr, "scalar")
                kctr += 1
            kctr = project_v(128, 128 * hp, [(0, 0), (64, 128)], kctr)
            if stop_after == "B0":
                P.barrier()
                toks = do_dumps([("attn_na", attn_na), ("attn_swa", attn_swa), ("qT", qT), ("kT", kT), ("vaug", vaug)])
                return finish_now(toks)

            items = [(2 + 2 * jp + jj, e) for jp in range(NTO // 2) for e in range(2) for jj in range(2)]

            def na_scores(w, j, e, hp=hp):
                buf = w % 2
                cl = qcls(j)

                def fn(en):
                    ins = None
                    bo0 = ((cl * 5 + 0) * 2 + e) * 128
                    rhs4 = btab[:, bo0:bo0 + 1024].rearrange("p (s x) -> p s x", x=256)[:, :, 0:128]
                    en.matmul(psS_v[:, buf, 0:512].rearrange("p (s q) -> p s q", q=128), lhsT=ident[:], rhs=rhs4,
                              start=True, stop=False)
                    bo4 = ((cl * 5 + 4) * 2 + e) * 128
                    en.matmul(psS_v[:, buf, 512:640], lhsT=ident[:], rhs=btab[:, bo4:bo4 + 128], start=True, stop=False)
                    for si in range(5):
                        t = j + si - 2
                        o = psS_v[:, buf, si * 128:(si + 1) * 128]
                        ins = en.matmul(o, lhsT=kT[:, t * 128:(t + 1) * 128],
                                        rhs=qT[:, e, (j - 2) * 128:(j - 1) * 128], start=False, stop=(si >= 3))
                    return ins
                P.op("tensor", fn, reads=["kT", "qT", "btab", "ident"], writes=["psS%d" % buf])
                P.op("scalar", lambda en: en.activation(out=PT[buf][:, 0:640], in_=psS_v[:, buf, 0:640], func=AF.Exp),
                     writes=["psS%d" % buf, "PT%d" % buf])

            def na_pv(w, j, e, hp=hp):
                buf = w % 2
                ob = (w // 2) % 2
                jj = w % 2
                po = psO[ob][:, jj * 128:(jj + 1) * 128]

                def fn(en):
                    ins = None
                    for si in range(5):
                        t = j + si - 2
                        ins = en.matmul(po, lhsT=vaug[:, t, 64 * e:64 * e + 128],
                                        rhs=PT[buf][:, si * 128:(si + 1) * 128], start=(si == 0), stop=(si == 4))
                    return ins
                P.op("tensor", fn, reads=["PT%d" % buf, "vaug", "vones"], writes=["psT%d" % ob])
                if jj == 0:
                    return
                dl, dh = (64, 128) if e == 0 else (0, 64)
                ol, oh = (0, 64) if e == 0 else (64, 128)
                rc = rec[ob]
                pp = psO[ob][:, 0:256]
                j0 = j - 1
                P.op("scalar", lambda en: en.activation(out=rc[dl:dh, 0:256], in_=pp[dl:dh, :], func=AF.Ln),
                     writes=["psT%d" % ob, "rec%da" % ob])
                P.op("scalar", lambda en: en.activation(out=rc[dl:dh, 0:256], in_=rc[dl:dh, 0:256], func=AF.Exp, scale=-1.0),
                     reads=["rec%da" % ob], writes=["rec%d" % ob])
                P.op("vector", lambda en: en.tensor_tensor(out=attn_na[ol:oh, hp, (j0 - 2) * 128:j0 * 128],
                                                           in0=pp[ol:oh, :], in1=rc[dl:dh, 0:256], op=ALU.mult),
                     reads=["rec%d" % ob], writes=["psT%d" % ob, "attn_na"])

            for w in range(len(items) + 1):
                if w < len(items):
                    na_scores(w, *items[w])
                if w >= 1:
                    na_pv(w - 1, *items[w - 1])
            if stop_after == "B1":
                P.barrier()
                toks = do_dumps([("attn_na", attn_na), ("attn_swa", attn_swa), ("qT", qT), ("kT", kT), ("vaug", vaug)])
                return finish_now(toks)

        kTb = btab[:, 0:TOK]
        for g in range(2):
            P.op("gpsimd", dma(wq, w_in_v[:, :, 1536 + 256 * g:1536 + 256 * g + 256]), writes=["wq"], dma_sem="wq")
            for hh in range(2):
                P.op("gpsimd", dma(wk[:, :, 64 * hh:64 * hh + 64], w_in_v[:, :, 2048 + 64 * g:2048 + 64 * g + 64]),
                     writes=["wk"], dma_sem="wk")
            P.op("gpsimd", dma(wv[:, :, 0:64], w_in_v[:, :, 2176 + 64 * g:2176 + 64 * g + 64]), writes=["wv"], dma_sem="wv")
            if g == 0:
                P.op("gpsimd", lambda e: e.memset(kT[64:128, :], 0.0), writes=["kT"])
                P.op("gpsimd", lambda e: e.memset(kTb[0:64, :], 0.0), writes=["kT", "btab"])

            def rot_stages(w_a, wres, col0, tok0, c_d, s_d, tabtok0, bca, dst_ap, res_dst, k, split=None):
                rb = k % 2
                pa = psA[rb]
                rpa = "psA%d" % rb
                pp = psS[:, 1024 * rb:1024 * rb + 512]
                rpp = "psS%d" % rb
                qb = qbb[rb]
                ta = tmpa[rb]
                bc = bcol[:, bca:bca + 1]

                def stA():
                    P.op("sync", dma(rotC[rb], c_d[:, tabtok0:tabtok0 + 512]), writes=["rotC%d" % rb], dma_sem="rc%d" % rb)
                    P.op("sync", dma(rotS[rb], s_d[:, tabtok0:tabtok0 + 512]), writes=["rotS%d" % rb], dma_sem="rs%d" % rb)
                    P.op("tensor", mm_group(pa[:], [(w_a[:, kc, col0:col0 + 128], hT[:, kc, tok0:tok0 + 512])
                                                    for kc in range(8)]),
                         reads=[wres] + hT_res(tok0, 512), writes=[rpa])
                    P.op("scalar", lambda e: e.activation(out=qb, in_=pa[:], func=AF.Identity, bias=bc, scale=1.0),
                         reads=["bcol"], writes=[rpa, "qbb%d" % rb])
                    P.op("vector", lambda e: e.scalar_tensor_tensor(out=ta, in0=pa[:], scalar=bc, in1=rotC[rb],
                                                                    op0=ALU.add, op1=ALU.mult),
                         reads=["bcol", "rotC%d" % rb], writes=[rpa, "tmpa%d" % rb])

                def stB():
                    P.op("tensor", lambda e: e.matmul(pp, lhsT=permb[:], rhs=qb, start=True, stop=True),
                         reads=["permb", "qbb%d" % rb], writes=[rpp])
                    P.op("vector", lambda e: e.tensor_tensor(out=tmpb, in0=pp, in1=rotS[rb], op=ALU.mult),
                         reads=["rotS%d" % rb], writes=[rpp, "tmpb"])
                    if split is None:
                        P.op("gpsimd", lambda e: e.tensor_tensor(out=dst_ap, in0=ta, in1=tmpb, op=ALU.add),
                             reads=["tmpa%d" % rb, "tmpb"], writes=[res_dst])
                    else:
                        def f_sp(e):
                            ins = None
                            for (lo, hi, dap) in split:
                                ins = e.tensor_tensor(out=dap, in0=ta[lo:hi, :], in1=tmpb[lo:hi, :], op=ALU.add)
                            return ins
                        P.op("gpsimd", f_sp, reads=["tmpa%d" % rb, "tmpb"], writes=[res_dst, "btab"])
                return stA, stB

            blocks = []
            for c in range(2):
                for nb in range(4):
                    blocks.append(rot_stages(wq, "wq", 128 * c, HALO + 512 * nb, cq_d, sq_d, 512 * nb,
                                             BC_QB + 2 * g + c, qT[:, c, 512 * nb:512 * nb + 512], "qT", kctr))
                    kctr += 1
            for kb in range(5):
                blocks.append(rot_stages(wk, "wk", 0, 512 * kb, ck_d, sk_d, 512 * kb, BC_KB + g, None, "kT", kctr,
                                         split=[(0, 64, kT[0:64, 512 * kb:512 * kb + 512]),
                                                (64, 128, kTb[64:128, 512 * kb:512 * kb + 512])]))
                kctr += 1
            for bi in range(len(blocks) + 1):
                if bi < len(blocks):
                    blocks[bi][0]()
                if bi >= 1:
                    blocks[bi - 1][1]()
            kctr = project_v(64, 512 + 64 * g, [(0, 0), (0, 128)], kctr)

            items = [(j, e) for j in range(2, 2 + NTO) for e in range(2)]

            def sw_scores(w, j, e, g=g):
                buf = w % 2
                cl = scls(j)

                def fn(en):
                    ins = None
                    for si in range(3):
                        t = j + si - 1
                        o = psS_v[:, buf, si * 256:(si + 1) * 256]
                        kk_ = kT if e == 0 else kTb
                        ins = en.matmul(o.rearrange("p (c q) -> p c q", c=2),
                                        lhsT=kk_[:, t * 128:(t + 1) * 128],
                                        rhs=qT[:, :, (j - 2) * 128:(j - 1) * 128],
                                        start=True, stop=(si == 1))
                        if si != 1:
                            mo = (cl * 2 + (0 if si == 0 else 1)) * 256
                            ins = en.matmul(o, lhsT=ident[:], rhs=smask[:, mo:mo + 256], start=False, stop=True)
                    return ins
                P.op("tensor", fn, reads=["kT", "qT", "smask", "ident"], writes=["psS%d" % buf])
                P.op("scalar", lambda en: en.activation(out=PT[buf][:, 0:768], in_=psS_v[:, buf, 0:768], func=AF.Exp),
                     writes=["psS%d" % buf, "PT%d" % buf])

            def sw_pv(w, j, e, g=g):
                buf = w % 2
                ob = w % 2
                po = psO[ob][:, 0:256]

                def fn(en):
                    ins = None
                    for si in range(3):
                        t = j + si - 1
                        ins = en.matmul(po, lhsT=vaug[:, t, 64 * e:64 * e + 128],
                                        rhs=PT[buf][:, si * 256:(si + 1) * 256], start=(si == 0), stop=False)
                    so = (g * 2 + e) * 256
                    return en.matmul(po, lhsT=indrow[0:1, e * 128:(e + 1) * 128], rhs=sinkrow[0:1, so:so + 256],
                                     start=False, stop=True)
                P.op("tensor", fn, reads=["PT%d" % buf, "vaug", "vones", "indrow", "sinkrow"], writes=["psT%d" % ob])
                dl, dh = (64, 128) if e == 0 else (0, 64)
                ol, oh = (0, 64) if e == 0 else (64, 128)
                rc = rec[ob]

                P.op("scalar", lambda en: en.activation(out=rc[dl:dh, 0:256], in_=po[dl:dh, 0:256], func=AF.Ln),
                     writes=["psT%d" % ob, "rec%da" % ob])
                P.op("scalar", lambda en: en.activation(out=rc[dl:dh, 0:256], in_=rc[dl:dh, 0:256], func=AF.Exp, scale=-1.0),
                     reads=["rec%da" % ob], writes=["rec%d" % ob])
                pov = po.rearrange("p (c q) -> p c q", c=2)
                rcv = rc[:, :].rearrange("p (c q) -> p c q", c=2)
                P.op("vector", lambda en: en.tensor_tensor(
                    out=attn_swa[ol:oh, 2 * g:2 * g + 2, (j - 2) * 128:(j - 1) * 128],
                    in0=pov[ol:oh], in1=rcv[dl:dh], op=ALU.mult),
                    reads=["rec%d" % ob], writes=["psT%d" % ob, "attn_swa"])

            for w in range(len(items) + 1):
                if w < len(items):
                    sw_scores(w, *items[w])
                if w >= 1:
                    sw_pv(w - 1, *items[w - 1])

        if stop_after == "B":
            P.barrier()
            toks = do_dumps([("attn_na", attn_na), ("attn_swa", attn_swa), ("qT", qT), ("kT", kT), ("vaug", vaug)])
            return finish_now(toks)

        bC = Bump("C1")
        mixed = bC.at(16384).n("mixed").bf([8, 2048])
        bC.at(36352)
        wg = [[bC.n("wg%d" % b).bf([8, 128]) for _ in range(2)] for b in range(2)]
        wb = [[bC.n("wb%d" % b).bf([4, 128]) for _ in range(2)] for b in range(2)]
        sg = [bC.n("sg%d" % i).f32([512]) for i in range(2)]
        tm = [bC.n("tm%d" % i).f32([512]) for i in range(2)]
        wout_a = bC.n("wout0").bf([8, 512])
        assert bC.off <= ARENA
        boutb = bC.at(32768).n("boutb").f32([1024])
        P.op("sync", dma(boutb, brow_d[:, BR_BOUT:BR_BOUT + 1024]), writes=["boutb"], dma_sem="bo")
        pre_tiles = list(range(NTO))
        banks = [psA[0][:], psA[1][:], psS[:, 0:512], psS[:, 512:1024], psS[:, 1024:1536], psS[:, 1536:2048],
                 psO[0], psO[1]]
        it = 0
        def load_f(f):
            fb = f % 2
            P.op("gpsimd", dma(wg[fb][0], w_in_v[:, :, 2304 + 128 * f:2304 + 128 * f + 128]), writes=["wg%d" % fb], dma_sem="wg%d" % fb)
            P.op("gpsimd", dma(wg[fb][1], w_in_v[:, :, 3328 + 128 * f:3328 + 128 * f + 128]), writes=["wg%d" % fb], dma_sem="wg%d" % fb)
            P.op("gpsimd", dma(wb[fb][0], w_na_v[:, :, 128 * f:128 * f + 128]), writes=["wb%d" % fb], dma_sem="wb%d" % fb)
            P.op("gpsimd", dma(wb[fb][1], w_swa_v[:, :, 128 * f:128 * f + 128]), writes=["wb%d" % fb], dma_sem="wb%d" % fb)

        load_f(0)
        for f in range(8):
            fb = f % 2
            if f + 1 < 8:
                load_f(f + 1)
            if f == 1:
                P.op("gpsimd", dma(wout_a, w_out_v[:, :, 0:512]), writes=["wout0"], dma_sem="wout0")
            for nb in range(4):
                s4 = (it % 2) * 4
                it += 1
                tok0 = HALO + 512 * nb
                pga, pgb, pyn, pys = banks[s4:s4 + 4]
                rn = ["bank%d" % (s4 + i) for i in range(4)]
                P.op("tensor", mm_group(pga, [(wg[fb][0][:, kc, :], hT[:, kc, tok0:tok0 + 512]) for kc in range(8)]),
                     reads=["wg%d" % fb] + hT_res(tok0, 512), writes=[rn[0]])
                P.op("tensor", mm_group(pgb, [(wg[fb][1][:, kc, :], hT[:, kc, tok0:tok0 + 512]) for kc in range(8)]),
                     reads=["wg%d" % fb] + hT_res(tok0, 512), writes=[rn[1]])
                P.op("tensor", mm_group(pyn, [(wb[fb][0][:, c, :], attn_na[:, c, 512 * nb:512 * nb + 512]) for c in range(4)]),
                     reads=["wb%d" % fb, "attn_na"], writes=[rn[2]])
                P.op("tensor", mm_group(pys, [(wb[fb][1][:, c, :], attn_swa[:, c, 512 * nb:512 * nb + 512]) for c in range(4)]),
                     reads=["wb%d" % fb, "attn_swa"], writes=[rn[3]])
                P.op("scalar", lambda e, pga=pga, f=f: e.activation(out=sg[0], in_=pga, func=AF.Sigmoid,
                                                                     bias=bcol[:, BC_GA + f:BC_GA + f + 1], scale=1.0),
                     reads=["bcol"], writes=[rn[0], "sg0"])
                P.op("scalar", lambda e, pgb=pgb, f=f: e.activation(out=sg[1], in_=pgb, func=AF.Sigmoid,
                                                                     bias=bcol[:, BC_GB + f:BC_GB + f + 1], scale=1.0),
                     reads=["bcol"], writes=[rn[1], "sg1"])
                P.op("vector", lambda e, pyn=pyn: e.tensor_tensor(out=tm[0], in0=pyn, in1=sg[0], op=ALU.mult),
                     reads=["sg0"], writes=[rn[2], "tm0"])
                P.op("vector", lambda e, pys=pys: e.tensor_tensor(out=tm[1], in0=pys, in1=sg[1], op=ALU.mult),
                     reads=["sg1"], writes=[rn[3], "tm1"])
                P.op("gpsimd", lambda e, f=f, nb=nb: e.tensor_tensor(out=mixed[:, f, 512 * nb:512 * nb + 512],
                                                                      in0=tm[0], in1=tm[1], op=ALU.add),
                     reads=["tm0", "tm1"], writes=["mixed"])
                if pre_tiles:
                    tt = pre_tiles.pop(0)
                    P.op("vector", lambda e, tt=tt: e.scalar_tensor_tensor(out=R[:, tt, :], in0=R[:, tt, :], scalar=ALPHA,
                                                                          in1=boutb, op0=ALU.mult, op1=ALU.add),
                         reads=["boutb"], writes=["R:%d" % tt])

        bD = Bump("C2")
        wout_b = bD.at(0).n("wout1").bf([8, 512])
        prm1 = bD.n("lnparams").f32([2, 1024])
        zb = [bD.n("z%d" % i).f32([1024]) for i in range(3)]
        hbC = [bD.n("hb%d" % i).bf([1024]) for i in range(2)]
        assert bD.off <= 16384
        wouts = [wout_a, wout_b]
        P.op("gpsimd", dma(wout_b, w_out_v[:, :, 512:1024]), writes=["wout1"], dma_sem="wout1")
        P.op("sync", dma(prm1[:, 0, :], brow_d[:, BR_LN1G:BR_LN1G + 1024]), writes=["lnparams"], dma_sem="prm")
        P.op("sync", dma(prm1[:, 1, :], brow_d[:, BR_LN1B:BR_LN1B + 1024]), writes=["lnparams"], dma_sem="prm")
        def c2_mm(tt):
            for half in range(2):
                P.op("tensor", mm_group(psA[half][:], [(mixed[:, f, tt * 128:(tt + 1) * 128],
                                                        wouts[half][:, f, :]) for f in range(8)]),
                     reads=["mixed", "wout%d" % half], writes=["psA%d" % half])

        def c2_add(tt):
            z = zb[tt % 3]
            rz = "z%d" % (tt % 3)
            for half in range(2):
                P.op("vector", lambda e, half=half, z=z, tt=tt: e.tensor_tensor(
                    out=z[:, 512 * half:512 * half + 512], in0=R[:, tt, 512 * half:512 * half + 512],
                    in1=psA[half][:], op=ALU.add),
                    reads=["R:%d" % tt], writes=["psA%d" % half, rz])

        def ln1_job(tt):
            z = zb[tt % 3]
            rz = "z%d" % (tt % 3)
            hb = hbC[tt % 2]
            rhb = "hb%d" % (tt % 2)
            o = (tt % 4) * 16
            stt = small[:, o:o + 12]
            mv = small[:, o + 12:o + 14]
            rs = small[:, o + 14:o + 15]
            nm = small[:, o + 15:o + 16]
            sn = "lnst%d" % (tt % 4)
            pt = psT[tt % 2]
            rp = "psT%d" % (tt % 2)
            ptv = pt[:, :].rearrange("p (c q) -> p c q", q=128)
            tok0 = HALO + tt * 128

            def st_stats():
                def f_stats(e):
                    e.bn_stats(stt[:, 0:6], z[:, 0:512])
                    return e.bn_stats(stt[:, 6:12], z[:, 512:1024])
                P.op("vector", f_stats, reads=[rz], writes=[sn + "a"])
                P.op("vector", lambda e: e.bn_aggr(mv, stt), reads=[sn + "a"], writes=[sn + "b"])
                P.op("scalar", lambda e: e.activation(out=rs, in_=mv[:, 1:2], func=AF.Sqrt, bias=epsc[:, 0:1], scale=1.0),
                     reads=[sn + "b", "epsc"], writes=[sn + "c0"])

            def st_norm():
                P.op("vector", lambda e: e.reciprocal(out=rs, in_=rs), reads=[sn + "c0"], writes=[sn + "c"])
                P.op("vector", lambda e: e.tensor_scalar(out=nm, in0=mv[:, 0:1], scalar1=-1.0, scalar2=rs,
                                                         op0=ALU.mult, op1=ALU.mult),
                     reads=[sn + "b", sn + "c"], writes=[sn + "d"])
                P.op("scalar", lambda e: e.activation(out=hb, in_=z, func=AF.Identity, bias=nm, scale=rs),
                     reads=[rz, sn + "c", sn + "d"], writes=[rhb])
                P.op("scalar", lambda e: e.activation(out=R[:, tt, :], in_=z, func=AF.Identity, bias=nm, scale=rs),
                     reads=[rz, sn + "c", sn + "d"], writes=["R:%d" % tt])

            def st_tr():
                def f_tr(e):
                    ins = None
                    for c in range(8):
                        ins = e.transpose(out=pt[:, c * 128:(c + 1) * 128], in_=hb[:, c * 128:(c + 1) * 128],
                                          identity=ident[:])
                    return ins
                P.op("tensor", f_tr, reads=[rhb, "ident"], writes=[rp])
                P.op("gpsimd", lambda e: e.tensor_tensor(out=R[:, tt, :], in0=R[:, tt, :], in1=prm1[:, 0, :], op=ALU.mult),
                     reads=["lnparams"], writes=["R:%d" % tt])
                P.op("gpsimd", lambda e: e.tensor_tensor(out=R[:, tt, :], in0=R[:, tt, :], in1=prm1[:, 1, :], op=ALU.add),
                     reads=["lnparams"], writes=["R:%d" % tt])

            def st_evac():
                def f_ev(e):
                    ins = None
                    for c in range(4):
                        ins = e.tensor_scalar(out=hT[:, c, tok0:tok0 + 128], in0=ptv[:, c, :],
                                              scalar1=bcol[:, BC_G1 + c:BC_G1 + c + 1],
                                              scalar2=bcol[:, BC_B1 + c:BC_B1 + c + 1], op0=ALU.mult, op1=ALU.add)
                    return ins
                P.op("vector", f_ev, reads=["bcol"], writes=[rp, "hT:%d" % (tok0 // 128)])

                def f_ev2(e):
                    ins = None
                    for c in range(4, 8):
                        ins = e.activation(out=hT[:, c, tok0:tok0 + 128], in_=ptv[:, c, :], func=AF.Identity,
                                           bias=bcol[:, BC_B1 + c:BC_B1 + c + 1], scale=bcol[:, BC_G1 + c:BC_G1 + c + 1])
                    return ins
                P.op("scalar", f_ev2, reads=["bcol"], writes=[rp, "hT:%d" % (tok0 // 128)])
            return [st_stats, st_norm, st_tr, st_evac]

        bE = Bump("D")
        w1 = [None, None]
        w2 = [None, None]
        w1[0] = bE.at(32768).n("w1_0").bf([8, 1024])
        P.op("gpsimd", dma(w1[0], w_ff1_v[:, :, 0:1024]), writes=["w1_0"], dma_sem="w1_0")
        pipeC = Pipe()
        c2_mm(0)
        c2_add(0)
        for tt in range(NTO):
            if tt + 1 < NTO:
                c2_mm(tt + 1)
            pipeC.push(ln1_job(tt))
            pipeC.tick()
            if tt + 1 < NTO:
                c2_add(tt + 1)
        pipeC.drain()

        rt = [bE.n("rt%d" % i).f32([512]) for i in range(2)]
        prm2 = bE.n("lnparams").f32([3, 1024])
        w2[0] = bE.at(0).n("w2_0").bf([8, 1024])
        P.op("gpsimd", dma(w2[0], w_ff2_v[:, 0:8, :]), writes=["w2_0"], dma_sem="w2_0")
        ub = [bE.n(*["u%d_%d" % (i, f) for f in range(8)]).bf([8, 512]) for i in range(2)]
        w1[1] = bE.at(16384).n("w1_1").bf([8, 1024])
        w2[1] = bE.n("w2_1").bf([8, 1024])
        P.op("sync", dma(prm2[:, 0, :], brow_d[:, BR_LN2G:BR_LN2G + 1024]), writes=["lnparams"], dma_sem="prm")
        P.op("sync", dma(prm2[:, 1, :], brow_d[:, BR_LN2B:BR_LN2B + 1024]), writes=["lnparams"], dma_sem="prm")
        P.op("sync", dma(prm2[:, 2, :], brow_d[:, BR_BFF2:BR_BFF2 + 1024]), writes=["lnparams"], dma_sem="prm")

        if stop_after == "C":
            P.barrier()
            toks = do_dumps([("hT", hT[:]), ("R", R[:])])
            return finish_now(toks)

        xbanks = [psA[0][:], psA[1][:], psO[0], psO[1]]
        ybanks = [psS[:, 0:512], psS[:, 512:1024], psS[:, 1024:1536], psS[:, 1536:2048]]
        out_toks = []
        deferred = []
        cnt = {"x": 0, "y": 0}

        def load_q(qf):
            wbuf = qf % 2
            if qf > 0:
                P.op("gpsimd", dma(w1[wbuf], w_ff1_v[:, :, 1024 * qf:1024 * qf + 1024]), writes=["w1_%d" % wbuf], dma_sem="w1_%d" % wbuf)
                P.op("gpsimd", dma(w2[wbuf], w_ff2_v[:, 8 * qf:8 * qf + 8, :]), writes=["w2_%d" % wbuf], dma_sem="w2_%d" % wbuf)

        def ffn1(bi, qf, nb):
            wbuf = qf % 2
            u = ub[bi % 2]
            ru = "u%d" % (bi % 2)
            tok0 = HALO + 512 * nb
            for fp in range(4):
                xa = cnt["x"] % 4
                cnt["x"] += 2
                pxs = [xbanks[xa], xbanks[xa + 1]]
                rpxs = ["xb%d" % xa, "xb%d" % (xa + 1)]

                def fn(e, fp=fp, pxs=pxs, wbuf=wbuf, tok0=tok0):
                    ins = None
                    for i in range(2):
                        ffc = 2 * fp + i
                        for kc in range(8):
                            ins = e.matmul(pxs[i], lhsT=w1[wbuf][:, kc, 128 * ffc:128 * ffc + 128],
                                           rhs=hT[:, kc, tok0:tok0 + 512], start=(kc == 0), stop=(kc == 7))
                    return ins
                P.op("tensor", fn, reads=["w1_%d" % wbuf] + hT_res(tok0, 512), writes=rpxs)
                for i in range(2):
                    ffc = 2 * fp + i
                    r_ = rt[i]
                    rr = "rt%d" % i
                    bcidx = BC_FF1 + 8 * qf + ffc
                    P.op("scalar", lambda e, px=pxs[i], r_=r_, bcidx=bcidx: e.activation(
                        out=r_, in_=px, func=AF.Relu, bias=bcol[:, bcidx:bcidx + 1], scale=1.0),
                        reads=["bcol"], writes=[rpxs[i], rr])
                    P.op("gpsimd", lambda e, r_=r_, u=u, ffc=ffc: e.tensor_tensor(out=u[:, ffc, :], in0=r_, in1=r_, op=ALU.mult),
                         reads=[rr], writes=[ru + "_%d" % ffc])

        def ffn2(bi, qf, nb):
            wbuf = qf % 2
            u = ub[bi % 2]
            ru = "u%d" % (bi % 2)
            for ti in range(4):
                tt = 4 * nb + ti
                ya = cnt["y"] % 4
                cnt["y"] += 2
                pys = [ybanks[ya], ybanks[ya + 1]]
                rpys = ["yb%d" % ya, "yb%d" % (ya + 1)]

                def fn(e, ti=ti, pys=pys, u=u, wbuf=wbuf):
                    ins = None
                    for half in range(2):
                        for ffc in range(8):
                            ins = e.matmul(pys[half], lhsT=u[:, ffc, ti * 128:(ti + 1) * 128],
                                           rhs=w2[wbuf][:, ffc, 512 * half:512 * half + 512],
                                           start=(ffc == 0), stop=(ffc == 7))
                    return ins
                P.op("tensor", fn, reads=["w2_%d" % wbuf] + [ru + "_%d" % ffc for ffc in range(8)], writes=rpys)
                for half in range(2):
                    py = pys[half]
                    if qf == 0:
                        P.op("vector", lambda e, py=py, tt=tt, half=half: e.scalar_tensor_tensor(
                            out=R[:, tt, 512 * half:512 * half + 512], in0=R[:, tt, 512 * half:512 * half + 512],
                            scalar=ALPHA, in1=py, op0=ALU.mult, op1=ALU.add),
                            writes=[rpys[half], "R:%d" % tt])
                    else:
                        P.op("vector", lambda e, py=py, tt=tt, half=half: e.tensor_tensor(
                            out=R[:, tt, 512 * half:512 * half + 512], in0=R[:, tt, 512 * half:512 * half + 512],
                            in1=py, op=ALU.add),
                            writes=[rpys[half], "R:%d" % tt])
                if qf == 0:
                    P.op("gpsimd", lambda e, tt=tt: e.tensor_tensor(out=R[:, tt, :], in0=R[:, tt, :], in1=prm2[:, 2, :],
                                                                   op=ALU.add),
                         reads=["lnparams"], writes=["R:%d" % tt])
                if qf == 3:
                    prev = list(deferred)
                    del deferred[:]
                    layer_norm(R[:, tt, :], R[:, tt, :], prm2[:, 0, :], prm2[:, 1, :], tt, "R:%d" % tt, "R:%d" % tt,
                               defer=deferred)

                    def st_out(tt=tt):
                        out_toks.append(P.op("sync", dma(out_d[tt * 128:(tt + 1) * 128, :], R[:, tt, :]),
                                             reads=["R:%d" % tt], dma_sem="out"))
                    deferred.append(st_out)
                    for f_ in prev:
                        f_()

        steps = [(qf, nb) for qf in range(4) for nb in range(4)]
        load_q(0)
        load_q(1)
        ffn1(0, *steps[0])
        for bi, (qf, nb) in enumerate(steps):
            if bi + 1 < len(steps):
                ffn1(bi + 1, *steps[bi + 1])
            ffn2(bi, qf, nb)
            if nb == 3 and qf + 2 < 4:
                load_q(qf + 2)
        for f_ in deferred:
            f_()
        P.finish("sync", out_toks)
        P.emit()
    return nc


def _na_tables(rpb, ch):
    g0 = 16 * ch
    rep_local = [2, 3, 4, 16, 17]
    kk = np.arange(128)
    rk, ck = kk // 64, kk % 64
    out = np.full((4, 128, 5, 5, 2, 128), NEG, np.float32)
    for cl, jq in enumerate(rep_local):
        gq = g0 + jq - 2
        Rq = 2 * gq + rk
        cq = ck
        rs = np.clip(Rq - 4, 0, 120)
        cs = np.clip(cq - 8, 0, 48)
        for si in range(5):
            lk = jq + si - 2
            gk = g0 + lk - 2
            if ch == 0 and lk == 0:
                gk = 3
            if ch == 3 and lk == 19:
                gk = 60
            if gk < 0 or gk > 63:
                continue
            Rk = 2 * gk + rk
            valid = ((Rk[:, None] >= rs[None, :]) & (Rk[:, None] <= rs[None, :] + 7)
                     & (ck[:, None] >= cs[None, :]) & (ck[:, None] < cs[None, :] + 16))
            dr = np.clip(Rk[:, None] - Rq[None, :] + 7, 0, 14)
            dc = np.clip(ck[:, None] - cq[None, :] + 15, 0, 30)
            for h in range(8):
                tab = np.where(valid, rpb[h][dr, dc], np.float32(NEG)).astype(np.float32)
                out[h // 2, :, cl, si, h % 2, :] = tab
    return np.ascontiguousarray(out.reshape(4, 128, 6400))


def _swa_masks(ch):
    g0 = 16 * ch
    kk = np.arange(128)
    out = np.zeros((128, 3, 2, 2, 128), np.float32)
    for cl, jq in enumerate([2, 4, 17]):
        gq = g0 + jq - 2
        qpos = gq * 128 + kk
        for si, s in enumerate([-1, 1]):
            kpos = (gq + s) * 128 + kk
            valid = (np.abs(kpos[:, None] - qpos[None, :]) <= 128) & (kpos[:, None] >= 0) & (kpos[:, None] < 8192)
            m = np.where(valid, np.float32(0.0), np.float32(NEG)).astype(np.float32)
            out[:, cl, si, 0, :] = m
            out[:, cl, si, 1, :] = m
    return np.ascontiguousarray(out.reshape(128, 1536))


def _rot_tables(pos, scale):
    inv_freq = np.power(np.float32(500000.0), -np.arange(0, 16, 2, dtype=np.float32) / np.float32(16)).astype(np.float32)
    ang = pos.astype(np.float32)[:, None] * inv_freq[None, :]
    cos = np.cos(ang).astype(np.float32).T
    sin = np.sin(ang).astype(np.float32).T
    C = np.ones((128, pos.shape[0]), np.float32)
    S = np.zeros((128, pos.shape[0]), np.float32)
    for base in (0, 64):
        C[base:base + 8] = cos
        C[base + 8:base + 16] = cos
        S[base:base + 8] = -sin
        S[base + 8:base + 16] = sin
    return np.ascontiguousarray(C * np.float32(scale)), np.ascontiguousarray(S * np.float32(scale))


def _swap_cols(w, nheads):
    out = np.zeros_like(w)
    for h in range(nheads):
        b = 64 * h
        out[..., b:b + 8] = w[..., b + 8:b + 16]
        out[..., b + 8:b + 16] = w[..., b:b + 8]
    return out


def make_in_maps(inputs):
    f = lambda a: np.ascontiguousarray(np.asarray(a, dtype=np.float32))
    x = f(inputs["x"])
    w_in = f(inputs["w_in"])[0]
    b_in = f(inputs["b_in"])[0]
    rpb = f(inputs["na_rpb"])[0]
    sink = f(inputs["swa_sink"])[0]
    wq_b = w_in[:, 1536:2048]
    wk_b = w_in[:, 2048:2176]
    perm = np.zeros((128, 128), np.float32)
    for m in range(128):
        b0, d = (m // 64) * 64, m % 64
        k = b0 + d + 8 if d < 8 else (b0 + d - 8 if d < 16 else m)
        perm[k, m] = 1.0
    bq_s = _swap_cols(b_in[1536:2048], 8)
    bk_s = _swap_cols(b_in[2048:2176], 2)
    bcol = np.zeros((128, NBCOL), np.float32)
    bcol[:, BC_QA:BC_QA + 4] = b_in[0:512].reshape(4, 128).T
    bcol[:, BC_KA:BC_KA + 4] = b_in[512:1024].reshape(4, 128).T
    bcol[:, BC_QB:BC_QB + 4] = b_in[1536:2048].reshape(4, 128).T
    bcol[:, BC_QBS:BC_QBS + 4] = bq_s.reshape(4, 128).T
    for g in range(2):
        bcol[:, BC_KB + g] = np.tile(b_in[2048 + 64 * g:2048 + 64 * g + 64], 2)
        bcol[:, BC_KBS + g] = np.tile(bk_s[64 * g:64 * g + 64], 2)
    bcol[:, BC_GA:BC_GA + 8] = b_in[2304:3328].reshape(8, 128).T
    bcol[:, BC_GB:BC_GB + 8] = b_in[3328:4352].reshape(8, 128).T
    bcol[:, BC_FF1:BC_FF1 + 32] = f(inputs["b_ff1"])[0].reshape(32, 128).T
    bcol[:, BC_G0:BC_G0 + 8] = f(inputs["ln0_g"]).reshape(8, 128).T
    bcol[:, BC_B0:BC_B0 + 8] = f(inputs["ln0_b"]).reshape(8, 128).T
    bcol[:, BC_G1:BC_G1 + 8] = f(inputs["ln1_g"])[0].reshape(8, 128).T
    bcol[:, BC_B1:BC_B1 + 8] = f(inputs["ln1_b"])[0].reshape(8, 128).T
    brow = np.zeros((NBROW,), np.float32)
    brow[BR_LN0G:BR_LN0G + 1024] = f(inputs["ln0_g"])
    brow[BR_LN0B:BR_LN0B + 1024] = f(inputs["ln0_b"])
    brow[BR_BV:BR_BV + 512] = b_in[1024:1536]
    brow[BR_BV + 512:BR_BV + 640] = b_in[2176:2304]
    brow[BR_BOUT:BR_BOUT + 1024] = f(inputs["b_out"])[0]
    brow[BR_LN1G:BR_LN1G + 1024] = f(inputs["ln1_g"])[0]
    brow[BR_LN1B:BR_LN1B + 1024] = f(inputs["ln1_b"])[0]
    brow[BR_BFF2:BR_BFF2 + 1024] = f(inputs["b_ff2"])[0]
    brow[BR_LN2G:BR_LN2G + 1024] = f(inputs["ln2_g"])[0]
    brow[BR_LN2B:BR_LN2B + 1024] = f(inputs["ln2_b"])[0]
    brow_bc = np.ascontiguousarray(np.broadcast_to(brow[None, :], (128, NBROW)))
    sink_bc = np.ascontiguousarray(np.broadcast_to(sink[None, :], (128, 8)))
    shared = {
        "w_in": w_in, "perm": perm, "w_na": f(inputs["w_branch_na"])[0], "w_swa": f(inputs["w_branch_swa"])[0],
        "w_out": f(inputs["w_out"])[0], "w_ff1": f(inputs["w_ff1"])[0], "w_ff2": f(inputs["w_ff2"])[0],
        "bcol": bcol, "brow": brow_bc, "sink": sink_bc, "ident": np.eye(128, dtype=np.float32),
    }
    tabs = {ch: (_na_tables(rpb, ch), _swa_masks(ch)) for ch in range(4)}
    in_maps = []
    for core in range(NCORES):
        b, ch = core // 4, core % 4
        g0 = 16 * ch
        xl = np.zeros((TOK, D), np.float32)
        pos = np.zeros((TOK,), np.int64)
        for t in range(NT):
            gt = g0 + t - 2
            if ch == 0 and t == 0:
                gt = 3
            if ch == 3 and t == 19:
                gt = 60
            if 0 <= gt < 64:
                xl[t * 128:(t + 1) * 128] = x[b, gt * 128:(gt + 1) * 128]
                pos[t * 128:(t + 1) * 128] = gt * 128 + np.arange(128)
        cq, sq = _rot_tables(pos[HALO:HALO + 2048], 0.125)
        ck, sk = _rot_tables(pos, 1.0)
        m = dict(shared)
        m.update({"x": xl, "nabias": tabs[ch][0], "swamask": tabs[ch][1],
                  "rot_cq": cq, "rot_sq": sq, "rot_ck": ck, "rot_sk": sk})
        in_maps.append(m)
    return in_maps


_NC_CACHE = {}


def kernel(**inputs):
    in_maps = make_in_maps(inputs)
    if "nc" not in _NC_CACHE:
        _NC_CACHE["nc"] = build_program()
    nc = _NC_CACHE["nc"]
    res = run_bass_kernel_spmd(nc, in_maps, core_ids=list(range(NCORES)))
    out = np.zeros((2, 8192, D), np.float32)
    for core in range(NCORES):
        b, ch = core // 4, core % 4
        out[b, ch * 2048:(ch + 1) * 2048] = res.results[core]["out"]
    return out
```

```python
import numpy as np
from contextlib import ExitStack
import concourse.bass as bass
import concourse.mybir as mybir
from concourse.bass_utils import run_bass_kernel_spmd

F32 = mybir.dt.float32
BF16 = mybir.dt.bfloat16
AF = mybir.ActivationFunctionType
ALU = mybir.AluOpType

ENGS = ("tensor", "vector", "scalar", "gpsimd", "sync")

NCORES = 8
D = 1024
NT = 20
NTO = 16
TOK = NT * 128
HALO = 256
IN_W = 4352
NEG = -30000.0
ALPHA = 2.0 ** 0.25
EPS = 1e-5


class Prog:
    def __init__(self, nc):
        self.nc = nc
        self.ops = {e: [] for e in ENGS}
        self.cnt = {e: 0 for e in ENGS}
        self.last_w = {}
        self.readers = {}
        self.waited = {e: {} for e in ENGS}
        self.dma_sems = {}
        self.final = {}
        self.alias = {}
        self.rename = {}
        self.expand = {}

    def register(self, phase, names, lo, hi):
        for n in names:
            q = phase + "." + n
            self.rename[n] = q
            self.alias.setdefault(q, []).append((lo, hi))

    def _map(self, names):
        out = []
        for n in names:
            for m in self.expand.get(n, [n]):
                out.append(self.rename.get(m, m))
        return out

    def _overlaps(self, r):
        res = []
        for (lo, hi) in self.alias.get(r, ()):
            for q, ivs in self.alias.items():
                if q == r:
                    continue
                for (a, b) in ivs:
                    if a < hi and lo < b:
                        res.append(q)
                        break
        return res

    def _need(self, eng, waits, tok):
        if tok is None:
            return
        key, val = tok
        if key == eng and eng == "tensor":
            return
        if val > self.waited[eng].get(key, 0):
            self.waited[eng][key] = val
            waits[key] = max(waits.get(key, 0), val)

    def op(self, eng, fn, reads=(), writes=(), dma_sem=None):
        waits = {}
        reads = self._map(reads)
        writes = self._map(writes)
        for r in reads:
            self._need(eng, waits, self.last_w.get(r))
        for r in writes:
            self._need(eng, waits, self.last_w.get(r))
            for k, v in self.readers.get(r, {}).items():
                self._need(eng, waits, (k, v))
            for q in self._overlaps(r):
                self._need(eng, waits, self.last_w.get(q))
                for k, v in self.readers.get(q, {}).items():
                    self._need(eng, waits, (k, v))
        if dma_sem is not None:
            ent = self.dma_sems.setdefault(dma_sem, [0])
            ent[0] += 16
            tok = ("dma:" + dma_sem, ent[0])
            inc = ("dma:" + dma_sem, 16)
        else:
            self.cnt[eng] += 1
            tok = (eng, self.cnt[eng])
            inc = (eng, 1)
        for r in reads:
            d = self.readers.setdefault(r, {})
            d[tok[0]] = max(d.get(tok[0], 0), tok[1])
        for r in writes:
            self.last_w[r] = tok
            self.readers[r] = {}
        self.ops[eng].append((waits, fn, inc))
        return tok

    def barrier(self):
        snap = {e: self.cnt[e] for e in ENGS if self.cnt[e] > 0}
        for k, v in self.dma_sems.items():
            snap["dma:" + k] = v[0]
        for e in ENGS:
            waits = {}
            for k, v in snap.items():
                if k == e:
                    continue
                if v > self.waited[e].get(k, 0):
                    self.waited[e][k] = v
                    waits[k] = v
            if waits:
                self.ops[e].append((waits, None, None))

    def finish(self, eng, toks):
        d = self.final.setdefault(eng, {})
        for k, v in toks:
            d[k] = max(d.get(k, 0), v)

    def emit(self):
        nc = self.nc
        with ExitStack() as st:
            sems = {}
            for e in ENGS:
                sems[e] = st.enter_context(nc.semaphore("s_" + e))
            for k in self.dma_sems:
                sems["dma:" + k] = st.enter_context(nc.semaphore("d_" + k))
            block = st.enter_context(nc.Block())

            def make(eng):
                def body(e):
                    for waits, fn, inc in self.ops[eng]:
                        for k, v in waits.items():
                            e.wait_ge(sems[k], v)
                        if fn is not None:
                            ins = fn(e)
                            ins.then_inc(sems[inc[0]], inc[1])
                    for k, v in self.final.get(eng, {}).items():
                        e.wait_ge(sems[k], v)
                return body

            for eng in ENGS:
                if self.ops[eng] or eng in self.final:
                    getattr(block, eng)(make(eng))


BC_QA, BC_KA, BC_QB, BC_QBS, BC_KB, BC_KBS, BC_GA, BC_GB, BC_FF1 = 0, 4, 8, 12, 16, 18, 20, 28, 36
BC_G0, BC_B0, BC_G1, BC_B1 = 68, 76, 84, 92
NBCOL = 100
BR_LN0G, BR_LN0B, BR_BV, BR_BOUT, BR_LN1G, BR_LN1B, BR_BFF2, BR_LN2G, BR_LN2B = (
    0, 1024, 2048, 2688, 3712, 4736, 5760, 6784, 7808)
NBROW = 8832


def build_program(stop_after=None, dumps=()):
    nc = bass.Bass("TRN2", target_bir_lowering=False)

    def din(name, shape, dt=F32):
        return nc.dram_tensor(name, list(shape), dt, kind="ExternalInput").ap()

    x_d = din("x", [TOK, D])
    w_in_d = din("w_in", [D, IN_W])
    perm_d = din("perm", [128, 128])
    w_na_d = din("w_na", [512, D])
    w_swa_d = din("w_swa", [512, D])
    w_out_d = din("w_out", [D, D])
    w_ff1_d = din("w_ff1", [D, 4096])
    w_ff2_d = din("w_ff2", [4096, D])
    bcol_d = din("bcol", [128, NBCOL])
    brow_d = din("brow", [128, NBROW])
    sink_d = din("sink", [128, 8])
    nab_d = din("nabias", [4, 128, 6400])
    smask_d = din("swamask", [128, 1536])
    cq_d = din("rot_cq", [128, 2048])
    sq_d = din("rot_sq", [128, 2048])
    ck_d = din("rot_ck", [128, TOK])
    sk_d = din("rot_sk", [128, TOK])
    ident_d = din("ident", [128, 128])
    out_d = nc.dram_tensor("out", [NTO * 128, D], F32, kind="ExternalOutput").ap()
    dump_d = {}
    for name, shape, dt in dumps:
        dump_d[name] = nc.dram_tensor("dbg_" + name, list(shape), dt, kind="ExternalOutput").ap()

    w_in_v = w_in_d.rearrange("(kc p) c -> p kc c", p=128)
    w_na_v = w_na_d.rearrange("(kc p) c -> p kc c", p=128)
    w_swa_v = w_swa_d.rearrange("(kc p) c -> p kc c", p=128)
    w_out_v = w_out_d.rearrange("(kc p) c -> p kc c", p=128)
    w_ff1_v = w_ff1_d.rearrange("(kc p) c -> p kc c", p=128)
    w_ff2_v = w_ff2_d.rearrange("(kc p) c -> p kc c", p=128)

    with ExitStack() as st:
        def sb(name, shape, dt):
            return st.enter_context(nc.sbuf_tensor("sb_" + name, list(shape), dt))

        def ps(name, shape, dt=F32):
            return st.enter_context(nc.psum_tensor("ps_" + name, list(shape), dt))

        R = sb("R", [128, NTO, D], F32)
        hT = sb("hT", [128, 8, TOK], BF16)
        ident = sb("ident", [128, 128], BF16)
        permb = sb("permb", [128, 128], BF16)
        bcol = sb("bcol", [128, NBCOL], F32)
        esink = sb("esink", [128, 8], F32)
        small = sb("small", [128, 64], F32)
        epsc = sb("epsc", [128, 1], F32)
        sinkrow = sb("sinkrow", [1, 4 * 256], BF16)
        indrow = sb("indrow", [1, 2 * 128], BF16)
        ARENA = 50960
        arena = sb("arena", [128, ARENA], BF16)

        psS = ps("psS", [128, 2048])
        psA = [ps("psA0", [128, 512]), ps("psA1", [128, 512])]
        psT = [ps("psT0", [128, 1024], BF16), ps("psT1", [128, 1024], BF16)]
        psO = [psT[0][:, :].bitcast(F32), psT[1][:, :].bitcast(F32)]

        class Bump:
            def __init__(self, phase="X"):
                self.off = 0
                self.phase = phase
                self.names = None

            def at(self, off):
                self.off = off
                return self

            def n(self, *names):
                self.names = list(names)
                return self

            def take(self, nelem_bf16):
                a = self.off
                self.off += (nelem_bf16 + 15) // 16 * 16
                assert self.off <= ARENA, self.off
                assert self.names, "arena buffer needs resource names"
                P.register(self.phase, self.names, a, self.off)
                self.names = None
                return a

            def bf(self, shape):
                n = int(np.prod(shape))
                a = self.take(n)
                v = arena[:, a:a + n]
                if len(shape) == 2:
                    return v.rearrange("p (a b) -> p a b", b=shape[1])
                if len(shape) == 3:
                    return v.rearrange("p (a b c) -> p a b c", b=shape[1], c=shape[2])
                return v

            def f32(self, shape):
                n = int(np.prod(shape))
                a = self.take(2 * n)
                v = arena[:, a:a + 2 * n].bitcast(F32)
                if len(shape) == 2:
                    return v.rearrange("p (a b) -> p a b", b=shape[1])
                return v

        P = Prog(nc)
        P.expand = {"psS0": ["pb0", "pb1"], "psS1": ["pb2", "pb3"], "psA0": ["pb4"], "psA1": ["pb5"],
                    "psT0": ["pb6"], "psT1": ["pb7"],
                    "bank0": ["pb4"], "bank1": ["pb5"], "bank2": ["pb0"], "bank3": ["pb1"], "bank4": ["pb2"],
                    "bank5": ["pb3"], "bank6": ["pb6"], "bank7": ["pb7"],
                    "xb0": ["pb4"], "xb1": ["pb5"], "xb2": ["pb6"], "xb3": ["pb7"],
                    "yb0": ["pb0"], "yb1": ["pb1"], "yb2": ["pb2"], "yb3": ["pb3"]}

        def dma(out, in_):
            return lambda e: e.dma_start(out=out, in_=in_)

        def mm_group(out_ps, pairs):
            def fn(e):
                n = len(pairs)
                ins = None
                for i, (l, r) in enumerate(pairs):
                    ins = e.matmul(out_ps, lhsT=l, rhs=r, start=(i == 0), stop=(i == n - 1))
                return ins
            return fn

        P.op("gpsimd", dma(ident[:], ident_d), writes=["ident"], dma_sem="c0")
        P.op("gpsimd", dma(permb[:], perm_d), writes=["permb"], dma_sem="c5")
        P.op("scalar", dma(bcol[:], bcol_d), writes=["bcol"], dma_sem="c1")
        P.op("scalar", dma(esink[:], sink_d), writes=["esink"], dma_sem="c2")
        P.op("vector", lambda e: e.memset(epsc[:], EPS), writes=["epsc"])
        P.op("scalar", lambda e: e.activation(out=esink[:], in_=esink[:], func=AF.Exp),
             reads=["esink"], writes=["esink"])

        def f_ind(e):
            e.memset(indrow[0:1, 0:64], 0.0)
            e.memset(indrow[0:1, 64:128], 1.0)
            e.memset(indrow[0:1, 128:192], 1.0)
            return e.memset(indrow[0:1, 192:256], 0.0)
        P.op("vector", f_ind, writes=["indrow"])
        P.op("vector", lambda e: e.memset(sinkrow[:], 1.0), writes=["sinkrow"])

        def f_srow(e):
            ins = None
            for g_ in range(2):
                for e_ in range(2):
                    for c_ in range(2):
                        h_ = 4 * g_ + 2 * c_ + e_
                        o_ = (g_ * 2 + e_) * 256 + c_ * 128
                        ins = e.tensor_scalar(out=sinkrow[0:1, o_:o_ + 128], in0=sinkrow[0:1, o_:o_ + 128],
                                              scalar1=esink[0:1, h_:h_ + 1], scalar2=None, op0=ALU.mult)
            return ins
        P.op("vector", f_srow, reads=["esink"], writes=["sinkrow"])

        def layer_norm(src, dst, g_bc, b_bc, k, res_src, res_dst, defer=None):
            o = (k % 4) * 16
            stt = small[:, o:o + 12]
            mv = small[:, o + 12:o + 14]
            rs = small[:, o + 14:o + 15]
            nm = small[:, o + 15:o + 16]
            sn = "lnst%d" % (k % 4)

            def f_stats(e):
                e.bn_stats(stt[:, 0:6], src[:, 0:512])
                return e.bn_stats(stt[:, 6:12], src[:, 512:1024])
            P.op("vector", f_stats, reads=[res_src], writes=[sn + "a"])
            P.op("vector", lambda e: e.bn_aggr(mv, stt), reads=[sn + "a"], writes=[sn + "b"])
            P.op("scalar", lambda e: e.activation(out=rs, in_=mv[:, 1:2], func=AF.Sqrt, bias=epsc[:, 0:1], scale=1.0),
                 reads=[sn + "b", "epsc"], writes=[sn + "c0"])
            P.op("vector", lambda e: e.tensor_scalar(out=nm, in0=mv[:, 0:1], scalar1=-1.0, scalar2=None, op0=ALU.mult),
                 reads=[sn + "b"], writes=[sn + "d"])
            P.op("vector", lambda e: e.reciprocal(out=rs, in_=rs), reads=[sn + "c0"], writes=[sn + "c"])
            P.op("vector", lambda e: e.tensor_scalar(out=src, in0=src, scalar1=nm, scalar2=rs, op0=ALU.add, op1=ALU.mult),
                 reads=[sn + "d", sn + "c"], writes=[res_src])
            P.op("gpsimd", lambda e: e.tensor_tensor(out=src, in0=src, in1=g_bc, op=ALU.mult),
                 reads=["lnparams"], writes=[res_src])
            def fin():
                P.op("vector", lambda e: e.tensor_tensor(out=dst, in0=src, in1=b_bc, op=ALU.add),
                     reads=[res_src, "lnparams"], writes=[res_dst])
            if defer is None:
                fin()
            else:
                defer.append(fin)

        def to_feature_major(hf, res_hf, hb, res_hb, k, tok0):
            P.op("scalar", lambda e: e.activation(out=hb, in_=hf, func=AF.Identity), reads=[res_hf], writes=[res_hb])
            pt = psT[k % 2]
            rp = "psT%d" % (k % 2)

            def f_tr(e):
                ins = None
                for c in range(8):
                    ins = e.transpose(out=pt[:, c * 128:(c + 1) * 128], in_=hb[:, c * 128:(c + 1) * 128],
                                      identity=ident[:])
                return ins
            P.op("tensor", f_tr, reads=[res_hb, "ident"], writes=[rp])
            ptv = pt[:, :].rearrange("p (c t) -> p c t", t=128)
            eng = "scalar"
            if eng == "vector":
                P.op("vector", lambda e: e.tensor_copy(out=hT[:, :, tok0:tok0 + 128], in_=ptv),
                     writes=[rp, "hT:%d" % (tok0 // 128)])
            else:
                P.op("scalar", lambda e: e.activation(out=hT[:, :, tok0:tok0 + 128], in_=ptv, func=AF.Identity),
                     writes=[rp, "hT:%d" % (tok0 // 128)])

        def hT_res(tok0, n):
            return ["hT:%d" % t for t in range(tok0 // 128, (tok0 + n) // 128)]

        class Pipe:
            def __init__(self):
                self.jobs = []

            def push(self, stages):
                self.jobs.append(list(stages))

            def tick(self):
                for job in list(self.jobs):
                    job.pop(0)()
                    if not job:
                        self.jobs.remove(job)

            def drain(self):
                while self.jobs:
                    self.tick()

        bA = Bump("A")
        xs = [bA.n("xs%d" % i).f32([1024]) for i in range(4)]
        hbA = [bA.n("hb%d" % i).bf([1024]) for i in range(3)]
        prm = bA.n("lnparams").f32([2, 1024])
        P.op("scalar", dma(prm[:, 0, :], brow_d[:, BR_LN0G:BR_LN0G + 1024]), writes=["lnparams"], dma_sem="prm")
        P.op("scalar", dma(prm[:, 1, :], brow_d[:, BR_LN0B:BR_LN0B + 1024]), writes=["lnparams"], dma_sem="prm")
        def ln0_job(t):
            xb = xs[t % 4]
            rx = "xs%d" % (t % 4)
            own = 2 <= t < 2 + NTO
            hb = hbA[t % 3]
            rhb = "hb%d" % (t % 3)
            o = (t % 4) * 16
            stt = small[:, o:o + 12]
            mv = small[:, o + 12:o + 14]
            rs = small[:, o + 14:o + 15]
            nm = small[:, o + 15:o + 16]
            sn = "lnst%d" % (t % 4)
            pt = psT[t % 2]
            rp = "psT%d" % (t % 2)
            ptv = pt[:, :].rearrange("p (c q) -> p c q", q=128)

            def st_ld():
                P.op("sync", dma(xb, x_d[t * 128:(t + 1) * 128, :]), writes=[rx], dma_sem=rx)

            def st_stats():
                def f_stats(e):
                    e.bn_stats(stt[:, 0:6], xb[:, 0:512])
                    return e.bn_stats(stt[:, 6:12], xb[:, 512:1024])
                P.op("vector", f_stats, reads=[rx], writes=[sn + "a"])
                P.op("vector", lambda e: e.bn_aggr(mv, stt), reads=[sn + "a"], writes=[sn + "b"])
                P.op("scalar", lambda e: e.activation(out=rs, in_=mv[:, 1:2], func=AF.Sqrt, bias=epsc[:, 0:1], scale=1.0),
                     reads=[sn + "b", "epsc"], writes=[sn + "c0"])

            def st_norm():
                P.op("vector", lambda e: e.reciprocal(out=rs, in_=rs), reads=[sn + "c0"], writes=[sn + "c"])
                P.op("vector", lambda e: e.tensor_scalar(out=nm, in0=mv[:, 0:1], scalar1=-1.0, scalar2=rs,
                                                         op0=ALU.mult, op1=ALU.mult),
                     reads=[sn + "b", sn + "c"], writes=[sn + "d"])
                P.op("scalar", lambda e: e.activation(out=hb, in_=xb, func=AF.Identity, bias=nm, scale=rs),
                     reads=[rx, sn + "c", sn + "d"], writes=[rhb])
                if own:
                    P.op("scalar", lambda e: e.activation(out=R[:, t - 2, :], in_=xb, func=AF.Identity, bias=nm, scale=rs),
                         reads=[rx, sn + "c", sn + "d"], writes=["R:%d" % (t - 2)])

            def st_tr():
                def f_tr(e):
                    ins = None
                    for c in range(8):
                        ins = e.transpose(out=pt[:, c * 128:(c + 1) * 128], in_=hb[:, c * 128:(c + 1) * 128],
                                          identity=ident[:])
                    return ins
                P.op("tensor", f_tr, reads=[rhb, "ident"], writes=[rp])

            def st_evac():
                def f_ev(e):
                    ins = None
                    for c in range(5):
                        ins = e.tensor_scalar(out=hT[:, c, t * 128:(t + 1) * 128], in0=ptv[:, c, :],
                                              scalar1=bcol[:, BC_G0 + c:BC_G0 + c + 1],
                                              scalar2=bcol[:, BC_B0 + c:BC_B0 + c + 1], op0=ALU.mult, op1=ALU.add)
                    return ins
                P.op("vector", f_ev, reads=["bcol"], writes=[rp, "hT:%d" % t])

                def f_ev2(e):
                    ins = None
                    for c in range(5, 8):
                        ins = e.activation(out=hT[:, c, t * 128:(t + 1) * 128], in_=ptv[:, c, :], func=AF.Identity,
                                           bias=bcol[:, BC_B0 + c:BC_B0 + c + 1], scale=bcol[:, BC_G0 + c:BC_G0 + c + 1])
                    return ins
                P.op("scalar", f_ev2, reads=["bcol"], writes=[rp, "hT:%d" % t])
            return [st_ld, st_stats, st_norm, st_tr, st_evac]

        pipeA = Pipe()
        for t in range(NT):
            pipeA.push(ln0_job(t))
            pipeA.tick()
        pipeA.drain()

        def r_fixup(tt):
            P.op("gpsimd", lambda e: e.tensor_tensor(out=R[:, tt, :], in0=R[:, tt, :], in1=prm[:, 0, :], op=ALU.mult),
                 reads=["lnparams"], writes=["R:%d" % tt])
            P.op("gpsimd", lambda e: e.tensor_tensor(out=R[:, tt, :], in0=R[:, tt, :], in1=prm[:, 1, :], op=ALU.add),
                 reads=["lnparams"], writes=["R:%d" % tt])

        def do_dumps(which):
            toks = []
            for name, ap in which:
                if name in dump_d:
                    toks.append(P.op("sync", dma(dump_d[name], ap), reads=[], dma_sem="dump"))
            return toks

        def finish_now(extra_toks):
            P.barrier()
            P.finish("sync", extra_toks)
            P.emit()
            return nc

        if stop_after == "A":
            P.barrier()
            toks = do_dumps([("hT", hT[:]), ("R", R[:])])
            return finish_now(toks)

        bB = Bump("B")
        attn_na = bB.n("attn_na").bf([4, 2048])
        attn_swa = bB.n("attn_swa").bf([4, 2048])
        qT = bB.n("qT").bf([2, 2048])
        kT = bB.n("kT").bf([TOK])
        vaug = bB.n("vaug", "vones").bf([NT, 192])
        PT = [bB.n("PT%d" % i).bf([768]) for i in range(2)]
        btab = bB.n("btab").bf([6400])
        smask = bB.n("smask").bf([1536])
        wq = bB.n("wq").bf([8, 256])
        qbb = [bB.n("qbb%d" % i).bf([512]) for i in range(2)]
        wk = bB.n("wk").bf([8, 128])
        wv = bB.n("wv").bf([8, 128])
        rotC = [bB.n("rotC%d" % i).f32([512]) for i in range(2)]
        rotS = [bB.n("rotS%d" % i).f32([512]) for i in range(2)]
        tmpa = [bB.n("tmpa%d" % i).f32([512]) for i in range(2)]
        tmpb = bB.n("tmpb").f32([512])
        bv_bc = bB.n("bv").f32([640])
        rec = [bB.n("rec%d" % i, "rec%da" % i).f32([256]) for i in range(2)]
        assert bB.off == 50944, bB.off

        P.op("sync", dma(bv_bc, brow_d[:, BR_BV:BR_BV + 640]), writes=["bv"], dma_sem="c3")
        P.op("gpsimd", dma(smask, smask_d), writes=["smask"], dma_sem="c4")
        P.op("gpsimd", lambda e: e.memset(vaug[:, :, 64:128], 1.0), writes=["vones"])

        psS_v = psS[:, :].rearrange("p (b n) -> p b n", b=2)

        def qcls(j):
            return {2: 0, 3: 1, 16: 3, 17: 4}.get(j, 2)

        def scls(j):
            return {2: 0, 17: 2}.get(j, 1)

        def project_fm(w_t, ncol0, tok0, bias_col, dst_ap, res_dst, k, eng, scale=None, wres="wk"):
            pa = psA[k % 2]
            rp = "psA%d" % (k % 2)
            P.op("tensor", mm_group(pa[:], [(w_t[:, kc, ncol0:ncol0 + 128], hT[:, kc, tok0:tok0 + 512])
                                            for kc in range(8)]),
                 reads=[wres] + hT_res(tok0, 512), writes=[rp])
            bc = bcol[:, bias_col:bias_col + 1]
            if eng == "split":
                def f_split(e):
                    ins = None
                    for (lo, hi, dap) in dst_ap:
                        ins = e.tensor_scalar(out=dap, in0=pa[lo:hi, :], scalar1=bcol[lo:hi, bias_col:bias_col + 1],
                                              scalar2=scale, op0=ALU.add, op1=ALU.mult)
                    return ins
                P.op("vector", f_split, reads=["bcol"], writes=[rp, res_dst])
            elif eng == "vector":
                P.op("vector", lambda e: e.tensor_scalar(out=dst_ap, in0=pa[:], scalar1=bc, scalar2=scale,
                                                         op0=ALU.add, op1=ALU.mult),
                     reads=["bcol"], writes=[rp, res_dst])
            else:
                P.op("scalar", lambda e: e.activation(out=dst_ap, in_=pa[:], func=AF.Identity, bias=bc, scale=1.0),
                     reads=["bcol"], writes=[rp, res_dst])

        def project_v(ncols, bvoff, dsts, k0):
            kk = k0
            for t0 in range(0, NT, 4):
                pa = psA[kk % 2]
                rp = "psA%d" % (kk % 2)

                def fn(e, t0=t0, pa=pa):
                    ins = None
                    for i in range(4):
                        for kc in range(8):
                            ins = e.matmul(pa[:, i * ncols:(i + 1) * ncols],
                                           lhsT=hT[:, kc, (t0 + i) * 128:(t0 + i + 1) * 128],
                                           rhs=wv[:, kc, 0:ncols], start=(kc == 0), stop=(kc == 7))
                    return ins
                P.op("tensor", fn, reads=["wv"] + hT_res(t0 * 128, 512), writes=[rp])
                pav = pa[:, 0:4 * ncols].rearrange("p (t c) -> p t c", t=4)
                for (src_off, dst_off) in dsts:
                    for i in range(4):
                        P.op("vector", lambda e, i=i, so=src_off, do=dst_off, pav=pav, t0=t0:
                             e.tensor_tensor(out=vaug[:, t0 + i, do:do + 64], in0=pav[:, i, so:so + 64],
                                             in1=bv_bc[:, bvoff + so:bvoff + so + 64], op=ALU.add),
                             reads=["bv"], writes=[rp, "vaug"])
                kk += 1
            return kk

        kctr = 0
        P.op("gpsimd", lambda e: e.memset(qT[:, :, :], 0.0), writes=["qT"])
        for hp in range(4):
            P.op("gpsimd", dma(wq[:, :, 0:128], w_in_v[:, :, 128 * hp:128 * hp + 128]), writes=["wq"], dma_sem="wq")
            P.op("gpsimd", dma(wk, w_in_v[:, :, 512 + 128 * hp:512 + 128 * hp + 128]), writes=["wk"], dma_sem="wk")
            P.op("gpsimd", dma(wv, w_in_v[:, :, 1024 + 128 * hp:1024 + 128 * hp + 128]), writes=["wv"], dma_sem="wv")
            P.op("gpsimd", dma(btab, nab_d[hp]), writes=["btab"], dma_sem="bt")
            for tt in range(4 * hp, 4 * hp + 4):
                r_fixup(tt)
            for nb in range(4):
                project_fm(wq, 0, HALO + 512 * nb, BC_QA + hp,
                           [(0, 64, qT[0:64, 0, 512 * nb:512 * nb + 512]), (64, 128, qT[64:128, 1, 512 * nb:512 * nb + 512])],
                           "qT", kctr, "split", scale=0.125, wres="wq")
                kctr += 1
            for kb in range(5):
                project_fm(wk, 0, 512 * kb, BC_KA + hp, kT[:, 512 * kb:512 * kb + 512], "kT", kctr, "scalar")
                kctr += 1
            kctr = project_v(128, 128 * hp, [(0, 0), (64, 128)], kctr)
            if stop_after == "B0":
                P.barrier()
                toks = do_dumps([("attn_na", attn_na), ("attn_swa", attn_swa), ("qT", qT), ("kT", kT), ("vaug", vaug)])
                return finish_now(toks)

            items = [(2 + 2 * jp + jj, e) for jp in range(NTO // 2) for e in range(2) for jj in range(2)]

            def na_scores(w, j, e, hp=hp):
                buf = w % 2
                cl = qcls(j)

                def fn(en):
                    ins = None
                    bo0 = ((cl * 5 + 0) * 2 + e) * 128
                    rhs4 = btab[:, bo0:bo0 + 1024].rearrange("p (s x) -> p s x", x=256)[:, :, 0:128]
                    en.matmul(psS_v[:, buf, 0:512].rearrange("p (s q) -> p s q", q=128), lhsT=ident[:], rhs=rhs4,
                              start=True, stop=False)
                    bo4 = ((cl * 5 + 4) * 2 + e) * 128
                    en.matmul(psS_v[:, buf, 512:640], lhsT=ident[:], rhs=btab[:, bo4:bo4 + 128], start=True, stop=False)
                    for si in range(5):
                        t = j + si - 2
                        o = psS_v[:, buf, si * 128:(si + 1) * 128]
                        ins = en.matmul(o, lhsT=kT[:, t * 128:(t + 1) * 128],
                                        rhs=qT[:, e, (j - 2) * 128:(j - 1) * 128], start=False, stop=(si >= 3))
                    return ins
                P.op("tensor", fn, reads=["kT", "qT", "btab", "ident"], writes=["psS%d" % buf])
                P.op("scalar", lambda en: en.activation(out=PT[buf][:, 0:640], in_=psS_v[:, buf, 0:640], func=AF.Exp),
                     writes=["psS%d" % buf, "PT%d" % buf])

            def na_pv(w, j, e, hp=hp):
                buf = w % 2
                ob = (w // 2) % 2
                jj = w % 2
                po = psO[ob][:, jj * 128:(jj + 1) * 128]

                def fn(en):
                    ins = None
                    for si in range(5):
                        t = j + si - 2
                        ins = en.matmul(po, lhsT=vaug[:, t, 64 * e:64 * e + 128],
                                        rhs=PT[buf][:, si * 128:(si + 1) * 128], start=(si == 0), stop=(si == 4))
                    return ins
                P.op("tensor", fn, reads=["PT%d" % buf, "vaug", "vones"], writes=["psT%d" % ob])
                if jj == 0:
                    return
                dl, dh = (64, 128) if e == 0 else (0, 64)
                ol, oh = (0, 64) if e == 0 else (64, 128)
                rc = rec[ob]
                pp = psO[ob][:, 0:256]
                j0 = j - 1
                P.op("scalar", lambda en: en.activation(out=rc[dl:dh, 0:256], in_=pp[dl:dh, :], func=AF.Ln),
                     writes=["psT%d" % ob, "rec%da" % ob])
                P.op("scalar", lambda en: en.activation(out=rc[dl:dh, 0:256], in_=rc[dl:dh, 0:256], func=AF.Exp, scale=-1.0),
                     reads=["rec%da" % ob], writes=["rec%d" % ob])
                P.op("vector", lambda en: en.tensor_tensor(out=attn_na[ol:oh, hp, (j0 - 2) * 128:j0 * 128],
                                                           in0=pp[ol:oh, :], in1=rc[dl:dh, 0:256], op=ALU.mult),
                     reads=["rec%d" % ob], writes=["psT%d" % ob, "attn_na"])

            for w in range(len(items) + 1):
                if w < len(items):
                    na_scores(w, *items[w])
                if w >= 1:
                    na_pv(w - 1, *items[w - 1])
            if stop_after == "B1":
                P.barrier()
                toks = do_dumps([("attn_na", attn_na), ("attn_swa", attn_swa), ("qT", qT), ("kT", kT), ("vaug", vaug)])
                return finish_now(toks)

        kTb = btab[:, 0:TOK]
        for g in range(2):
            P.op("gpsimd", dma(wq, w_in_v[:, :, 1536 + 256 * g:1536 + 256 * g + 256]), writes=["wq"], dma_sem="wq")
            for hh in range(2):
                P.op("gpsimd", dma(wk[:, :, 64 * hh:64 * hh + 64], w_in_v[:, :, 2048 + 64 * g:2048 + 64 * g + 64]),
                     writes=["wk"], dma_sem="wk")
            P.op("gpsimd", dma(wv[:, :, 0:64], w_in_v[:, :, 2176 + 64 * g:2176 + 64 * g + 64]), writes=["wv"], dma_sem="wv")
            if g == 0:
                P.op("gpsimd", lambda e: e.memset(kT[64:128, :], 0.0), writes=["kT"])
                P.op("gpsimd", lambda e: e.memset(kTb[0:64, :], 0.0), writes=["kT", "btab"])

            def rot_stages(w_a, wres, col0, tok0, c_d, s_d, tabtok0, bca, dst_ap, res_dst, k, split=None):
                rb = k % 2
                pa = psA[rb]
                rpa = "psA%d" % rb
                pp = psS[:, 1024 * rb:1024 * rb + 512]
                rpp = "psS%d" % rb
                qb = qbb[rb]
                ta = tmpa[rb]
                bc = bcol[:, bca:bca + 1]

                def stA():
                    P.op("sync", dma(rotC[rb], c_d[:, tabtok0:tabtok0 + 512]), writes=["rotC%d" % rb], dma_sem="rc%d" % rb)
                    P.op("sync", dma(rotS[rb], s_d[:, tabtok0:tabtok0 + 512]), writes=["rotS%d" % rb], dma_sem="rs%d" % rb)
                    P.op("tensor", mm_group(pa[:], [(w_a[:, kc, col0:col0 + 128], hT[:, kc, tok0:tok0 + 512])
                                                    for kc in range(8)]),
                         reads=[wres] + hT_res(tok0, 512), writes=[rpa])
                    P.op("scalar", lambda e: e.activation(out=qb, in_=pa[:], func=AF.Identity, bias=bc, scale=1.0),
                         reads=["bcol"], writes=[rpa, "qbb%d" % rb])
                    P.op("vector", lambda e: e.scalar_tensor_tensor(out=ta, in0=pa[:], scalar=bc, in1=rotC[rb],
                                                                    op0=ALU.add, op1=ALU.mult),
                         reads=["bcol", "rotC%d" % rb], writes=[rpa, "tmpa%d" % rb])

                def stB():
                    P.op("tensor", lambda e: e.matmul(pp, lhsT=permb[:], rhs=qb, start=True, stop=True),
                         reads=["permb", "qbb%d" % rb], writes=[rpp])
                    P.op("vector", lambda e: e.tensor_tensor(out=tmpb, in0=pp, in1=rotS[rb], op=ALU.mult),
                         reads=["rotS%d" % rb], writes=[rpp, "tmpb"])
                    if split is None:
                        P.op("gpsimd", lambda e: e.tensor_tensor(out=dst_ap, in0=ta, in1=tmpb, op=ALU.add),
                             reads=["tmpa%d" % rb, "tmpb"], writes=[res_dst])
                    else:
                        def f_sp(e):
                            ins = None
                            for (lo, hi, dap) in split:
                                ins = e.tensor_tensor(out=dap, in0=ta[lo:hi, :], in1=tmpb[lo:hi, :], op=ALU.add)
                            return ins
                        P.op("gpsimd", f_sp, reads=["tmpa%d" % rb, "tmpb"], writes=[res_dst, "btab"])
                return stA, stB

            blocks = []
            for c in range(2):
                for nb in range(4):
                    blocks.append(rot_stages(wq, "wq", 128 * c, HALO + 512 * nb, cq_d, sq_d, 512 * nb,
                                             BC_QB + 2 * g + c, qT[:, c, 512 * nb:512 * nb + 512], "qT", kctr))
                    kctr += 1
            for kb in range(5):
                blocks.append(rot_stages(wk, "wk", 0, 512 * kb, ck_d, sk_d, 512 * kb, BC_KB + g, None, "kT", kctr,
                                         split=[(0, 64, kT[0:64, 512 * kb:512 * kb + 512]),
                                                (64, 128, kTb[64:128, 512 * kb:512 * kb + 512])]))
                kctr += 1
            for bi in range(len(blocks) + 1):
                if bi < len(blocks):
                    blocks[bi][0]()
                if bi >= 1:
                    blocks[bi - 1][1]()
            kctr = project_v(64, 512 + 64 * g, [(0, 0), (0, 128)], kctr)

            items = [(j, e) for j in range(2, 2 + NTO) for e in range(2)]

            def sw_scores(w, j, e, g=g):
                buf = w % 2
                cl = scls(j)

                def fn(en):
                    ins = None
                    for si in range(3):
                        t = j + si - 1
                        o = psS_v[:, buf, si * 256:(si + 1) * 256]
                        kk_ = kT if e == 0 else kTb
                        ins = en.matmul(o.rearrange("p (c q) -> p c q", c=2),
                                        lhsT=kk_[:, t * 128:(t + 1) * 128],
                                        rhs=qT[:, :, (j - 2) * 128:(j - 1) * 128],
                                        start=True, stop=(si == 1))
                        if si != 1:
                            mo = (cl * 2 + (0 if si == 0 else 1)) * 256
                            ins = en.matmul(o, lhsT=ident[:], rhs=smask[:, mo:mo + 256], start=False, stop=True)
                    return ins
                P.op("tensor", fn, reads=["kT", "qT", "smask", "ident"], writes=["psS%d" % buf])
                P.op("scalar", lambda en: en.activation(out=PT[buf][:, 0:768], in_=psS_v[:, buf, 0:768], func=AF.Exp),
                     writes=["psS%d" % buf, "PT%d" % buf])

            def sw_pv(w, j, e, g=g):
                buf = w % 2
                ob = w % 2
                po = psO[ob][:, 0:256]

                def fn(en):
                    ins = None
                    for si in range(3):
                        t = j + si - 1
                        ins = en.matmul(po, lhsT=vaug[:, t, 64 * e:64 * e + 128],
                                        rhs=PT[buf][:, si * 256:(si + 1) * 256], start=(si == 0), stop=False)
                    so = (g * 2 + e) * 256
                    return en.matmul(po, lhsT=indrow[0:1, e * 128:(e + 1) * 128], rhs=sinkrow[0:1, so:so + 256],
                                     start=False, stop=True)
                P.op("tensor", fn, reads=["PT%d" % buf, "vaug", "vones", "indrow", "sinkrow"], writes=["psT%d" % ob])
                dl, dh = (64, 128) if e == 0 else (0, 64)
                ol, oh = (0, 64) if e == 0 else (64, 128)
                rc = rec[ob]

                P.op("scalar", lambda en: en.activation(out=rc[dl:dh, 0:256], in_=po[dl:dh, 0:256], func=AF.Ln),
                     writes=["psT%d" % ob, "rec%da" % ob])
                P.op("scalar", lambda en: en.activation(out=rc[dl:dh, 0:256], in_=rc[dl:dh, 0:256], func=AF.Exp, scale=-1.0),
                     reads=["rec%da" % ob], writes=["rec%d" % ob])
                pov = po.rearrange("p (c q) -> p c q", c=2)
                rcv = rc[:, :].rearrange("p (c q) -> p c q", c=2)
                P.op("vector", lambda en: en.tensor_tensor(
                    out=attn_swa[ol:oh, 2 * g:2 * g + 2, (j - 2) * 128:(j - 1) * 128],
                    in0=pov[ol:oh], in1=rcv[dl:dh], op=ALU.mult),
                    reads=["rec%d" % ob], writes=["psT%d" % ob, "attn_swa"])

            for w in range(len(items) + 1):
                if w < len(items):
                    sw_scores(w, *items[w])
                if w >= 1:
                    sw_pv(w - 1, *items[w - 1])

        if stop_after == "B":
            P.barrier()
            toks = do_dumps([("attn_na", attn_na), ("attn_swa", attn_swa), ("qT", qT), ("kT", kT), ("vaug", vaug)])
            return finish_now(toks)

        bC = Bump("C1")
        mixed = bC.at(16384).n("mixed").bf([8, 2048])
        bC.at(36352)
        wg = [[bC.n("wg%d" % b).bf([8, 128]) for _ in range(2)] for b in range(2)]
        wb = [[bC.n("wb%d" % b).bf([4, 128]) for _ in range(2)] for b in range(2)]
        sg = [bC.n("sg%d" % i).f32([512]) for i in range(2)]
        tm = [bC.n("tm%d" % i).f32([512]) for i in range(2)]
        wout_a = bC.n("wout0").bf([8, 512])
        assert bC.off <= ARENA
        boutb = bC.at(32768).n("boutb").f32([1024])
        P.op("sync", dma(boutb, brow_d[:, BR_BOUT:BR_BOUT + 1024]), writes=["boutb"], dma_sem="bo")
        pre_tiles = list(range(NTO))
        banks = [psA[0][:], psA[1][:], psS[:, 0:512], psS[:, 512:1024], psS[:, 1024:1536], psS[:, 1536:2048],
                 psO[0], psO[1]]
        it = 0
        def load_f(f):
            fb = f % 2
            P.op("gpsimd", dma(wg[fb][0], w_in_v[:, :, 2304 + 128 * f:2304 + 128 * f + 128]), writes=["wg%d" % fb], dma_sem="wg%d" % fb)
            P.op("gpsimd", dma(wg[fb][1], w_in_v[:, :, 3328 + 128 * f:3328 + 128 * f + 128]), writes=["wg%d" % fb], dma_sem="wg%d" % fb)
            P.op("gpsimd", dma(wb[fb][0], w_na_v[:, :, 128 * f:128 * f + 128]), writes=["wb%d" % fb], dma_sem="wb%d" % fb)
            P.op("gpsimd", dma(wb[fb][1], w_swa_v[:, :, 128 * f:128 * f + 128]), writes=["wb%d" % fb], dma_sem="wb%d" % fb)

        load_f(0)
        for f in range(8):
            fb = f % 2
            if f + 1 < 8:
                load_f(f + 1)
            if f == 1:
                P.op("gpsimd", dma(wout_a, w_out_v[:, :, 0:512]), writes=["wout0"], dma_sem="wout0")
            for nb in range(4):
                s4 = (it % 2) * 4
                it += 1
                tok0 = HALO + 512 * nb
                pga, pgb, pyn, pys = banks[s4:s4 + 4]
                rn = ["bank%d" % (s4 + i) for i in range(4)]
                P.op("tensor", mm_group(pga, [(wg[fb][0][:, kc, :], hT[:, kc, tok0:tok0 + 512]) for kc in range(8)]),
                     reads=["wg%d" % fb] + hT_res(tok0, 512), writes=[rn[0]])
                P.op("tensor", mm_group(pgb, [(wg[fb][1][:, kc, :], hT[:, kc, tok0:tok0 + 512]) for kc in range(8)]),
                     reads=["wg%d" % fb] + hT_res(tok0, 512), writes=[rn[1]])
                P.op("tensor", mm_group(pyn, [(wb[fb][0][:, c, :], attn_na[:, c, 512 * nb:512 * nb + 512]) for c in range(4)]),
                     reads=["wb%d" % fb, "attn_na"], writes=[rn[2]])
                P.op("tensor", mm_group(pys, [(wb[fb][1][:, c, :], attn_swa[:, c, 512 * nb:512 * nb + 512]) for c in range(4)]),
                     reads=["wb%d" % fb, "attn_swa"], writes=[rn[3]])
                P.op("scalar", lambda e, pga=pga, f=f: e.activation(out=sg[0], in_=pga, func=AF.Sigmoid,
                                                                     bias=bcol[:, BC_GA + f:BC_GA + f + 1], scale=1.0),
                     reads=["bcol"], writes=[rn[0], "sg0"])
                P.op("scalar", lambda e, pgb=pgb, f=f: e.activation(out=sg[1], in_=pgb, func=AF.Sigmoid,
                                                                     bias=bcol[:, BC_GB + f:BC_GB + f + 1], scale=1.0),
                     reads=["bcol"], writes=[rn[1], "sg1"])
                P.op("vector", lambda e, pyn=pyn: e.tensor_tensor(out=tm[0], in0=pyn, in1=sg[0], op=ALU.mult),
                     reads=["sg0"], writes=[rn[2], "tm0"])
                P.op("vector", lambda e, pys=pys: e.tensor_tensor(out=tm[1], in0=pys, in1=sg[1], op=ALU.mult),
                     reads=["sg1"], writes=[rn[3], "tm1"])
                P.op("gpsimd", lambda e, f=f, nb=nb: e.tensor_tensor(out=mixed[:, f, 512 * nb:512 * nb + 512],
                                                                      in0=tm[0], in1=tm[1], op=ALU.add),
                     reads=["tm0", "tm1"], writes=["mixed"])
                if pre_tiles:
                    tt = pre_tiles.pop(0)
                    P.op("vector", lambda e, tt=tt: e.scalar_tensor_tensor(out=R[:, tt, :], in0=R[:, tt, :], scalar=ALPHA,
                                                                          in1=boutb, op0=ALU.mult, op1=ALU.add),
                         reads=["boutb"], writes=["R:%d" % tt])

        bD = Bump("C2")
        wout_b = bD.at(0).n("wout1").bf([8, 512])
        prm1 = bD.n("lnparams").f32([2, 1024])
        zb = [bD.n("z%d" % i).f32([1024]) for i in range(3)]
        hbC = [bD.n("hb%d" % i).bf([1024]) for i in range(2)]
        assert bD.off <= 16384
        wouts = [wout_a, wout_b]
        P.op("gpsimd", dma(wout_b, w_out_v[:, :, 512:1024]), writes=["wout1"], dma_sem="wout1")
        P.op("sync", dma(prm1[:, 0, :], brow_d[:, BR_LN1G:BR_LN1G + 1024]), writes=["lnparams"], dma_sem="prm")
        P.op("sync", dma(prm1[:, 1, :], brow_d[:, BR_LN1B:BR_LN1B + 1024]), writes=["lnparams"], dma_sem="prm")
        def c2_mm(tt):
            for half in range(2):
                P.op("tensor", mm_group(psA[half][:], [(mixed[:, f, tt * 128:(tt + 1) * 128],
                                                        wouts[half][:, f, :]) for f in range(8)]),
                     reads=["mixed", "wout%d" % half], writes=["psA%d" % half])

        def c2_add(tt):
            z = zb[tt % 3]
            rz = "z%d" % (tt % 3)
            for half in range(2):
                P.op("vector", lambda e, half=half, z=z, tt=tt: e.tensor_tensor(
                    out=z[:, 512 * half:512 * half + 512], in0=R[:, tt, 512 * half:512 * half + 512],
                    in1=psA[half][:], op=ALU.add),
                    reads=["R:%d" % tt], writes=["psA%d" % half, rz])

        def ln1_job(tt):
            z = zb[tt % 3]
            rz = "z%d" % (tt % 3)
            hb = hbC[tt % 2]
            rhb = "hb%d" % (tt % 2)
            o = (tt % 4) * 16
            stt = small[:, o:o + 12]
            mv = small[:, o + 12:o + 14]
            rs = small[:, o + 14:o + 15]
            nm = small[:, o + 15:o + 16]
            sn = "lnst%d" % (tt % 4)
            pt = psT[tt % 2]
            rp = "psT%d" % (tt % 2)
            ptv = pt[:, :].rearrange("p (c q) -> p c q", q=128)
            tok0 = HALO + tt * 128

            def st_stats():
                def f_stats(e):
                    e.bn_stats(stt[:, 0:6], z[:, 0:512])
                    return e.bn_stats(stt[:, 6:12], z[:, 512:1024])
                P.op("vector", f_stats, reads=[rz], writes=[sn + "a"])
                P.op("vector", lambda e: e.bn_aggr(mv, stt), reads=[sn + "a"], writes=[sn + "b"])
                P.op("scalar", lambda e: e.activation(out=rs, in_=mv[:, 1:2], func=AF.Sqrt, bias=epsc[:, 0:1], scale=1.0),
                     reads=[sn + "b", "epsc"], writes=[sn + "c0"])

            def st_norm():
                P.op("vector", lambda e: e.reciprocal(out=rs, in_=rs), reads=[sn + "c0"], writes=[sn + "c"])
                P.op("vector", lambda e: e.tensor_scalar(out=nm, in0=mv[:, 0:1], scalar1=-1.0, scalar2=rs,
                                                         op0=ALU.mult, op1=ALU.mult),
                     reads=[sn + "b", sn + "c"], writes=[sn + "d"])
                P.op("scalar", lambda e: e.activation(out=hb, in_=z, func=AF.Identity, bias=nm, scale=rs),
                     reads=[rz, sn + "c", sn + "d"], writes=[rhb])
                P.op("scalar", lambda e: e.activation(out=R[:, tt, :], in_=z, func=AF.Identity, bias=nm, scale=rs),
                     reads=[rz, sn + "c", sn + "d"], writes=["R:%d" % tt])

            def st_tr():
                def f_tr(e):
                    ins = None
                    for c in range(8):
                        ins = e.transpose(out=pt[:, c * 128:(c + 1) * 128], in_=hb[:, c * 128:(c + 1) * 128],
                                          identity=ident[:])
                    return ins
                P.op("tensor", f_tr, reads=[rhb, "ident"], writes=[rp])
                P.op("gpsimd", lambda e: e.tensor_tensor(out=R[:, tt, :], in0=R[:, tt, :], in1=prm1[:, 0, :], op=ALU.mult),
                     reads=["lnparams"], writes=["R:%d" % tt])

            def st_evac():
                def f_ev(e):
                    ins = None
                    for c in range(4):
                        ins = e.tensor_scalar(out=hT[:, c, tok0:tok0 + 128], in0=ptv[:, c, :],
                                              scalar1=bcol[:, BC_G1 + c:BC_G1 + c + 1],
                                              scalar2=bcol[:, BC_B1 + c:BC_B1 + c + 1], op0=ALU.mult, op1=ALU.add)
                    return ins
                P.op("vector", f_ev, reads=["bcol"], writes=[rp, "hT:%d" % (tok0 // 128)])

                def f_ev2(e):
                    ins = None
                    for c in range(4, 8):
                        ins = e.activation(out=hT[:, c, tok0:tok0 + 128], in_=ptv[:, c, :], func=AF.Identity,
                                           bias=bcol[:, BC_B1 + c:BC_B1 + c + 1], scale=bcol[:, BC_G1 + c:BC_G1 + c + 1])
                    return ins
                P.op("scalar", f_ev2, reads=["bcol"], writes=[rp, "hT:%d" % (tok0 // 128)])
            return [st_stats, st_norm, st_tr, st_evac]

        bE = Bump("D")
        w1 = [None, None]
        w2 = [None, None]
        w1[0] = bE.at(32768).n("w1_0").bf([8, 1024])
        P.op("gpsimd", dma(w1[0], w_ff1_v[:, :, 0:1024]), writes=["w1_0"], dma_sem="w1_0")
        pipeC = Pipe()
        c2_mm(0)
        c2_add(0)
        for tt in range(NTO):
            if tt + 1 < NTO:
                c2_mm(tt + 1)
            pipeC.push(ln1_job(tt))
            pipeC.tick()
            if tt + 1 < NTO:
                c2_add(tt + 1)
        pipeC.drain()

        rt = [bE.n("rt%d" % i).f32([512]) for i in range(2)]
        prm2 = bE.n("lnparams").f32([3, 1024])
        w2[0] = bE.at(0).n("w2_0").bf([8, 1024])
        P.op("gpsimd", dma(w2[0], w_ff2_v[:, 0:8, :]), writes=["w2_0"], dma_sem="w2_0")
        ub = [bE.n(*["u%d_%d" % (i, f) for f in range(8)]).bf([8, 512]) for i in range(2)]
        w1[1] = bE.at(16384).n("w1_1").bf([8, 1024])
        w2[1] = bE.n("w2_1").bf([8, 1024])
        P.op("sync", dma(prm2[:, 0, :], brow_d[:, BR_LN1B:BR_LN1B + 1024]), writes=["lnparams"], dma_sem="prm")
        P.op("sync", dma(prm2[:, 1, :], brow_d[:, BR_LN2B:BR_LN2B + 1024]), writes=["lnparams"], dma_sem="prm")
        P.op("sync", dma(prm2[:, 2, :], brow_d[:, BR_BFF2:BR_BFF2 + 1024]), writes=["lnparams"], dma_sem="prm")
        P.op("vector", lambda e: e.scalar_tensor_tensor(out=prm2[:, 2, :], in0=prm2[:, 0, :], scalar=ALPHA,
                                                        in1=prm2[:, 2, :], op0=ALU.mult, op1=ALU.add),
             writes=["lnparams"])
        P.op("sync", dma(prm2[:, 0, :], brow_d[:, BR_LN2G:BR_LN2G + 1024]), writes=["lnparams"], dma_sem="prm")

        if stop_after == "C":
            P.barrier()
            toks = do_dumps([("hT", hT[:]), ("R", R[:])])
            return finish_now(toks)

        xbanks = [psA[0][:], psA[1][:], psO[0], psO[1]]
        ybanks = [psS[:, 0:512], psS[:, 512:1024], psS[:, 1024:1536], psS[:, 1536:2048]]
        out_toks = []
        deferred = []
        cnt = {"x": 0, "y": 0}

        def load_q(qf):
            wbuf = qf % 2
            if qf > 0:
                P.op("gpsimd", dma(w1[wbuf], w_ff1_v[:, :, 1024 * qf:1024 * qf + 1024]), writes=["w1_%d" % wbuf], dma_sem="w1_%d" % wbuf)
                P.op("gpsimd", dma(w2[wbuf], w_ff2_v[:, 8 * qf:8 * qf + 8, :]), writes=["w2_%d" % wbuf], dma_sem="w2_%d" % wbuf)

        def ffn1(bi, qf, nb):
            wbuf = qf % 2
            u = ub[bi % 2]
            ru = "u%d" % (bi % 2)
            tok0 = HALO + 512 * nb
            for fp in range(4):
                xa = cnt["x"] % 4
                cnt["x"] += 2
                pxs = [xbanks[xa], xbanks[xa + 1]]
                rpxs = ["xb%d" % xa, "xb%d" % (xa + 1)]

                def fn(e, fp=fp, pxs=pxs, wbuf=wbuf, tok0=tok0):
                    ins = None
                    for i in range(2):
                        ffc = 2 * fp + i
                        for kc in range(8):
                            ins = e.matmul(pxs[i], lhsT=w1[wbuf][:, kc, 128 * ffc:128 * ffc + 128],
                                           rhs=hT[:, kc, tok0:tok0 + 512], start=(kc == 0), stop=(kc == 7))
                    return ins
                P.op("tensor", fn, reads=["w1_%d" % wbuf] + hT_res(tok0, 512), writes=rpxs)
                for i in range(2):
                    ffc = 2 * fp + i
                    r_ = rt[i]
                    rr = "rt%d" % i
                    bcidx = BC_FF1 + 8 * qf + ffc
                    P.op("scalar", lambda e, px=pxs[i], r_=r_, bcidx=bcidx: e.activation(
                        out=r_, in_=px, func=AF.Relu, bias=bcol[:, bcidx:bcidx + 1], scale=1.0),
                        reads=["bcol"], writes=[rpxs[i], rr])
                    P.op("gpsimd", lambda e, r_=r_, u=u, ffc=ffc: e.tensor_tensor(out=u[:, ffc, :], in0=r_, in1=r_, op=ALU.mult),
                         reads=[rr], writes=[ru + "_%d" % ffc])

        def ffn2(bi, qf, nb):
            wbuf = qf % 2
            u = ub[bi % 2]
            ru = "u%d" % (bi % 2)
            for ti in range(4):
                tt = 4 * nb + ti
                ya = cnt["y"] % 4
                cnt["y"] += 2
                pys = [ybanks[ya], ybanks[ya + 1]]
                rpys = ["yb%d" % ya, "yb%d" % (ya + 1)]

                def fn(e, ti=ti, pys=pys, u=u, wbuf=wbuf):
                    ins = None
                    for half in range(2):
                        for ffc in range(8):
                            ins = e.matmul(pys[half], lhsT=u[:, ffc, ti * 128:(ti + 1) * 128],
                                           rhs=w2[wbuf][:, ffc, 512 * half:512 * half + 512],
                                           start=(ffc == 0), stop=(ffc == 7))
                    return ins
                P.op("tensor", fn, reads=["w2_%d" % wbuf] + [ru + "_%d" % ffc for ffc in range(8)], writes=rpys)
                for half in range(2):
                    py = pys[half]
                    if qf == 0:
                        P.op("vector", lambda e, py=py, tt=tt, half=half: e.scalar_tensor_tensor(
                            out=R[:, tt, 512 * half:512 * half + 512], in0=R[:, tt, 512 * half:512 * half + 512],
                            scalar=ALPHA, in1=py, op0=ALU.mult, op1=ALU.add),
                            writes=[rpys[half], "R:%d" % tt])
                    else:
                        P.op("vector", lambda e, py=py, tt=tt, half=half: e.tensor_tensor(
                            out=R[:, tt, 512 * half:512 * half + 512], in0=R[:, tt, 512 * half:512 * half + 512],
                            in1=py, op=ALU.add),
                            writes=[rpys[half], "R:%d" % tt])
                if qf == 0:
                    P.op("gpsimd", lambda e, tt=tt: e.tensor_tensor(out=R[:, tt, :], in0=R[:, tt, :], in1=prm2[:, 2, :],
                                                                   op=ALU.add),
                         reads=["lnparams"], writes=["R:%d" % tt])
                if qf == 3:
                    prev = list(deferred)
                    del deferred[:]
                    layer_norm(R[:, tt, :], R[:, tt, :], prm2[:, 0, :], prm2[:, 1, :], tt, "R:%d" % tt, "R:%d" % tt,
                               defer=deferred)

                    def st_out(tt=tt):
                        out_toks.append(P.op("sync", dma(out_d[tt * 128:(tt + 1) * 128, :], R[:, tt, :]),
                                             reads=["R:%d" % tt], dma_sem="out"))
                    deferred.append(st_out)
                    for f_ in prev:
                        f_()

        steps = [(qf, nb) for qf in range(4) for nb in range(4)]
        load_q(0)
        load_q(1)
        ffn1(0, *steps[0])
        for bi, (qf, nb) in enumerate(steps):
            if bi + 1 < len(steps):
                ffn1(bi + 1, *steps[bi + 1])
            ffn2(bi, qf, nb)
            if nb == 3 and qf + 2 < 4:
                load_q(qf + 2)
        for f_ in deferred:
            f_()
        P.finish("sync", out_toks)
        P.emit()
    return nc


def _na_tables(rpb, ch):
    g0 = 16 * ch
    rep_local = [2, 3, 4, 16, 17]
    kk = np.arange(128)
    rk, ck = kk // 64, kk % 64
    out = np.full((4, 128, 5, 5, 2, 128), NEG, np.float32)
    for cl, jq in enumerate(rep_local):
        gq = g0 + jq - 2
        Rq = 2 * gq + rk
        cq = ck
        rs = np.clip(Rq - 4, 0, 120)
        cs = np.clip(cq - 8, 0, 48)
        for si in range(5):
            lk = jq + si - 2
            gk = g0 + lk - 2
            if ch == 0 and lk == 0:
                gk = 3
            if ch == 3 and lk == 19:
                gk = 60
            if gk < 0 or gk > 63:
                continue
            Rk = 2 * gk + rk
            valid = ((Rk[:, None] >= rs[None, :]) & (Rk[:, None] <= rs[None, :] + 7)
                     & (ck[:, None] >= cs[None, :]) & (ck[:, None] < cs[None, :] + 16))
            dr = np.clip(Rk[:, None] - Rq[None, :] + 7, 0, 14)
            dc = np.clip(ck[:, None] - cq[None, :] + 15, 0, 30)
            for h in range(8):
                tab = np.where(valid, rpb[h][dr, dc], np.float32(NEG)).astype(np.float32)
                out[h // 2, :, cl, si, h % 2, :] = tab
    return np.ascontiguousarray(out.reshape(4, 128, 6400))


def _swa_masks(ch):
    g0 = 16 * ch
    kk = np.arange(128)
    out = np.zeros((128, 3, 2, 2, 128), np.float32)
    for cl, jq in enumerate([2, 4, 17]):
        gq = g0 + jq - 2
        qpos = gq * 128 + kk
        for si, s in enumerate([-1, 1]):
            kpos = (gq + s) * 128 + kk
            valid = (np.abs(kpos[:, None] - qpos[None, :]) <= 128) & (kpos[:, None] >= 0) & (kpos[:, None] < 8192)
            m = np.where(valid, np.float32(0.0), np.float32(NEG)).astype(np.float32)
            out[:, cl, si, 0, :] = m
            out[:, cl, si, 1, :] = m
    return np.ascontiguousarray(out.reshape(128, 1536))


def _rot_tables(pos, scale):
    inv_freq = np.power(np.float32(500000.0), -np.arange(0, 16, 2, dtype=np.float32) / np.float32(16)).astype(np.float32)
    ang = pos.astype(np.float32)[:, None] * inv_freq[None, :]
    cos = np.cos(ang).astype(np.float32).T
    sin = np.sin(ang).astype(np.float32).T
    C = np.ones((128, pos.shape[0]), np.float32)
    S = np.zeros((128, pos.shape[0]), np.float32)
    for base in (0, 64):
        C[base:base + 8] = cos
        C[base + 8:base + 16] = cos
        S[base:base + 8] = -sin
        S[base + 8:base + 16] = sin
    return np.ascontiguousarray(C * np.float32(scale)), np.ascontiguousarray(S * np.float32(scale))


def _swap_cols(w, nheads):
    out = np.zeros_like(w)
    for h in range(nheads):
        b = 64 * h
        out[..., b:b + 8] = w[..., b + 8:b + 16]
        out[..., b + 8:b + 16] = w[..., b:b + 8]
    return out


def make_in_maps(inputs):
    f = lambda a: np.ascontiguousarray(np.asarray(a, dtype=np.float32))
    x = f(inputs["x"])
    w_in = f(inputs["w_in"])[0]
    b_in = f(inputs["b_in"])[0]
    rpb = f(inputs["na_rpb"])[0]
    sink = f(inputs["swa_sink"])[0]
    wq_b = w_in[:, 1536:2048]
    wk_b = w_in[:, 2048:2176]
    perm = np.zeros((128, 128), np.float32)
    for m in range(128):
        b0, d = (m // 64) * 64, m % 64
        k = b0 + d + 8 if d < 8 else (b0 + d - 8 if d < 16 else m)
        perm[k, m] = 1.0
    bq_s = _swap_cols(b_in[1536:2048], 8)
    bk_s = _swap_cols(b_in[2048:2176], 2)
    bcol = np.zeros((128, NBCOL), np.float32)
    bcol[:, BC_QA:BC_QA + 4] = b_in[0:512].reshape(4, 128).T
    bcol[:, BC_KA:BC_KA + 4] = b_in[512:1024].reshape(4, 128).T
    bcol[:, BC_QB:BC_QB + 4] = b_in[1536:2048].reshape(4, 128).T
    bcol[:, BC_QBS:BC_QBS + 4] = bq_s.reshape(4, 128).T
    for g in range(2):
        bcol[:, BC_KB + g] = np.tile(b_in[2048 + 64 * g:2048 + 64 * g + 64], 2)
        bcol[:, BC_KBS + g] = np.tile(bk_s[64 * g:64 * g + 64], 2)
    bcol[:, BC_GA:BC_GA + 8] = b_in[2304:3328].reshape(8, 128).T
    bcol[:, BC_GB:BC_GB + 8] = b_in[3328:4352].reshape(8, 128).T
    bcol[:, BC_FF1:BC_FF1 + 32] = f(inputs["b_ff1"])[0].reshape(32, 128).T
    bcol[:, BC_G0:BC_G0 + 8] = f(inputs["ln0_g"]).reshape(8, 128).T
    bcol[:, BC_B0:BC_B0 + 8] = f(inputs["ln0_b"]).reshape(8, 128).T
    bcol[:, BC_G1:BC_G1 + 8] = f(inputs["ln1_g"])[0].reshape(8, 128).T
    bcol[:, BC_B1:BC_B1 + 8] = f(inputs["ln1_b"])[0].reshape(8, 128).T
    brow = np.zeros((NBROW,), np.float32)
    brow[BR_LN0G:BR_LN0G + 1024] = f(inputs["ln0_g"])
    brow[BR_LN0B:BR_LN0B + 1024] = f(inputs["ln0_b"])
    brow[BR_BV:BR_BV + 512] = b_in[1024:1536]
    brow[BR_BV + 512:BR_BV + 640] = b_in[2176:2304]
    brow[BR_BOUT:BR_BOUT + 1024] = f(inputs["b_out"])[0]
    brow[BR_LN1G:BR_LN1G + 1024] = f(inputs["ln1_g"])[0]
    brow[BR_LN1B:BR_LN1B + 1024] = f(inputs["ln1_b"])[0]
    brow[BR_BFF2:BR_BFF2 + 1024] = f(inputs["b_ff2"])[0]
    brow[BR_LN2G:BR_LN2G + 1024] = f(inputs["ln2_g"])[0]
    brow[BR_LN2B:BR_LN2B + 1024] = f(inputs["ln2_b"])[0]
    brow_bc = np.ascontiguousarray(np.broadcast_to(brow[None, :], (128, NBROW)))
    sink_bc = np.ascontiguousarray(np.broadcast_to(sink[None, :], (128, 8)))
    shared = {
        "w_in": w_in, "perm": perm, "w_na": f(inputs["w_branch_na"])[0], "w_swa": f(inputs["w_branch_swa"])[0],
        "w_out": f(inputs["w_out"])[0], "w_ff1": f(inputs["w_ff1"])[0], "w_ff2": f(inputs["w_ff2"])[0],
        "bcol": bcol, "brow": brow_bc, "sink": sink_bc, "ident": np.eye(128, dtype=np.float32),
    }
    tabs = {ch: (_na_tables(rpb, ch), _swa_masks(ch)) for ch in range(4)}
    in_maps = []
    for core in range(NCORES):
        b, ch = core // 4, core % 4
        g0 = 16 * ch
        xl = np.zeros((TOK, D), np.float32)
        pos = np.zeros((TOK,), np.int64)
        for t in range(NT):
            gt = g0 + t - 2
            if ch == 0 and t == 0:
                gt = 3
            if ch == 3 and t == 19:
                gt = 60
            if 0 <= gt < 64:
                xl[t * 128:(t + 1) * 128] = x[b, gt * 128:(gt + 1) * 128]
                pos[t * 128:(t + 1) * 128] = gt * 128 + np.arange(128)
        cq, sq = _rot_tables(pos[HALO:HALO + 2048], 0.125)
        ck, sk = _rot_tables(pos, 1.0)
        m = dict(shared)
        m.update({"x": xl, "nabias": tabs[ch][0], "swamask": tabs[ch][1],
                  "rot_cq": cq, "rot_sq": sq, "rot_ck": ck, "rot_sk": sk})
        in_maps.append(m)
    return in_maps


_NC_CACHE = {}


def kernel(**inputs):
    in_maps = make_in_maps(inputs)
    if "nc" not in _NC_CACHE:
        _NC_CACHE["nc"] = build_program()
    nc = _NC_CACHE["nc"]
    res = run_bass_kernel_spmd(nc, in_maps, core_ids=list(range(NCORES)))
    out = np.zeros((2, 8192, D), np.float32)
    for core in range(NCORES):
        b, ch = core // 4, core % 4
        out[b, ch * 2048:(ch + 1) * 2048] = res.results[core]["out"]
    return out
```
